# Optimizing a Trainium2 kernel written in Bass

```python
import jax
import jax.numpy as jnp
from jax import lax
import numpy as np

D_MODEL = 1024
BATCH = 8
SEQ = 2048
DEPTH = 2

GRID_W = 64
CTX_LEN = 256
HEAD_DIM = 64
H_ATT = 6
H_KV = 2
H_RET = 4
H_SSD = 6
SSD_GROUPS = 2
SSD_STATE = 128
SSD_CONV = 5
D_ATT = H_ATT * HEAD_DIM
D_KV = H_KV * HEAD_DIM
D_RET = H_RET * HEAD_DIM
D_SSD = H_SSD * HEAD_DIM
D_XBC = D_SSD + 2 * SSD_GROUPS * SSD_STATE
D_MIX = D_ATT + D_RET + D_SSD
D_FF = 4 * D_MODEL
IN_SPLITS = (D_ATT, D_KV, D_KV, D_RET, D_RET, D_RET, D_RET, D_SSD, D_XBC, 2 * H_SSD)
D_IN = sum(IN_SPLITS)
Q_BLOCK = 128
CHUNK = 128
ROPE_THETA = 10000.0
EPS = 1e-6
F32 = jnp.float32

kernel_name = 'hybrid_parallel_heads_dit_block'


def rmsnorm(x, g):
    xf = x.astype(F32)
    y = xf * lax.rsqrt(jnp.mean(xf * xf, axis=-1, keepdims=True) + EPS)
    return (y * g.astype(F32)).astype(x.dtype)


def grid_rope(n):
    rows = n // GRID_W
    row = jnp.broadcast_to(jnp.arange(rows)[:, None], (rows, GRID_W)).reshape(n)
    col = jnp.broadcast_to(jnp.arange(GRID_W)[None, :], (rows, GRID_W)).reshape(n)
    half = HEAD_DIM // 2
    inv_freq = ROPE_THETA ** (-jnp.arange(0, half, 2, dtype=F32) / half)
    ang = jnp.stack([row, col], axis=-1).astype(F32)[:, :, None] * inv_freq
    ang = jnp.concatenate([ang, ang], axis=-1)
    return jnp.cos(ang), jnp.sin(ang)


def apply_rope(x, cos, sin):
    b, n, h, d = x.shape
    xr = x.astype(F32).reshape(b, n, h, 2, d // 2)
    x1, x2 = jnp.split(xr, 2, axis=-1)
    rot = jnp.concatenate([-x2, x1], axis=-1)
    out = xr * cos[:, None] + rot * sin[:, None]
    return out.reshape(b, n, h, d).astype(x.dtype)


def depthwise_conv(x, w, b):
    y = lax.conv_general_dilated(x, w[:, None, :].astype(x.dtype), window_strides=(1,),
                                 padding=[(SSD_CONV // 2, SSD_CONV // 2)],
                                 dimension_numbers=('NWC', 'WIO', 'NWC'),
                                 feature_group_count=x.shape[-1])
    return y + b.astype(x.dtype)


def chunked_recurrence(q, k, v, log_a, s0, strict):
    b, h, l, n = q.shape
    p = v.shape[-1]
    nc = l // CHUNK
    qc = q.astype(F32).reshape(b, h, nc, CHUNK, n)
    kc = k.astype(F32).reshape(b, h, nc, CHUNK, n)
    vc = v.astype(F32).reshape(b, h, nc, CHUNK, p)
    cum = jnp.cumsum(log_a.astype(F32).reshape(b, h, nc, CHUNK), axis=-1)
    idx = jnp.arange(CHUNK)
    mask = (idx[:, None] > idx[None, :]) if strict else (idx[:, None] >= idx[None, :])
    decay = jnp.exp(jnp.where(mask, cum[..., :, None] - cum[..., None, :], -jnp.inf))
    scores = jnp.einsum('bhcin,bhcjn->bhcij', qc, kc) * decay
    y_intra = jnp.einsum('bhcij,bhcjp->bhcip', scores, vc)
    last = cum[..., -1:]
    d_state = jnp.einsum('bhcj,bhcjn,bhcjp->bhcnp', jnp.exp(last - cum), kc, vc)
    chunk_decay = jnp.exp(last[..., 0])

    def step(s, inp):
        dec, ds = inp
        return dec[..., None, None] * s + ds, s

    s_final, s_in = lax.scan(step, s0.astype(F32),
                             (jnp.moveaxis(chunk_decay, 2, 0), jnp.moveaxis(d_state, 2, 0)))
    s_in = jnp.moveaxis(s_in, 0, 2)
    y_inter = jnp.exp(cum)[..., None] * jnp.einsum('bhcin,bhcnp->bhcip', qc, s_in)
    return (y_intra + y_inter).reshape(b, h, l, p), s_final


def final_state(k, v, log_a):
    cum = jnp.cumsum(log_a.astype(F32), axis=-1)
    w = jnp.exp(cum[..., -1:] - cum)
    return jnp.einsum('bhl,bhln,bhlp->bhnp', w, k.astype(F32), v.astype(F32))


def bidirectional_recurrence(q, ks, v, las, qc, kcs, vc, lacs, need_ctx_out):
    bsz, h, _, n_state = q.shape
    p = v.shape[-1]
    y_dirs = []
    yc_dirs = []
    for direction in range(2):
        rev = direction == 1
        fl = (lambda t: jnp.flip(t, axis=2)) if rev else (lambda t: t)
        if need_ctx_out:
            yc_d, s_ctx = chunked_recurrence(fl(qc), fl(kcs[direction]), fl(vc), fl(lacs[direction]),
                                             jnp.zeros((bsz, h, n_state, p), F32), rev)
            yc_dirs.append(fl(yc_d))
        else:
            s_ctx = final_state(fl(kcs[direction]), fl(vc), fl(lacs[direction]))
        y_d, _ = chunked_recurrence(fl(q), fl(ks[direction]), fl(v), fl(las[direction]), s_ctx, rev)
        y_dirs.append(fl(y_d))
    y = (y_dirs[0] + y_dirs[1]).astype(v.dtype)
    if not need_ctx_out:
        return y, None
    return y, (yc_dirs[0] + yc_dirs[1]).astype(vc.dtype)


def sdpa(qh, keys, vals):
    b, lq = qh.shape[:2]
    qg = qh.reshape(b, lq, H_KV, H_ATT // H_KV, HEAD_DIM)
    s = jnp.einsum('bqkgd,bskd->bkgqs', qg, keys).astype(F32) * (HEAD_DIM ** -0.5)
    pr = jax.nn.softmax(s, axis=-1).astype(vals.dtype)
    o = jnp.einsum('bkgqs,bskd->bqkgd', pr, vals)
    return o.reshape(b, lq, D_ATT)


def attention_mixer(q, k, v, qc, kc, vc, qn_g, kn_g, cos, sin, need_ctx_out):
    b, n, _ = q.shape
    lc = kc.shape[1]
    q = apply_rope(rmsnorm(q.reshape(b, n, H_ATT, HEAD_DIM), qn_g), cos, sin)
    k = apply_rope(rmsnorm(k.reshape(b, n, H_KV, HEAD_DIM), kn_g), cos, sin)
    kc = rmsnorm(kc.reshape(b, lc, H_KV, HEAD_DIM), kn_g)
    vc = vc.reshape(b, lc, H_KV, HEAD_DIM)
    keys = jnp.concatenate([k, kc], axis=1)
    vals = jnp.concatenate([v.reshape(b, n, H_KV, HEAD_DIM), vc], axis=1)
    nb = n // Q_BLOCK
    q_blocks = jnp.moveaxis(q.reshape(b, nb, Q_BLOCK, H_ATT, HEAD_DIM), 1, 0)
    out = lax.map(lambda qb: sdpa(qb, keys, vals), q_blocks)
    out = jnp.moveaxis(out, 0, 1).reshape(b, n, D_ATT)
    if not need_ctx_out:
        return out, None
    out_c = sdpa(rmsnorm(qc.reshape(b, lc, H_ATT, HEAD_DIM), qn_g), kc, vc)
    return out, out_c


def head_groupnorm(y, gain, bias):
    mu = jnp.mean(y, axis=-1, keepdims=True)
    var = jnp.mean(jnp.square(y - mu), axis=-1, keepdims=True)
    yn = (y - mu) * lax.rsqrt(var + EPS)
    b, h, l, p = y.shape
    yn = jnp.transpose(yn, (0, 2, 1, 3)).reshape(b, l, h * p)
    return yn * gain.astype(F32) + bias.astype(F32)


def retention_mixer(q, k, v, g, qc, kc, vc, gc, decay_logit, gn_g, gn_b, cos, sin, need_ctx_out):
    b, n, _ = q.shape
    lc = qc.shape[1]
    heads = lambda t: t.reshape(t.shape[0], t.shape[1], H_RET, HEAD_DIM)
    bhld = lambda t: jnp.transpose(t, (0, 2, 1, 3))
    kscale = HEAD_DIM ** -0.5
    q = bhld(apply_rope(heads(q), cos, sin))
    k = bhld(apply_rope(heads(k), cos, sin)) * kscale
    v = bhld(heads(v))
    qc = bhld(heads(qc))
    kc = bhld(heads(kc)) * kscale
    vc = bhld(heads(vc))
    log_gamma = jax.nn.log_sigmoid(decay_logit.astype(F32))
    las = tuple(jnp.broadcast_to(log_gamma[d][None, :, None], (b, H_RET, n)) for d in range(2))
    lacs = tuple(jnp.broadcast_to(log_gamma[d][None, :, None], (b, H_RET, lc)) for d in range(2))
    y, yc = bidirectional_recurrence(q, (k, k), v, las, qc, (kc, kc), vc, lacs, need_ctx_out)
    out = (head_groupnorm(y.astype(F32), gn_g, gn_b) * jax.nn.silu(g.astype(F32))).astype(g.dtype)
    if not need_ctx_out:
        return out, None
    out_c = (head_groupnorm(yc.astype(F32), gn_g, gn_b) * jax.nn.silu(gc.astype(F32))).astype(gc.dtype)
    return out, out_c


def ssd_prep(xbc, dt_raw, conv_w, conv_b, dt_bias, a_log):
    b, l, _ = xbc.shape
    xbc = jax.nn.silu(depthwise_conv(xbc, conv_w, conv_b))
    xs, bm, cm = jnp.split(xbc, [D_SSD, D_SSD + SSD_GROUPS * SSD_STATE], axis=-1)
    rep = H_SSD // SSD_GROUPS
    xs = jnp.transpose(xs.reshape(b, l, H_SSD, HEAD_DIM), (0, 2, 1, 3))
    bm = jnp.repeat(jnp.transpose(bm.reshape(b, l, SSD_GROUPS, SSD_STATE), (0, 2, 1, 3)), rep, axis=1)
    cm = jnp.repeat(jnp.transpose(cm.reshape(b, l, SSD_GROUPS, SSD_STATE), (0, 2, 1, 3)), rep, axis=1)
    dt = jax.nn.softplus(dt_raw.astype(F32).reshape(b, l, 2, H_SSD) + dt_bias.astype(F32))
    dt = jnp.transpose(dt, (2, 0, 3, 1))
    a = -jnp.exp(a_log.astype(F32))
    las = tuple(dt[d] * a[d][None, :, None] for d in range(2))
    ks = tuple(bm.astype(F32) * dt[d][..., None] for d in range(2))
    return xs, cm, ks, las


def ssd_finish(y, xs, z, d_skip, norm_g):
    y = y + d_skip.astype(y.dtype)[None, :, None, None] * xs
    b, _, l, _ = y.shape
    y = jnp.transpose(y, (0, 2, 1, 3)).reshape(b, l, D_SSD)
    return rmsnorm(y * jax.nn.silu(z), norm_g)


def ssd_mixer(z, xbc, dt_raw, zc, xbcc, dtc_raw, conv_w, conv_b, dt_bias, a_log, d_skip, norm_g, need_ctx_out):
    xs, cm, ks, las = ssd_prep(xbc, dt_raw, conv_w, conv_b, dt_bias, a_log)
    xsc, cmc, ksc, lasc = ssd_prep(xbcc, dtc_raw, conv_w, conv_b, dt_bias, a_log)
    y, yc = bidirectional_recurrence(cm, ks, xs, las, cmc, ksc, xsc, lasc, need_ctx_out)
    out = ssd_finish(y, xs, z, d_skip, norm_g)
    if not need_ctx_out:
        return out, None
    return out, ssd_finish(yc, xsc, zc, d_skip, norm_g)


def squared_relu_mlp(h, w1, w2):
    return jnp.square(jax.nn.relu(h @ w1)) @ w2


def trunk_layer(x, xc, mod, mod_c, norm_g, w_in, w_out, qn_g, kn_g, ret_decay, ret_gn_g, ret_gn_b,
                conv_w, conv_b, dt_bias, a_log, d_skip, ssd_norm_g, w_ff1, w_ff2, cos, sin, need_ctx_out):
    sh1, sc1, g1, sh2, sc2, g2 = jnp.split(mod, 6, axis=-1)
    csh1, csc1, cg1, csh2, csc2, cg2 = jnp.split(mod_c, 6, axis=-1)
    split_idx = np.cumsum(IN_SPLITS)[:-1].tolist()
    h = rmsnorm(x, norm_g[0]) * (1 + sc1[:, None]) + sh1[:, None]
    hc = rmsnorm(xc, norm_g[0]) * (1 + csc1) + csh1
    qa, ka, va, qr, kr, vr, gr, z, xbc, dt = jnp.split(h @ w_in, split_idx, axis=-1)
    qac, kac, vac, qrc, krc, vrc, grc, zc, xbcc, dtc = jnp.split(hc @ w_in, split_idx, axis=-1)
    att, att_c = attention_mixer(qa, ka, va, qac, kac, vac, qn_g, kn_g, cos, sin, need_ctx_out)
    ret, ret_c = retention_mixer(qr, kr, vr, gr, qrc, krc, vrc, grc, ret_decay, ret_gn_g, ret_gn_b,
                                 cos, sin, need_ctx_out)
    ssd, ssd_c = ssd_mixer(z, xbc, dt, zc, xbcc, dtc, conv_w, conv_b, dt_bias, a_log, d_skip, ssd_norm_g,
                           need_ctx_out)
    o = jnp.concatenate([att, ret, ssd], axis=-1) @ w_out
    x = x + g1[:, None] * rmsnorm(o, norm_g[1])
    h2 = rmsnorm(x, norm_g[2]) * (1 + sc2[:, None]) + sh2[:, None]
    x = x + g2[:, None] * rmsnorm(squared_relu_mlp(h2, w_ff1, w_ff2), norm_g[3])
    if need_ctx_out:
        oc = jnp.concatenate([att_c, ret_c, ssd_c], axis=-1) @ w_out
        xc = xc + cg1 * rmsnorm(oc, norm_g[1])
        h2c = rmsnorm(xc, norm_g[2]) * (1 + csc2) + csh2
        xc = xc + cg2 * rmsnorm(squared_relu_mlp(h2c, w_ff1, w_ff2), norm_g[3])
    return x, xc


def setup_inputs(seed: int = 0) -> dict:
    key = jax.random.key(seed)
    ks = jax.random.split(key, 24)
    nrm = lambda k, shape, s: jax.random.normal(k, shape, F32) * s
    gamma = 1.0 - 2.0 ** (-5.0 - jnp.arange(H_RET, dtype=F32))
    ret_logit = jnp.log(gamma) - jnp.log1p(-gamma)
    dt0 = jnp.exp(jax.random.uniform(ks[15], (DEPTH, 2, H_SSD), F32, jnp.log(1e-3), jnp.log(1e-1)))
    return {
        'x': nrm(ks[0], (BATCH, SEQ, D_MODEL), 1.0),
        'c': nrm(ks[1], (BATCH, D_MODEL), 1.0),
        'ctx': nrm(ks[2], (BATCH, CTX_LEN, D_MODEL), 1.0),
        'c_ctx': nrm(ks[3], (D_MODEL,), 1.0),
        'w_mod': nrm(ks[4], (DEPTH, D_MODEL, 6 * D_MODEL), 0.5 * D_MODEL ** -0.5),
        'b_mod': nrm(ks[5], (DEPTH, 6 * D_MODEL), 0.01),
        'norm_g': 1.0 + nrm(ks[6], (DEPTH, 4, D_MODEL), 0.05),
        'w_in': nrm(ks[7], (DEPTH, D_MODEL, D_IN), D_MODEL ** -0.5),
        'w_out': nrm(ks[8], (DEPTH, D_MIX, D_MODEL), D_MIX ** -0.5),
        'q_norm_g': 1.0 + nrm(ks[9], (DEPTH, HEAD_DIM), 0.05),
        'k_norm_g': 1.0 + nrm(ks[10], (DEPTH, HEAD_DIM), 0.05),
        'ret_decay_logit': ret_logit + nrm(ks[11], (DEPTH, 2, H_RET), 0.05),
        'ret_gn_g': 1.0 + nrm(ks[12], (DEPTH, D_RET), 0.05),
        'ret_gn_b': nrm(ks[13], (DEPTH, D_RET), 0.01),
        'ssd_conv_w': nrm(ks[14], (DEPTH, SSD_CONV, D_XBC), SSD_CONV ** -0.5),
        'ssd_conv_b': nrm(ks[16], (DEPTH, D_XBC), 0.01),
        'ssd_dt_bias': dt0 + jnp.log(-jnp.expm1(-dt0)),
        'ssd_a_log': jnp.log(jax.random.uniform(ks[17], (DEPTH, 2, H_SSD), F32, 1.0, 16.0)),
        'ssd_d': 1.0 + nrm(ks[18], (DEPTH, H_SSD), 0.1),
        'ssd_norm_g': 1.0 + nrm(ks[19], (DEPTH, D_SSD), 0.05),
        'w_ff1': nrm(ks[20], (DEPTH, D_MODEL, D_FF), D_MODEL ** -0.5),
        'w_ff2': nrm(ks[21], (DEPTH, D_FF, D_MODEL), D_FF ** -0.5),
    }


def reference(x, c, ctx, c_ctx, w_mod, b_mod, norm_g, w_in, w_out, q_norm_g, k_norm_g, ret_decay_logit,
              ret_gn_g, ret_gn_b, ssd_conv_w, ssd_conv_b, ssd_dt_bias, ssd_a_log, ssd_d, ssd_norm_g,
              w_ff1, w_ff2):
    n = x.shape[1]
    cos, sin = grid_rope(n)
    sc = jax.nn.silu(c)
    scc = jax.nn.silu(c_ctx)
    xc = ctx
    for layer in range(DEPTH):
        mod = sc @ w_mod[layer] + b_mod[layer]
        mod_c = scc @ w_mod[layer] + b_mod[layer]
        x, xc = trunk_layer(x, xc, mod, mod_c, norm_g[layer], w_in[layer], w_out[layer],
                            q_norm_g[layer], k_norm_g[layer], ret_decay_logit[layer],
                            ret_gn_g[layer], ret_gn_b[layer], ssd_conv_w[layer], ssd_conv_b[layer],
                            ssd_dt_bias[layer], ssd_a_log[layer], ssd_d[layer], ssd_norm_g[layer],
                            w_ff1[layer], w_ff2[layer], cos, sin, layer < DEPTH - 1)
    return x
```

```python
import numpy as np
from contextlib import ExitStack
import concourse.bass as bass
import concourse.mybir as mybir
from concourse.bass_utils import run_bass_kernel_spmd

F32 = mybir.dt.float32
BF16 = mybir.dt.bfloat16
AF = mybir.ActivationFunctionType
ALU = mybir.AluOpType

D = 1024
KC = 8
D_IN = 2956
D_FF = 4096
EPS = 1e-6
GRID_W = 64
ROPE_THETA = 10000.0

ENGINES = ("pe", "act", "dve", "pool", "sp")
SEM_CHUNK = 1000
DMA_POOL = 8


class Buf:
    def __init__(self, ap, keys):
        self.ap = ap
        self.keys = tuple(keys)


def _keys(items):
    out = []
    for it in items:
        if isinstance(it, Buf):
            out.extend(it.keys)
        elif isinstance(it, (list,)):
            out.extend(_keys(it))
        else:
            out.append(it)
    return out


class Op:
    __slots__ = ("eng", "fn", "dma", "deps", "has_dep", "ms", "dma_idx", "idx")


class Prog:
    def __init__(self, nc):
        self.nc = nc
        self.ops = []
        self.last_writer = {}
        self.readers = {}

    def op(self, eng, fn, reads=(), writes=(), dma=False):
        if isinstance(fn, tuple):
            fn = [fn]
        o = Op()
        o.eng, o.fn, o.dma = eng, fn, dma
        o.has_dep, o.ms, o.dma_idx = False, None, None
        o.idx = len(self.ops)
        rk = _keys(reads)
        wk = _keys(writes)
        pr = [k for k in rk if isinstance(k, tuple) and k and k[0] == "ps"]
        if pr:
            rk = [k for k in rk if not (isinstance(k, tuple) and k and k[0] == "ps")]
            wk = wk + pr
        deps = set()
        lw, rd = self.last_writer, self.readers
        for r in rk:
            w = lw.get(r)
            if w is not None:
                deps.add(w)
        for r in wk:
            w = lw.get(r)
            if w is not None:
                deps.add(w)
            x = rd.get(r)
            if x:
                deps.update(x)
        for r in rk:
            rd.setdefault(r, []).append(o.idx)
        for r in wk:
            lw[r] = o.idx
            rd[r] = []
        deps.discard(o.idx)
        if eng == "pe":
            deps = {d for d in deps if self.ops[d].eng != "pe"}
        o.deps = deps
        self.ops.append(o)
        return o

    def pe(self, fn, reads=(), writes=()):
        return self.op("pe", fn, reads, writes)

    def act(self, fn, reads=(), writes=()):
        return self.op("act", fn, reads, writes)

    def dve(self, fn, reads=(), writes=()):
        return self.op("dve", fn, reads, writes)

    def pool(self, fn, reads=(), writes=()):
        return self.op("pool", fn, reads, writes)

    def dma(self, eng, fn, reads=(), writes=()):
        return self.op(eng, fn, reads, writes, dma=True)

    def emit(self, stack):
        nc = self.nc
        ops = self.ops
        for o in ops:
            for d in o.deps:
                ops[d].has_dep = True
        cnt = {e: 0 for e in ENGINES}
        dcnt = {e: 0 for e in ENGINES}
        for o in ops:
            if o.dma:
                o.dma_idx = dcnt[o.eng]
                dcnt[o.eng] += 1
            elif o.has_dep:
                cnt[o.eng] += 1
                o.ms = cnt[o.eng]
        self.stats = dict(ms=dict(cnt), dma=dict(dcnt), n_ops={e: sum(1 for o in ops if o.eng == e) for e in ENGINES})
        esems, dsems = {}, {}
        for e in ENGINES:
            n = (cnt[e] + SEM_CHUNK - 1) // SEM_CHUNK
            esems[e] = [stack.enter_context(nc.semaphore(f"c_{e}_{i}")) for i in range(n)]
            n = min(DMA_POOL, dcnt[e])
            dsems[e] = [stack.enter_context(nc.semaphore(f"d_{e}_{i}")) for i in range(n)]

        def sem_of(o):
            if o.dma:
                return dsems[o.eng][o.dma_idx % DMA_POOL], 16 * (o.dma_idx // DMA_POOL + 1)
            m = o.ms - 1
            return esems[o.eng][m // SEM_CHUNK], (m % SEM_CHUNK) + 1

        by_eng = {e: [o for o in ops if o.eng == e] for e in ENGINES}
        block = stack.enter_context(nc.Block())

        def run(e, eng):
            waited = {}
            for o in by_eng[e]:
                need = {}
                for d in o.deps:
                    s, v = sem_of(ops[d])
                    k = id(s)
                    if need.get(k, (None, 0))[1] < v:
                        need[k] = (s, v)
                if o.dma and o.dma_idx >= DMA_POOL:
                    s = dsems[e][o.dma_idx % DMA_POOL]
                    v = 16 * (o.dma_idx // DMA_POOL)
                    k = id(s)
                    if need.get(k, (None, 0))[1] < v:
                        need[k] = (s, v)
                for k, (s, v) in need.items():
                    if waited.get(k, 0) < v:
                        eng.wait_ge(s, v)
                        waited[k] = v
                inst = None
                for (mname, kw) in o.fn:
                    inst = getattr(eng, mname)(**kw)
                if inst is None:
                    continue
                if o.dma:
                    inst.then_inc(sem_of(o)[0], 16)
                elif o.ms is not None:
                    inst.then_inc(sem_of(o)[0], 1)

        if by_eng["pe"]:
            block.tensor(lambda eng: run("pe", eng))
        if by_eng["act"]:
            block.scalar(lambda eng: run("act", eng))
        if by_eng["dve"]:
            block.vector(lambda eng: run("dve", eng))
        if by_eng["pool"]:
            block.gpsimd(lambda eng: run("pool", eng))
        if by_eng["sp"]:
            block.sync(lambda eng: run("sp", eng))


SLOT = 512


class ABuf(Buf):
    def __init__(self, flat, off, esz, shape):
        self.flat = flat
        self.off = off
        self.esz = esz
        self.shape = tuple(shape)
        n = int(np.prod(shape))
        self.n = n
        keys = [("A", s) for s in range(off // SLOT, (off + n * esz - 1) // SLOT + 1)]
        if len(shape) == 1:
            v = flat
        elif len(shape) == 2:
            v = flat.rearrange("p (a b) -> p a b", a=shape[0])
        elif len(shape) == 3:
            v = flat.rearrange("p (a b c) -> p a b c", a=shape[0], b=shape[1])
        elif len(shape) == 4:
            v = flat.rearrange("p (a b c d) -> p a b c d", a=shape[0], b=shape[1], c=shape[2])
        else:
            raise ValueError(shape)
        self.v = v
        Buf.__init__(self, v, keys)

    def sub(self, lo, hi):
        b0 = self.off + lo * self.esz
        b1 = self.off + hi * self.esz
        keys = [("A", s) for s in range(b0 // SLOT, (b1 - 1) // SLOT + 1)]
        return Buf(self.flat[:, lo:hi], keys)


class Arena:
    def __init__(self, nc, st, nbytes):
        self.nbytes = nbytes
        self.t = st.enter_context(nc.sbuf_tensor("arena", [128, nbytes // 2], BF16))
        self.off = 0
        self.hi = 0

    def alloc(self, shape, dt):
        if isinstance(shape, int):
            shape = (shape,)
        esz = 4 if dt == F32 else 2
        n = int(np.prod(shape))
        nb = n * esz
        off = (self.off + 63) // 64 * 64
        assert off + nb <= self.nbytes, f"arena overflow: need {off + nb} of {self.nbytes}"
        self.off = off + nb
        self.hi = max(self.hi, self.off)
        flat = self.t[:, off // 2:(off + nb) // 2]
        if dt == F32:
            flat = flat.bitcast(F32)
        return ABuf(flat, off, esz, shape)

    def mark(self):
        return self.off

    def release(self, m):
        self.off = m


C_ID, C_UF, C_UB, C_ONE, C_BLK, C_ROT, C_OND, C_MF, C_MB, C_HM = range(10)
NCON = 10 * 128


def host_consts():
    c = np.zeros((128, NCON), np.float32)
    i = np.arange(128)
    c[:, C_ID * 128:(C_ID + 1) * 128] = np.eye(128)
    c[:, C_UF * 128:(C_UF + 1) * 128] = (i[:, None] <= i[None, :])
    c[:, C_UB * 128:(C_UB + 1) * 128] = (i[:, None] >= i[None, :])
    c[:, C_ONE * 128:(C_ONE + 1) * 128] = 1.0
    blk = (i[:, None] // 64 == i[None, :] // 64).astype(np.float32) / 64.0
    c[:, C_BLK * 128:(C_BLK + 1) * 128] = blk
    rot = np.zeros((128, 128), np.float32)
    for d in range(128):
        if d % 32 < 16:
            rot[d + 16, d] = -1.0
        else:
            rot[d - 16, d] = 1.0
    c[:, C_ROT * 128:(C_ROT + 1) * 128] = rot
    c[:, C_OND * 128:(C_OND + 1) * 128] = 1.0 / 1024.0
    c[:, C_MF * 128:(C_MF + 1) * 128] = (i[None, :] >= i[:, None])
    c[:, C_MB * 128:(C_MB + 1) * 128] = (i[None, :] < i[:, None])
    hm = np.zeros((128, 128), np.float32)
    hm[:64, 0] = 1.0
    hm[64:, 1] = 1.0
    c[:, C_HM * 128:(C_HM + 1) * 128] = hm
    return c


def host_rope(L, LC):
    T = L + LC
    rows = L // GRID_W
    row = np.broadcast_to(np.arange(rows)[:, None], (rows, GRID_W)).reshape(L)
    col = np.broadcast_to(np.arange(GRID_W)[None, :], (rows, GRID_W)).reshape(L)
    half = 32
    inv_freq = (ROPE_THETA ** (-np.arange(0, half, 2, dtype=np.float32) / half)).astype(np.float32)
    ang = np.stack([row, col], axis=-1).astype(np.float32)[:, :, None] * inv_freq
    ang = np.concatenate([ang, ang], axis=-1).reshape(L, 64)
    cos = np.ones((64, T), np.float32)
    sin = np.zeros((64, T), np.float32)
    cos[:, :L] = np.cos(ang).T
    sin[:, :L] = np.sin(ang).T
    tab = np.zeros((128, 2, T), np.float32)
    tab[:64, 0] = cos
    tab[64:, 0] = cos
    tab[:64, 1] = sin
    tab[64:, 1] = sin
    return tab.reshape(128, 2 * T)


PO_BMOD = 0
PO_NG = PO_BMOD + 48
PO_QG = PO_NG + 32
PO_KG = PO_QG + 1
PO_RGG = PO_KG + 1
PO_RGB = PO_RGG + 2
PO_RLOG = PO_RGB + 2
PO_CW = PO_RLOG + 8
PO_CB = PO_CW + 35
PO_DTB = PO_CB + 7
PO_ALOG = PO_DTB + 12
PO_SD = PO_ALOG + 12
PO_SNG = PO_SD + 6
NPAR = PO_SNG + 3


def host_params(inp, depth):
    par = np.zeros((128, depth, NPAR), np.float32)
    fm = lambda v: np.ascontiguousarray(v.reshape(-1, 128).T)
    for l in range(depth):
        p = par[:, l]
        p[:, PO_BMOD:PO_BMOD + 48] = fm(inp["b_mod"][l])
        p[:, PO_NG:PO_NG + 32] = fm(inp["norm_g"][l].reshape(-1))
        p[:, PO_QG] = np.tile(inp["q_norm_g"][l], 2)
        p[:, PO_KG] = np.tile(inp["k_norm_g"][l], 2)
        p[:, PO_RGG:PO_RGG + 2] = fm(inp["ret_gn_g"][l])
        p[:, PO_RGB:PO_RGB + 2] = fm(inp["ret_gn_b"][l])
        p[:, PO_RLOG:PO_RLOG + 8] = np.broadcast_to(inp["ret_decay_logit"][l].reshape(1, 8), (128, 8))
        cw = inp["ssd_conv_w"][l]
        for cc in range(7):
            p[:, PO_CW + cc * 5:PO_CW + cc * 5 + 5] = cw[:, cc * 128:(cc + 1) * 128].T
        p[:, PO_CB:PO_CB + 7] = fm(inp["ssd_conv_b"][l])
        p[:, PO_DTB:PO_DTB + 12] = np.broadcast_to(inp["ssd_dt_bias"][l].reshape(1, 12), (128, 12))
        p[:, PO_ALOG:PO_ALOG + 12] = np.broadcast_to(inp["ssd_a_log"][l].reshape(1, 12), (128, 12))
        p[:, PO_SD:PO_SD + 6] = np.broadcast_to(inp["ssd_d"][l].reshape(1, 6), (128, 6))
        p[:, PO_SNG:PO_SNG + 3] = fm(inp["ssd_norm_g"][l])
    return par.reshape(128, depth * NPAR)


CO_QA, CO_KA, CO_VA = 0, 384, 512
CO_QR, CO_KR, CO_VR, CO_GR = 640, 896, 1152, 1408
CO_Z, CO_XBC, CO_DT = 1664, 2048, 2944


def I(m, **kw):
    return (m, kw)


class _Stop(Exception):
    pass


def build(L, LC, depth, stop_after=None, dbg=None):
    T = L + LC
    NT = T // 128
    NTL = L // 128
    blocks = [(i * 512, 512) for i in range(L // 512)] + [(L, LC)]
    lat_chunks = list(range(NTL))
    ctx_chunks = list(range(NTL, NT))
    fwd_order = ctx_chunks + lat_chunks
    bwd_order = ctx_chunks[::-1] + lat_chunks[::-1]

    nc = bass.Bass("TRN2", target_bir_lowering=False)
    dt_in = lambda name, shape: nc.dram_tensor(name, shape, F32, kind="ExternalInput").ap()
    x_d = dt_in("x", [L, D])
    ctx_d = dt_in("ctx", [LC, D])
    cv_d = dt_in("cv", [128, 16])
    par_d = dt_in("par", [128, depth * NPAR])
    con_d = dt_in("con", [128, NCON])
    rope_d = dt_in("rope", [128, 2 * T])
    wmod_d = dt_in("w_mod", [depth, D, 6 * D])
    win_d = dt_in("w_in", [depth, D, D_IN])
    wout_d = dt_in("w_out", [depth, D, D])
    w1_d = dt_in("w_ff1", [depth, D, D_FF])
    w2_d = dt_in("w_ff2", [depth, D_FF, D])
    out_d = nc.dram_tensor("out", [L, D], F32, kind="ExternalOutput").ap()
    dbg_d = None
    if dbg is not None:
        dbg_d = nc.dram_tensor("dbg", [128, dbg], F32, kind="ExternalOutput").ap()

    P = Prog(nc)
    st = ExitStack()
    with st:
        sbt = lambda name, shape, dt: st.enter_context(nc.sbuf_tensor(name, shape, dt))
        xT = sbt("xT", [128, KC, T], F32)
        hT = sbt("hT", [128, KC, T], BF16)
        c32_t = sbt("c32", [128, 4, 128], F32)
        cb_t = sbt("cb", [128, 6, 128], BF16)
        idb_t = sbt("idb", [128, 128], BF16)
        one_b_t = sbt("oneb", [128, 128], BF16)
        par_t = sbt("par_sb", [128, depth, NPAR], F32)
        ab_t = sbt("ab", [128, 6, KC, 2], F32)
        dI_t = sbt("dI", [128, 6, 128], BF16)
        sml_t = sbt("sml", [128, 64], F32)
        arena = Arena(nc, st, (nc.sbuf_bytes_remaining - 1024) // 64 * 64)
        banks = [Buf(st.enter_context(nc.psum_tensor(f"ps{i}", [128, 512], F32)), [("ps", i)]) for i in range(8)]

        ident32 = c32_t[:, 0, :]
        Uf32 = c32_t[:, 1, :]
        Ub32 = c32_t[:, 2, :]
        ones32 = c32_t[:, 3, :]
        blk64b = cb_t[:, 0, :]
        rotb = cb_t[:, 1, :]
        onesDb = cb_t[:, 2, :]
        mFb = cb_t[:, 3, :]
        mBb = cb_t[:, 4, :]
        hmb = cb_t[:, 5, 0:2]
        identb = idb_t[:, :]
        ones1b = one_b_t[:, :]
        CONST = "const"

        def hk(t0, n):
            return [("hT", t) for t in range(t0 // 128, (t0 + n + 127) // 128)]

        def xkk(t0, n):
            r = []
            for t in range(t0 // 128, (t0 + n + 127) // 128):
                r += [("xT", t, 0), ("xT", t, 1)]
            return r

        def b16(bank):
            return bank.ap[:].bitcast(BF16)

        P.dma("sp", I("dma_start", out=c32_t[:, 0:3, :], in_=con_d[:, 0:384].rearrange("p (a b) -> p a b", a=3)), writes=[CONST])
        P.dma("sp", I("dma_start", out=c32_t[:, 3, :], in_=con_d[:, C_ONE * 128:(C_ONE + 1) * 128]), writes=[CONST])
        P.dma("sp", I("dma_start", out=par_t[:], in_=par_d.rearrange("p (l n) -> p l n", l=depth)), writes=["par"])
        P.dma("pool", I("dma_start", out=cb_t[:], in_=con_d[:, C_BLK * 128:(C_HM + 1) * 128].rearrange("p (a b) -> p a b", a=6)), writes=[CONST])
        P.dma("pool", I("dma_start", out=idb_t[:], in_=con_d[:, C_ID * 128:(C_ID + 1) * 128]), writes=[CONST])
        P.dma("pool", I("dma_start", out=one_b_t[:], in_=con_d[:, C_ONE * 128:(C_ONE + 1) * 128]), writes=[CONST])

        def dump(ap_, col0, ncols, reads):
            if dbg_d is None:
                return
            if ap_.dtype != F32:
                tmp = arena.alloc(ncols, F32)
                P.dve(I("tensor_copy", out=tmp.flat, in_=ap_), reads=reads, writes=[tmp])
                ap_, reads = tmp.flat, [tmp]
            P.dma("sp", I("dma_start", out=dbg_d[:, col0:col0 + ncols], in_=ap_), reads=reads, writes=[("dbg", col0)])
            P.op("sp", [], reads=[("dbg", col0)])

        def phase_load():
            m = arena.mark()
            stg = [arena.alloc(D, F32) for _ in range(2)]
            for tt in range(NT):
                s = stg[tt % 2]
                src = x_d[tt * 128:(tt + 1) * 128, :] if tt < NTL else ctx_d[(tt - NTL) * 128:(tt - NTL + 1) * 128, :]
                P.dma("sp", I("dma_start", out=s.flat, in_=src), writes=[s])
                for half in range(2):
                    bank = banks[(tt % 2) * 2 + half]
                    P.pe([I("transpose", out=bank.ap[:, kk * 128:(kk + 1) * 128], in_=s.flat[:, (half * 4 + kk) * 128:(half * 4 + kk + 1) * 128], identity=ident32)
                          for kk in range(4)], reads=[s, CONST], writes=[bank])
                    dst = xT[:, half * 4:(half + 1) * 4, tt * 128:(tt + 1) * 128]
                    srcp = bank.ap[:].rearrange("p (k n) -> p k n", k=4)
                    if half == 0:
                        P.act(I("copy", out=dst, in_=srcp), reads=[bank], writes=[("xT", tt, half)])
                    else:
                        P.dve(I("tensor_copy", out=dst, in_=srcp), reads=[bank], writes=[("xT", tt, half)])
            arena.release(m)

        def phase_mod(l):
            m = arena.mark()
            cv32 = arena.alloc((KC, 2), F32)
            cvb = arena.alloc((KC, 2), BF16)
            P.dma("sp", I("dma_start", out=cv32.flat, in_=cv_d), writes=[cv32])
            P.act(I("activation", out=cvb.flat, in_=cv32.flat, func=AF.Silu), reads=[cv32], writes=[cvb])
            wm = [arena.alloc((KC, 768), BF16) for _ in range(2)]
            bank = banks[7]
            for piece in range(8):
                w = wm[piece % 2]
                P.dma("pool", I("dma_start", out=w.v, in_=wmod_d[l][:, piece * 768:(piece + 1) * 768].rearrange("(k p) n -> p k n", p=128)), writes=[w])
                ins = []
                for jj in range(6):
                    j = piece * 6 + jj
                    for k in range(KC):
                        ins.append(I("matmul", out=bank.ap[:, 2 * j:2 * j + 2], lhsT=w.v[:, k, jj * 128:(jj + 1) * 128], rhs=cvb.v[:, k, :],
                                     start=(k == 0), stop=(k == KC - 1)))
                P.pe(ins, reads=[w, cvb], writes=[bank])
            modT = arena.alloc((48, 2), F32)
            par = par_t[:, l, :]
            P.dve(I("tensor_tensor", out=modT.v, in0=bank.ap[:, 0:96].rearrange("p (j c) -> p j c", c=2),
                    in1=par[:, PO_BMOD:PO_BMOD + 48].unsqueeze(2).to_broadcast([128, 48, 2]), op=ALU.add),
                  reads=[bank, "par"], writes=[modT])
            ng = lambda f: par[:, PO_NG + f * 8:PO_NG + f * 8 + 8].unsqueeze(2).to_broadcast([128, KC, 2])
            mv = lambda i: modT.v[:, i * 8:(i + 1) * 8, :]
            P.dve([I("scalar_tensor_tensor", out=ab_t[:, 0], in0=mv(1), scalar=1.0, in1=ng(0), op0=ALU.add, op1=ALU.mult),
                   I("tensor_copy", out=ab_t[:, 1], in_=mv(0)),
                   I("tensor_tensor", out=ab_t[:, 2], in0=mv(2), in1=ng(1), op=ALU.mult),
                   I("scalar_tensor_tensor", out=ab_t[:, 3], in0=mv(4), scalar=1.0, in1=ng(2), op0=ALU.add, op1=ALU.mult),
                   I("tensor_copy", out=ab_t[:, 4], in_=mv(3)),
                   I("tensor_tensor", out=ab_t[:, 5], in0=mv(5), in1=ng(3), op=ALU.mult)],
                  reads=[modT, "par"], writes=["ab"])
            arena.release(m)

        def ms_block(src_k, n, bank, sqs, reads):
            for k in range(KC):
                sq = sqs[k % 2]
                P.act(I("activation", out=sq.flat[:, :n], in_=src_k(k), func=AF.Square), reads=reads, writes=[sq])
                P.pe(I("matmul", out=bank.ap[:, :n], lhsT=onesDb, rhs=sq.flat[:, :n], start=(k == 0), stop=(k == KC - 1)),
                     reads=[sq, CONST], writes=[bank])

        def rstd_from(bank, n, rs, scale=None):
            P.act(I("activation", out=rs.flat[:, :n], in_=bank.ap[:, :n], func=AF.Ln, bias=EPS, scale=(1.0 if scale is None else scale)),
                  reads=[bank], writes=[rs])
            P.act(I("activation", out=rs.flat[:, :n], in_=rs.flat[:, :n], func=AF.Exp, scale=-0.5), reads=[rs], writes=[rs])

        def sigmoid_from(bank, n, dst):
            P.act(I("activation", out=dst.flat[:, :n], in_=bank.ap[:, :n], func=AF.Exp, scale=-1.0), reads=[bank], writes=[dst])
            P.act(I("activation", out=dst.flat[:, :n], in_=dst.flat[:, :n], func=AF.Ln, bias=1.0, scale=1.0), reads=[dst], writes=[dst])
            P.act(I("activation", out=dst.flat[:, :n], in_=dst.flat[:, :n], func=AF.Exp, scale=-1.0), reads=[dst], writes=[dst])

        def phase_norm_h(ai, bi_):
            m = arena.mark()
            sqs = [arena.alloc(512, BF16) for _ in range(2)]
            rs = arena.alloc(512, F32)
            tmps = [arena.alloc(512, F32) for _ in range(2)]
            bank = banks[6]
            for (t0, n) in blocks:
                col = 0 if t0 < L else 1
                ms_block(lambda k: xT[:, k, t0:t0 + n], n, bank, sqs, xkk(t0, n))
                rstd_from(bank, n, rs)
                for k in range(KC):
                    tmp = tmps[k % 2]
                    P.dve(I("scalar_tensor_tensor", out=tmp.flat[:, :n], in0=xT[:, k, t0:t0 + n], scalar=ab_t[:, ai, k, col:col + 1],
                            in1=rs.flat[:, :n], op0=ALU.mult, op1=ALU.mult),
                          reads=xkk(t0, n) + [rs, "ab"], writes=[tmp])
                    P.act(I("activation", out=hT[:, k, t0:t0 + n], in_=tmp.flat[:, :n], func=AF.Identity,
                            bias=ab_t[:, bi_, k, col:col + 1], scale=1.0),
                          reads=[tmp, "ab"], writes=hk(t0, n))
            arena.release(m)

        def load_w(dram_ap, buf):
            P.dma("pool", I("dma_start", out=buf.v, in_=dram_ap.rearrange("(k p) n -> p k n", p=128)), writes=[buf])

        def proj_fm(bank, w, c0, M, t0, n):
            P.pe([I("matmul", out=bank.ap[0:M, :n], lhsT=w.v[:, k, c0:c0 + M], rhs=hT[:, k, t0:t0 + n], start=(k == 0), stop=(k == KC - 1))
                  for k in range(KC)], reads=[w] + hk(t0, n), writes=[bank])

        def scan(H, hg, la_of, dt_of, kt_chunk, n_kt, kidx, qg_of, vtok_of, extra_terms, extra_reads, finish, dbuf, shared=None, skip_out=(), pre=None):
            NG = H // hg
            HP = H * 64
            units = [(0, min(H, 4))] + ([(4, H - 4)] if H > 4 else [])
            NU = len(units)
            NPAR = 2 if dbuf else 1
            bS, bG, bT, bY = banks[0], banks[3], banks[4], banks[5]
            pbanks = [banks[1], banks[2], banks[6], banks[7]]
            bZ, bF = bG, bS
            m = arena.mark()
            sb_store = arena.alloc((NT, HP), BF16)
            Sst = [arena.alloc(HP, F32) for _ in range(2)]
            sfb = shared if shared is not None else arena.alloc(HP, BF16)
            mk2 = lambda shape, dt: [[arena.alloc(shape, dt) for _ in range(2)] for _ in range(NPAR)]
            pt = mk2(2 * H, F32)
            dif = mk2(H, F32)
            wv_ = mk2(H, F32)
            dec = mk2(H, F32)
            cw = mk2(H, F32)
            Vdt = mk2(HP, BF16) if dt_of(0) is not None else None
            E = mk2((H, 128), BF16)
            Ebc = mk2((H, 128), BF16)
            Gm = mk2((NG, 128), BF16)
            MT, QsT = E, Ebc
            Vw = sfb if shared is not None else arena.alloc(HP, BF16)
            ktok = arena.alloc((n_kt, 128), BF16)
            R = [[arena.alloc((hn, 128), F32) for (_, hn) in units] for _ in range(2)]
            Ud = [Uf32, Ub32]
            md = [mFb, mBb]
            hq = lambda ap_: ap_.rearrange("p (h q) -> p h q", h=H)
            g4 = lambda ap_: ap_.rearrange("p (g a) i -> p g a i", g=NG)

            def small(c, d, p):
                lap, lar = la_of(c)
                P.pe([I("matmul", out=bS.ap[:, 0:H], lhsT=Ud[d], rhs=lap[:, d, :], start=True, stop=True),
                      I("matmul", out=bS.ap[:, H:2 * H], lhsT=ones32, rhs=lap[:, d, :], start=True, stop=True)],
                     reads=[CONST] + lar, writes=[bS])
                P.act(I("copy", out=pt[p][d].flat, in_=bS.ap[:, 0:2 * H]), reads=[bS], writes=[pt[p][d]])
                P.dve(I("tensor_tensor", out=dif[p][d].flat, in0=pt[p][d].flat[:, H:2 * H], in1=pt[p][d].flat[:, 0:H], op=ALU.subtract),
                      reads=[pt[p][d]], writes=[dif[p][d]])
                P.act(I("activation", out=wv_[p][d].flat, in_=dif[p][d].flat, func=AF.Exp), reads=[dif[p][d]], writes=[wv_[p][d]])
                P.act(I("activation", out=dec[p][d].flat, in_=pt[p][d].flat[:, H:2 * H], func=AF.Exp), reads=[pt[p][d]], writes=[dec[p][d]])
                dtv = dt_of(c)
                if dtv is not None:
                    P.dve(I("tensor_tensor", out=cw[p][d].flat, in0=wv_[p][d].flat, in1=dtv[0][:, d, :], op=ALU.mult),
                          reads=[wv_[p][d]] + dtv[1], writes=[cw[p][d]])
                else:
                    P.dve(I("tensor_copy", out=cw[p][d].flat, in_=wv_[p][d].flat), reads=[wv_[p][d]], writes=[cw[p][d]])

            def dstate(c, d, S, p):
                vt, vr = vtok_of(c)
                P.dve(I("tensor_tensor", out=hq(Vw.flat), in0=hq(vt), in1=cw[p][d].flat.unsqueeze(2).to_broadcast([128, H, 64]), op=ALU.mult),
                      reads=vr + [cw[p][d]], writes=[Vw])
                bt16 = b16(bT)
                P.pe([I("transpose", out=bt16[:, i * 128:(i + 1) * 128], in_=kt_chunk(i, c)[0], identity=identb) for i in range(n_kt)],
                     reads=[CONST] + kt_chunk(0, c)[1], writes=[bT])
                P.act(I("copy", out=ktok.flat, in_=bt16[:, 0:n_kt * 128]), reads=[bT], writes=[ktok])
                P.pe([I("matmul", out=bT.ap[:, g * hg * 64:(g + 1) * hg * 64], lhsT=ktok.v[:, kidx(g), :], rhs=Vw.flat[:, g * hg * 64:(g + 1) * hg * 64],
                        start=True, stop=True) for g in range(NG)], reads=[ktok, Vw], writes=[bT])
                P.dve(I("tensor_tensor", out=hq(S.flat), in0=hq(S.flat), in1=dec[p][d].flat.unsqueeze(2).to_broadcast([128, H, 64]), op=ALU.mult),
                      reads=[S, dec[p][d]], writes=[S])
                P.dve(I("tensor_tensor", out=S.flat, in0=S.flat, in1=bT.ap[:, 0:HP], op=ALU.add), reads=[S, bT], writes=[S])

            P.pool(I("memset", ap=Sst[1].flat, constant=0.0), writes=[Sst[1]])
            P.pool(I("memset", ap=Sst[0].flat, constant=0.0), writes=[Sst[0]])
            for c in bwd_order:
                P.act(I("copy", out=sb_store.v[:, c, :], in_=Sst[1].flat), reads=[Sst[1]], writes=[sb_store.sub(c * HP, (c + 1) * HP)])
                small(c, 1, 0)
                dstate(c, 1, Sst[1], 0)

            loc = {}

            def local_part(c, p):
                qg, qgr = qg_of(c, p)
                lap, lar = la_of(c)
                loc[c] = (qg, qgr)
                P.pe([I("matmul", out=bG.ap[:, g * 128:(g + 1) * 128], lhsT=kt_chunk(kidx(g), c)[0], rhs=qg[:, g, :], start=True, stop=True)
                      for g in range(NG)], reads=kt_chunk(0, c)[1] + qgr, writes=[bG])
                for d in range(2):
                    P.dve(I("tensor_tensor", out=Gm[p][d].v, in0=bG.ap[:, 0:NG * 128].rearrange("p (g i) -> p g i", g=NG),
                            in1=md[d].unsqueeze(1).to_broadcast([128, NG, 128]), op=ALU.mult),
                          reads=[bG, CONST], writes=[Gm[p][d]])
                if pre is not None:
                    pre(c, p, bG)
                for d in range(2):
                    small(c, d, p)
                chains = [(d, ui, h0, hn) for d in range(2) for ui, (h0, hn) in enumerate(units)]
                bk = lambda d, ui: pbanks[(d * NU + ui) % 4]
                for (d, ui, h0, hn) in chains:
                    P.dve(I("tensor_tensor", out=R[d][ui].v, in0=Ud[d].unsqueeze(1).to_broadcast([128, hn, 128]),
                            in1=lap[:, d, h0:h0 + hn].unsqueeze(2).to_broadcast([128, hn, 128]), op=ALU.mult),
                          reads=[CONST] + lar, writes=[R[d][ui]])
                for (d, ui, h0, hn) in chains:
                    P.pe(I("matmul", out=bk(d, ui).ap[:, 0:hn * 128], lhsT=ones32, rhs=R[d][ui].flat, start=True, stop=True),
                         reads=[CONST, R[d][ui]], writes=[bk(d, ui)])
                for (d, ui, h0, hn) in chains:
                    P.act(I("activation", out=Ebc[p][d].flat[:, h0 * 128:(h0 + hn) * 128], in_=bk(d, ui).ap[:, 0:hn * 128], func=AF.Exp),
                          reads=[bk(d, ui)], writes=[Ebc[p][d].sub(h0 * 128, (h0 + hn) * 128)])
                    P.dve(I("tensor_tensor", out=R[d][ui].v, in0=bk(d, ui).ap[:, 0:hn * 128].rearrange("p (h i) -> p h i", h=hn),
                            in1=pt[p][d].flat[:, h0:h0 + hn].unsqueeze(2).to_broadcast([128, hn, 128]), op=ALU.subtract),
                          reads=[bk(d, ui), pt[p][d]], writes=[R[d][ui]])
                for (d, ui, h0, hn) in chains:
                    P.dve(I("tensor_tensor", out=R[d][ui].v, in0=R[d][ui].v, in1=md[d].unsqueeze(1).to_broadcast([128, hn, 128]), op=ALU.mult),
                          reads=[R[d][ui], CONST], writes=[R[d][ui]])
                for (d, ui, h0, hn) in chains:
                    P.act(I("activation", out=E[p][d].flat[:, h0 * 128:(h0 + hn) * 128], in_=R[d][ui].flat, func=AF.Exp),
                          reads=[R[d][ui]], writes=[E[p][d].sub(h0 * 128, (h0 + hn) * 128)])
                vt, vr = vtok_of(c)
                dtv = dt_of(c)
                for d in range(2):
                    P.dve(I("tensor_tensor", out=g4(QsT[p][d].v), in0=g4(Ebc[p][d].v), in1=qg.unsqueeze(2).to_broadcast([128, NG, hg, 128]), op=ALU.mult),
                          reads=[Ebc[p][d]] + qgr, writes=[QsT[p][d]])
                    if dtv is not None:
                        P.pool(I("tensor_tensor", out=hq(Vdt[p][d].flat), in0=hq(vt), in1=dtv[0][:, d, :].unsqueeze(2).to_broadcast([128, H, 64]), op=ALU.mult),
                               reads=vr + dtv[1], writes=[Vdt[p][d]])
                for d in range(2):
                    P.dve(I("tensor_tensor", out=g4(MT[p][d].v), in0=g4(E[p][d].v), in1=Gm[p][d].v.unsqueeze(2).to_broadcast([128, NG, hg, 128]), op=ALU.mult),
                          reads=[E[p][d], Gm[p][d]], writes=[MT[p][d]])

            def state_part(c, p):
                vt, vr = vtok_of(c)
                dtv = dt_of(c)
                P.act(I("copy", out=sfb.flat, in_=Sst[0].flat), reads=[Sst[0]], writes=[sfb])
                ins = []
                for h in range(H):
                    out = bY.ap[(h % 2) * 64:(h % 2) * 64 + 64, (h // 2) * 128:(h // 2 + 1) * 128]
                    terms = []
                    for d in range(2):
                        lhs = Vdt[p][d].flat[:, h * 64:(h + 1) * 64] if dtv is not None else vt[:, h * 64:(h + 1) * 64]
                        terms.append((lhs, MT[p][d].v[:, h, :]))
                    terms.append((sfb.flat[:, h * 64:(h + 1) * 64], QsT[p][0].v[:, h, :]))
                    terms.append((sb_store.v[:, c, h * 64:(h + 1) * 64], QsT[p][1].v[:, h, :]))
                    terms += extra_terms(c, h)
                    for i, (lh, rh) in enumerate(terms):
                        ins.append(I("matmul", out=out, lhsT=lh, rhs=rh, start=(i == 0), stop=(i == len(terms) - 1)))
                P.pe(ins, reads=([Vdt[p][0], Vdt[p][1]] if dtv is not None else []) + [MT[p][0], MT[p][1], QsT[p][0], QsT[p][1], sfb, sb_store.sub(c * HP, (c + 1) * HP)] + vr + extra_reads,
                     writes=[bY])
                dstate(c, 0, Sst[0], p)
                finish(c, bY, bF, p)

            order = []
            for c in fwd_order:
                if c in skip_out:
                    small(c, 0, 0)
                    dstate(c, 0, Sst[0], 0)
                else:
                    order.append(c)
            if dbuf:
                local_part(order[0], 0)
                for i, c in enumerate(order):
                    if i + 1 < len(order):
                        local_part(order[i + 1], (i + 1) % 2)
                    state_part(c, i % 2)
            else:
                for c in order:
                    local_part(c, 0)
                    state_part(c, 0)
            arena.release(m)

        def phase_ssd(l, oT_ssd, ctx_out):
            par = par_t[:, l, :]
            m0 = arena.mark()
            BT = arena.alloc((2, T), BF16)
            CT = arena.alloc((2, T), BF16)
            Xtok = arena.alloc((NT, 384), BF16)
            dtv = arena.alloc((NT, 2, 6), F32)
            lav = arena.alloc((NT, 2, 6), F32)
            wz = arena.alloc((KC, 384), BF16)
            load_w(win_d[l][:, CO_Z:CO_Z + 384], wz)
            m1 = arena.mark()
            wdt = arena.alloc((KC, 12), BF16)
            load_w(win_d[l][:, CO_DT:CO_DT + 12], wdt)
            bank = banks[0]
            P.pe([I("matmul", out=bank.ap[:, tt * 12:(tt + 1) * 12], lhsT=hT[:, k, tt * 128:(tt + 1) * 128], rhs=wdt.v[:, k, :],
                    start=(k == 0), stop=(k == KC - 1)) for tt in range(NT) for k in range(KC)],
                 reads=[wdt] + hk(0, T), writes=[bank])
            d3 = lambda b: b.flat.rearrange("p (t c) -> p t c", c=12)
            P.dve(I("tensor_tensor", out=d3(dtv), in0=bank.ap[:, 0:NT * 12].rearrange("p (t c) -> p t c", c=12),
                    in1=par[:, PO_DTB:PO_DTB + 12].unsqueeze(1).to_broadcast([128, NT, 12]), op=ALU.add),
                  reads=[bank, "par"], writes=[dtv])
            aexp = sml_t[:, 0:12]
            P.act(I("activation", out=dtv.flat, in_=dtv.flat, func=AF.Exp), reads=[dtv], writes=[dtv])
            P.act(I("activation", out=dtv.flat, in_=dtv.flat, func=AF.Ln, bias=1.0, scale=1.0), reads=[dtv], writes=[dtv])
            P.act(I("activation", out=aexp, in_=par[:, PO_ALOG:PO_ALOG + 12], func=AF.Exp), reads=["par"], writes=["aexp"])
            P.dve(I("scalar_tensor_tensor", out=d3(lav), in0=d3(dtv), scalar=-1.0, in1=aexp.unsqueeze(1).to_broadcast([128, NT, 12]),
                    op0=ALU.mult, op1=ALU.mult), reads=[dtv, "aexp"], writes=[lav])
            if stop_after == "ssd_a":
                dump(lav.flat, 0, NT * 12, [lav])
                raise _Stop()
            P.dve([I("tensor_scalar", out=dI_t[:, h, :], in0=identb, scalar1=par[:, PO_SD + h:PO_SD + h + 1], scalar2=None, op0=ALU.mult) for h in range(6)],
                  reads=[CONST, "par"], writes=["dI"])
            rawp = arena.alloc(T + 8, F32)
            acc = arena.alloc(T, F32)
            xs = [arena.alloc(T, BF16) for _ in range(2)]
            wx = [arena.alloc((KC, 128), BF16) for _ in range(2)]
            P.pool(I("memset", ap=rawp.flat, constant=0.0), writes=[rawp])
            roff = lambda t0: (2 + t0) if t0 < L else (t0 + 6)
            for cc in range(7):
                w = wx[cc % 2]
                load_w(win_d[l][:, CO_XBC + cc * 128:CO_XBC + (cc + 1) * 128], w)
                for bi, (t0, n) in enumerate(blocks):
                    bank = banks[1 + bi % 2]
                    proj_fm(bank, w, 0, 128, t0, n)
                    P.act(I("copy", out=rawp.flat[:, roff(t0):roff(t0) + n], in_=bank.ap[:, :n]), reads=[bank], writes=[rawp])
                for (s0, sn, ro) in [(0, L, 2), (L, LC, L + 6)]:
                    sa = acc.sub(s0, s0 + sn)
                    P.dve(I("tensor_scalar", out=sa.ap, in0=rawp.flat[:, ro - 2:ro - 2 + sn], scalar1=par[:, PO_CW + cc * 5:PO_CW + cc * 5 + 1],
                            scalar2=par[:, PO_CB + cc:PO_CB + cc + 1], op0=ALU.mult, op1=ALU.add), reads=[rawp, "par"], writes=[sa])
                    for j in range(1, 5):
                        P.dve(I("scalar_tensor_tensor", out=sa.ap, in0=rawp.flat[:, ro - 2 + j:ro - 2 + j + sn],
                                scalar=par[:, PO_CW + cc * 5 + j:PO_CW + cc * 5 + j + 1], in1=sa.ap, op0=ALU.mult, op1=ALU.add),
                              reads=[rawp, "par", sa], writes=[sa])
                if cc < 3:
                    dst = xs[cc % 2]
                    dflat_ = dst.flat
                elif cc < 5:
                    dst = BT.sub((cc - 3) * T, (cc - 2) * T)
                    dflat_ = dst.ap
                else:
                    dst = CT.sub((cc - 5) * T, (cc - 4) * T)
                    dflat_ = dst.ap
                P.act(I("activation", out=dflat_, in_=acc.flat, func=AF.Silu), reads=[acc], writes=[dst])
                if cc < 3:
                    for t8 in range(0, NT, 8):
                        nt8 = min(8, NT - t8)
                        bank = banks[3 + (t8 // 8) % 2]
                        P.pe([I("transpose", out=b16(bank)[:, i * 128:(i + 1) * 128], in_=dflat_[:, (t8 + i) * 128:(t8 + i + 1) * 128], identity=identb)
                              for i in range(nt8)], reads=[dst, CONST], writes=[bank])
                        P.dve(I("tensor_copy", out=Xtok.v[:, t8:t8 + nt8, cc * 128:(cc + 1) * 128],
                                in_=b16(bank)[:, 0:nt8 * 128].rearrange("p (t f) -> p t f", f=128)),
                              reads=[bank], writes=[Xtok])
            if stop_after == "ssd_b":
                dump(Xtok.flat[:, 0:768], 0, 768, [Xtok])
                raise _Stop()
            arena.release(m1)
            sz = arena.alloc(384, F32)
            vv = arena.alloc(384, F32)
            sq = arena.alloc(384, BF16)
            rs = arena.alloc(128, F32)

            def pre(c, p, bZ):
                P.pe([I("matmul", out=bZ.ap[:, pc * 128:(pc + 1) * 128], lhsT=wz.v[:, k, pc * 128:(pc + 1) * 128], rhs=hT[:, k, c * 128:(c + 1) * 128],
                        start=(k == 0), stop=(k == KC - 1)) for pc in range(3) for k in range(KC)],
                     reads=[wz] + hk(c * 128, 128), writes=[bZ])
                sigmoid_from(bZ, 384, sz)
                P.dve(I("tensor_tensor", out=sz.flat, in0=bZ.ap[:, 0:384], in1=sz.flat, op=ALU.mult), reads=[bZ, sz], writes=[sz])

            def finish(c, bY, bF, p):
                P.dve(I("tensor_tensor", out=vv.flat, in0=bY.ap[:, 0:384], in1=sz.flat, op=ALU.mult), reads=[bY, sz], writes=[vv])
                P.act(I("activation", out=sq.flat, in_=vv.flat, func=AF.Square), reads=[vv], writes=[sq])
                P.pe([I("matmul", out=bF.ap[:, 0:128], lhsT=ones1b, rhs=sq.flat[:, pc * 128:(pc + 1) * 128], start=(pc == 0), stop=(pc == 2)) for pc in range(3)],
                     reads=[sq, CONST], writes=[bF])
                rstd_from(bF, 128, rs, scale=1.0 / 384.0)
                P.dve([I("scalar_tensor_tensor", out=oT_ssd.v[:, pc, c * 128:(c + 1) * 128], in0=vv.flat[:, pc * 128:(pc + 1) * 128],
                         scalar=par[:, PO_SNG + pc:PO_SNG + pc + 1], in1=rs.flat, op0=ALU.mult, op1=ALU.mult) for pc in range(3)],
                      reads=[vv, rs, "par"], writes=[oT_ssd.sub(pc * T + c * 128, pc * T + (c + 1) * 128) for pc in range(3)])

            scan(H=6, hg=3,
                 la_of=lambda c: (lav.v[:, c], [lav]),
                 dt_of=lambda c: (dtv.v[:, c], [dtv]),
                 kt_chunk=lambda i, c: (BT.v[:, i, c * 128:(c + 1) * 128], [BT]),
                 n_kt=2, kidx=lambda g: g,
                 qg_of=lambda c, p: (CT.v[:, :, c * 128:(c + 1) * 128], [CT]),
                 vtok_of=lambda c: (Xtok.v[:, c, :], [Xtok]),
                 extra_terms=lambda c, h: [(Xtok.v[:, c, h * 64:(h + 1) * 64], dI_t[:, h, :])],
                 extra_reads=["dI"],
                 finish=finish, dbuf=False, shared=sq, skip_out=(() if ctx_out else tuple(ctx_chunks)), pre=pre)
            arena.release(m0)

        def rope_store(src, src_reads, n, t0, dst_ap, dst_buf, ropeb, scr, scale=1.0, dsts=None):
            qb, t1, t2 = scr
            bR = banks[7]
            if scale == 1.0:
                P.act(I("copy", out=qb.flat[:, :n], in_=src), reads=src_reads, writes=[qb])
            else:
                P.act(I("mul", out=qb.flat[:, :n], in_=src, mul=scale), reads=src_reads, writes=[qb])
            P.pe(I("matmul", out=bR.ap[:, :n], lhsT=rotb, rhs=qb.flat[:, :n], start=True, stop=True), reads=[qb, CONST], writes=[bR])
            P.dve(I("tensor_tensor", out=t1.flat[:, :n], in0=qb.flat[:, :n], in1=ropeb.v[:, 0, t0:t0 + n], op=ALU.mult), reads=[qb, ropeb], writes=[t1])
            P.dve(I("tensor_tensor", out=t2.flat[:, :n], in0=bR.ap[:, :n], in1=ropeb.v[:, 1, t0:t0 + n], op=ALU.mult), reads=[bR, ropeb], writes=[t2])
            if dsts is None:
                dsts = [(0, 128, dst_ap, dst_buf)]
            for (p0, p1, d_ap, d_buf) in dsts:
                P.pool(I("tensor_tensor", out=d_ap, in0=t1.flat[p0:p1, :n], in1=t2.flat[p0:p1, :n], op=ALU.add), reads=[t1, t2], writes=[d_buf])

        def load_rope():
            ropeb = arena.alloc((2, T), BF16)
            P.dma("pool", I("dma_start", out=ropeb.v, in_=rope_d.rearrange("p (a t) -> p a t", a=2)), writes=[ropeb])
            return ropeb

        def phase_ret(l, oT_ret, ctx_out):
            par = par_t[:, l, :]
            m0 = arena.mark()
            QTr = arena.alloc((2, T), BF16)
            KTr = arena.alloc((2, T), BF16)
            Vtok = arena.alloc((NT, 256), BF16)
            wg = arena.alloc((KC, 256), BF16)
            load_w(win_d[l][:, CO_GR:CO_GR + 256], wg)
            lar = arena.alloc((2, 4), F32)
            m1 = arena.mark()
            ropeb = load_rope()
            wq = arena.alloc((KC, 256), BF16)
            wk_ = arena.alloc((KC, 256), BF16)
            wv = arena.alloc((KC, 256), BF16)
            load_w(win_d[l][:, CO_QR:CO_QR + 256], wq)
            load_w(win_d[l][:, CO_KR:CO_KR + 256], wk_)
            load_w(win_d[l][:, CO_VR:CO_VR + 256], wv)
            scr = (arena.alloc(512, BF16), arena.alloc(512, F32), arena.alloc(512, F32))
            tl = sml_t[:, 16:24]
            P.act(I("activation", out=tl, in_=par[:, PO_RLOG:PO_RLOG + 8], func=AF.Exp, scale=-1.0), reads=["par"], writes=["tl"])
            P.act(I("activation", out=tl, in_=tl, func=AF.Ln, bias=1.0, scale=1.0), reads=["tl"], writes=["tl"])
            P.dve(I("tensor_scalar", out=lar.flat, in0=tl, scalar1=-1.0, scalar2=None, op0=ALU.mult), reads=["tl"], writes=[lar])
            for (w, dstT, scale) in [(wq, QTr, 1.0), (wk_, KTr, 0.125)]:
                for pc in range(2):
                    for bi, (t0, n) in enumerate(blocks):
                        bank = banks[1 + bi % 2]
                        proj_fm(bank, w, pc * 128, 128, t0, n)
                        rope_store(bank.ap[:, :n], [bank], n, t0, dstT.v[:, pc, t0:t0 + n], dstT.sub(pc * T + t0, pc * T + t0 + n), ropeb, scr, scale)
            for tt in range(0, NT, 2):
                n2 = min(2, NT - tt)
                bank = banks[3 + (tt // 2) % 2]
                P.pe([I("matmul", out=bank.ap[:, i * 256:(i + 1) * 256], lhsT=hT[:, k, (tt + i) * 128:(tt + i + 1) * 128], rhs=wv.v[:, k, :],
                        start=(k == 0), stop=(k == KC - 1)) for i in range(n2) for k in range(KC)],
                     reads=[wv] + hk(tt * 128, n2 * 128), writes=[bank])
                P.act(I("copy", out=Vtok.v[:, tt:tt + n2, :], in_=bank.ap[:, 0:n2 * 256].rearrange("p (t f) -> p t f", f=256)),
                      reads=[bank], writes=[Vtok])
            arena.release(m1)
            Qz = [arena.alloc((4, 128), BF16) for _ in range(2)]
            y32 = arena.alloc(256, F32)
            yb = arena.alloc(256, BF16)
            yc = arena.alloc(256, F32)
            sq = arena.alloc(256, BF16)
            rs = arena.alloc(256, F32)
            sgs = [arena.alloc(256, F32) for _ in range(2)]

            def qg_of(c, p):
                Qz_ = Qz[p]
                P.dve(I("tensor_tensor", out=Qz_.flat.rearrange("p (a b i) -> p a b i", a=2, b=2),
                        in0=QTr.v[:, :, c * 128:(c + 1) * 128].unsqueeze(2).to_broadcast([128, 2, 2, 128]),
                        in1=hmb.unsqueeze(1).unsqueeze(3).to_broadcast([128, 2, 2, 128]), op=ALU.mult),
                      reads=[QTr, CONST], writes=[Qz_])
                return (Qz_.v, [Qz_])

            def pre(c, p, bZ):
                sg = sgs[p]
                P.pe([I("matmul", out=bZ.ap[:, pc * 128:(pc + 1) * 128], lhsT=wg.v[:, k, pc * 128:(pc + 1) * 128], rhs=hT[:, k, c * 128:(c + 1) * 128],
                        start=(k == 0), stop=(k == KC - 1)) for pc in range(2) for k in range(KC)],
                     reads=[wg] + hk(c * 128, 128), writes=[bZ])
                sigmoid_from(bZ, 256, sg)
                P.dve(I("tensor_tensor", out=sg.flat, in0=bZ.ap[:, 0:256], in1=sg.flat, op=ALU.mult), reads=[bZ, sg], writes=[sg])

            def finish(c, bY, bF, p):
                sg = sgs[p]
                P.act(I("copy", out=y32.flat, in_=bY.ap[:, 0:256]), reads=[bY], writes=[y32])
                P.dve(I("tensor_copy", out=yb.flat, in_=y32.flat), reads=[y32], writes=[yb])
                P.pe(I("matmul", out=bF.ap[:, 0:256], lhsT=blk64b, rhs=yb.flat, start=True, stop=True), reads=[yb, CONST], writes=[bF])
                P.dve(I("tensor_tensor", out=yc.flat, in0=y32.flat, in1=bF.ap[:, 0:256], op=ALU.subtract), reads=[y32, bF], writes=[yc])
                P.act(I("activation", out=sq.flat, in_=yc.flat, func=AF.Square), reads=[yc], writes=[sq])
                P.pe(I("matmul", out=bF.ap[:, 0:256], lhsT=blk64b, rhs=sq.flat, start=True, stop=True), reads=[sq, CONST], writes=[bF])
                rstd_from(bF, 256, rs)
                P.dve(I("tensor_tensor", out=yc.flat, in0=yc.flat, in1=rs.flat, op=ALU.mult), reads=[yc, rs], writes=[yc])
                P.act([I("activation", out=yc.flat[:, pc * 128:(pc + 1) * 128], in_=yc.flat[:, pc * 128:(pc + 1) * 128], func=AF.Identity,
                         bias=par[:, PO_RGB + pc:PO_RGB + pc + 1], scale=par[:, PO_RGG + pc:PO_RGG + pc + 1]) for pc in range(2)],
                      reads=[yc, "par"], writes=[yc])
                P.dve(I("tensor_tensor", out=oT_ret.v[:, :, c * 128:(c + 1) * 128], in0=yc.flat.rearrange("p (a i) -> p a i", a=2),
                        in1=sg.flat.rearrange("p (a i) -> p a i", a=2), op=ALU.mult),
                      reads=[yc, sg], writes=[oT_ret.sub(pc * T + c * 128, pc * T + (c + 1) * 128) for pc in range(2)])

            scan(H=4, hg=1,
                 la_of=lambda c: (lar.v, [lar]),
                 dt_of=lambda c: None,
                 kt_chunk=lambda i, c: (KTr.v[:, i, c * 128:(c + 1) * 128], [KTr]),
                 n_kt=2, kidx=lambda g: g // 2,
                 qg_of=qg_of,
                 vtok_of=lambda c: (Vtok.v[:, c, :], [Vtok]),
                 extra_terms=lambda c, h: [],
                 extra_reads=[],
                 finish=finish, dbuf=True, skip_out=(() if ctx_out else tuple(ctx_chunks)), pre=pre)
            arena.release(m0)

        def phase_att(l, oT_att, ctx_out):
            par = par_t[:, l, :]
            m0 = arena.mark()
            ropeb = load_rope()
            kTd = arena.alloc((2, T), BF16)
            Vext = arena.alloc((NT, 2, 128), BF16)
            qz = [arena.alloc(T, BF16) for _ in range(2)]
            mA = arena.mark()
            scr = (arena.alloc(512, BF16), arena.alloc(512, F32), arena.alloc(512, F32))
            sqb = arena.alloc(512, BF16)
            rs = arena.alloc(512, F32)
            qn = arena.alloc(512, F32)
            mB = arena.mark()
            gq8 = sml_t[:, 32:33]
            P.dve(I("tensor_scalar", out=gq8, in0=par[:, PO_QG:PO_QG + 1], scalar1=0.125, scalar2=None, op0=ALU.mult), reads=["par"], writes=["gq8"])
            bN = banks[6]

            def normrope(bank, n, t0, gcol, greads, dsts):
                P.act(I("activation", out=sqb.flat[:, :n], in_=bank.ap[:, :n], func=AF.Square), reads=[bank], writes=[sqb])
                P.pe(I("matmul", out=bN.ap[:, :n], lhsT=blk64b, rhs=sqb.flat[:, :n], start=True, stop=True), reads=[sqb, CONST], writes=[bN])
                rstd_from(bN, n, rs)
                P.dve(I("scalar_tensor_tensor", out=qn.flat[:, :n], in0=bank.ap[:, :n], scalar=gcol, in1=rs.flat[:, :n], op0=ALU.mult, op1=ALU.mult),
                      reads=[bank, rs] + greads, writes=[qn])
                rope_store(qn.flat[:, :n], [qn], n, t0, None, None, ropeb, scr, dsts=dsts)

            wkd = arena.alloc((2, KC, 128), BF16)
            wv = arena.alloc((KC, 128), BF16)
            for g in range(2):
                for hh in range(2):
                    P.dma("pool", I("dma_start", out=wkd.v[:, g, :, hh * 64:(hh + 1) * 64],
                                    in_=win_d[l][:, CO_KA + g * 64:CO_KA + (g + 1) * 64].rearrange("(k p) n -> p k n", p=128)), writes=[wkd])
            load_w(win_d[l][:, CO_VA:CO_VA + 128], wv)
            P.pool(I("memset", ap=Vext.v[:, :, :, 64:65], constant=1.0), writes=[Vext])
            P.pool(I("memset", ap=qz[0].flat[64:128, :], constant=0.0), writes=[qz[0]])
            P.pool(I("memset", ap=qz[1].flat[0:64, :], constant=0.0), writes=[qz[1]])
            for g in range(2):
                for bi, (t0, n) in enumerate(blocks):
                    bank = banks[bi % 2]
                    P.pe([I("matmul", out=bank.ap[:, :n], lhsT=wkd.v[:, g, k, :], rhs=hT[:, k, t0:t0 + n], start=(k == 0), stop=(k == KC - 1))
                          for k in range(KC)], reads=[wkd] + hk(t0, n), writes=[bank])
                    normrope(bank, n, t0, par[:, PO_KG:PO_KG + 1], ["par"], [(0, 128, kTd.v[:, g, t0:t0 + n], kTd.sub(g * T + t0, g * T + t0 + n))])
            for tt in range(0, NT, 4):
                n4 = min(4, NT - tt)
                bank = banks[2 + (tt // 4) % 2]
                P.pe([I("matmul", out=bank.ap[:, i * 128:(i + 1) * 128], lhsT=hT[:, k, (tt + i) * 128:(tt + i + 1) * 128], rhs=wv.v[:, k, :],
                        start=(k == 0), stop=(k == KC - 1)) for i in range(n4) for k in range(KC)],
                     reads=[wv] + hk(tt * 128, n4 * 128), writes=[bank])
                bv = bank.ap[:, 0:n4 * 128].rearrange("p (t g d) -> p t g d", g=2, d=64)
                P.act(I("copy", out=Vext.v[:, tt:tt + n4, :, 0:64], in_=bv), reads=[bank], writes=[Vext])
                P.dve(I("tensor_copy", out=Vext.v[:, tt:tt + n4, :, 65:128], in_=bv[:, :, :, 0:63]), reads=[bank], writes=[Vext])
            arena.release(mB)
            wq = arena.alloc((KC, 128), BF16)
            end_off = arena.off
            arena.off = mA
            PT = [arena.alloc(512, BF16) for _ in range(4)]
            rr = arena.alloc(512, F32)
            bcs = arena.alloc(512, F32)
            tn = arena.alloc(512, BF16)
            assert arena.off <= mB
            arena.off = end_off
            pti = 0
            sbanks = [banks[0], banks[1], banks[2], banks[5]]
            obanks = [banks[3], banks[4]]
            bBc = banks[7]
            si = 0
            oi = 0
            for qc in range(3):
                w = wq
                load_w(win_d[l][:, CO_QA + qc * 128:CO_QA + (qc + 1) * 128], w)
                for bi, (t0, n) in enumerate(blocks):
                    bank = banks[bi % 2]
                    proj_fm(bank, w, 0, 128, t0, n)
                    normrope(bank, n, t0, gq8, ["gq8"], [(0, 64, qz[0].flat[0:64, t0:t0 + n], qz[0].sub(t0, t0 + n)),
                                                         (64, 128, qz[1].flat[64:128, t0:t0 + n], qz[1].sub(t0, t0 + n))])
                its = []
                for (t0, n) in blocks:
                    is_ctx = t0 >= L
                    if is_ctx and not ctx_out:
                        continue
                    kcs = ctx_chunks if is_ctx else list(range(NT))
                    for hh in range(2):
                        bO = obanks[oi % 2]
                        oi += 1
                        for ki, kc in enumerate(kcs):
                            its.append(dict(t0=t0, n=n, hh=hh, kc=kc, first=(ki == 0), last=(ki == len(kcs) - 1), bO=bO,
                                            bS=sbanks[si % 4], pt=PT[pti % 4]))
                            si += 1
                            pti += 1

                def emit_S(it):
                    t0, n, hh, kc, bSx, pt_ = it["t0"], it["n"], it["hh"], it["kc"], it["bS"], it["pt"]
                    g = (2 * qc + hh) // 3
                    P.pe(I("matmul", out=bSx.ap[:, :n], lhsT=kTd.v[:, g, kc * 128:(kc + 1) * 128], rhs=qz[hh].flat[:, t0:t0 + n], start=True, stop=True),
                         reads=[kTd.sub(g * T + kc * 128, g * T + (kc + 1) * 128), qz[hh].sub(t0, t0 + n)], writes=[bSx])
                    P.act(I("activation", out=pt_.flat[:, :n], in_=bSx.ap[:, :n], func=AF.Exp), reads=[bSx], writes=[pt_])

                def emit_PV(it):
                    t0, n, hh, kc, bO, pt_ = it["t0"], it["n"], it["hh"], it["kc"], it["bO"], it["pt"]
                    first, last = it["first"], it["last"]
                    g = (2 * qc + hh) // 3
                    P.pe(I("matmul", out=bO.ap[:, :n], lhsT=Vext.v[:, kc, g, :], rhs=pt_.flat[:, :n], start=first, stop=last),
                         reads=[Vext, pt_], writes=[bO])
                    if not last:
                        return
                    P.dve(I("reciprocal", out=rr.flat[64:65, :n], in_=bO.ap[64:65, :n]), reads=[bO], writes=[rr])
                    P.pe(I("matmul", out=bBc.ap[0:64, :n], lhsT=ones32[64:65, 0:64], rhs=rr.flat[64:65, :n], start=True, stop=True),
                         reads=[rr, CONST], writes=[bBc])
                    P.act(I("copy", out=bcs.flat[0:64, :n], in_=bBc.ap[0:64, :n]), reads=[bBc], writes=[bcs])
                    dst = oT_att.sub(qc * T + t0, qc * T + t0 + n)
                    if hh == 0:
                        P.dve(I("tensor_tensor", out=oT_att.v[0:64, qc, t0:t0 + n], in0=bO.ap[0:64, :n], in1=bcs.flat[0:64, :n], op=ALU.mult),
                              reads=[bO, bcs], writes=[dst])
                    else:
                        P.dve(I("tensor_tensor", out=tn.flat[0:64, :n], in0=bO.ap[0:64, :n], in1=bcs.flat[0:64, :n], op=ALU.mult),
                              reads=[bO, bcs], writes=[tn])
                        P.pe(I("matmul", out=bBc.ap[64:128, :n], lhsT=identb[0:64, 0:64], rhs=tn.flat[0:64, :n], start=True, stop=True),
                             reads=[tn, CONST], writes=[bBc])
                        P.act(I("copy", out=oT_att.v[64:128, qc, t0:t0 + n], in_=bBc.ap[64:128, :n]), reads=[bBc], writes=[dst])

                LOOK = 2
                for i in range(len(its) + LOOK):
                    if i < len(its):
                        emit_S(its[i])
                    if i >= LOOK:
                        emit_PV(its[i - LOOK])
            arena.release(m0)

        def phase_out(l, oT_parts, ctx_out):
            m0 = arena.mark()
            wo = arena.alloc((KC, D), BF16)
            load_w(wout_d[l], wo)
            yv = arena.alloc((KC, 512), F32)
            sqs = [arena.alloc(512, BF16) for _ in range(2)]
            rs = arena.alloc(512, F32)
            tmps = [arena.alloc(512, F32) for _ in range(2)]
            bMS = banks[6]
            pieces = []
            for (ob, nch) in oT_parts:
                for i in range(nch):
                    pieces.append((ob, i))
            for (t0, n) in blocks:
                if t0 >= L and not ctx_out:
                    continue
                col = 0 if t0 < L else 1
                for dc in range(KC):
                    bank = banks[dc % 4]
                    P.pe([I("matmul", out=bank.ap[:, :n], lhsT=wo.v[:, k, dc * 128:(dc + 1) * 128], rhs=ob.v[:, i, t0:t0 + n], start=(k == 0), stop=(k == KC - 1))
                          for k, (ob, i) in enumerate(pieces)],
                         reads=[wo] + [ob.sub(i * T + t0, i * T + t0 + n) for (ob, i) in pieces], writes=[bank])
                    P.act(I("copy", out=yv.v[:, dc, :n], in_=bank.ap[:, :n]), reads=[bank], writes=[yv.sub(dc * 512, dc * 512 + 512)])
                ms_block(lambda k: yv.v[:, k, :n], n, bMS, sqs, [yv])
                rstd_from(bMS, n, rs)
                for k in range(KC):
                    tmp = tmps[k % 2]
                    P.dve(I("scalar_tensor_tensor", out=tmp.flat[:, :n], in0=yv.v[:, k, :n], scalar=ab_t[:, 2, k, col:col + 1],
                            in1=rs.flat[:, :n], op0=ALU.mult, op1=ALU.mult),
                          reads=[yv.sub(k * 512, k * 512 + 512), rs, "ab"], writes=[tmp])
                    P.pool(I("tensor_tensor", out=xT[:, k, t0:t0 + n], in0=xT[:, k, t0:t0 + n], in1=tmp.flat[:, :n], op=ALU.add),
                           reads=[tmp] + xkk(t0, n), writes=xkk(t0, n))
            arena.release(m0)

        def phase_ffn(l, ctx_out):
            m0 = arena.mark()
            Tf = T if ctx_out else L
            groups = []
            t = 0
            while t < Tf:
                gn = min(768, Tf - t)
                groups.append((t, gn))
                t += gn
            yacc = arena.alloc((KC, 768), F32)
            w1 = [arena.alloc((KC, 512), BF16) for _ in range(2)]
            w2 = [arena.alloc((4, D), BF16) for _ in range(2)]
            uT = [arena.alloc((4, 512), BF16) for _ in range(2)]
            rl = [arena.alloc(512, BF16) for _ in range(2)]
            sqs = [arena.alloc(512, BF16) for _ in range(2)]
            rs = arena.alloc(512, F32)
            tmps = [arena.alloc(512, F32) for _ in range(2)]
            bMS = banks[7]
            ui = 0
            wi = 0
            for (g0, gn) in groups:
                subs = []
                t = g0
                while t < g0 + gn:
                    n = min(512, g0 + gn - t)
                    subs.append((t, n))
                    t += n
                its = []
                for fg in range(8):
                    for (t0, n) in subs:
                        its.append(dict(fg=fg, t0=t0, n=n, first_of_fg=((t0, n) == subs[0])))

                def u_phase(it):
                    nonlocal ui, wi
                    fg, t0, n = it["fg"], it["t0"], it["n"]
                    if it["first_of_fg"]:
                        it["wa"], it["wb"] = w1[wi % 2], w2[wi % 2]
                        wi += 1
                        load_w(w1_d[l][:, fg * 512:(fg + 1) * 512], it["wa"])
                        load_w(w2_d[l][fg * 512:(fg + 1) * 512, :], it["wb"])
                        cur["wa"], cur["wb"] = it["wa"], it["wb"]
                    else:
                        it["wa"], it["wb"] = cur["wa"], cur["wb"]
                    u = uT[ui % 2]
                    ui += 1
                    it["u"] = u
                    for fc in range(4):
                        bank = banks[fc % 2]
                        proj_fm(bank, it["wa"], fc * 128, 128, t0, n)
                        uk = u.sub(fc * 512, fc * 512 + 512)
                        r_ = rl[fc % 2]
                        P.act(I("activation", out=r_.flat[:, :n], in_=bank.ap[:, :n], func=AF.Relu), reads=[bank], writes=[r_])
                        if fc % 2 == 0:
                            P.dve(I("tensor_tensor", out=u.v[:, fc, :n], in0=r_.flat[:, :n], in1=r_.flat[:, :n], op=ALU.mult), reads=[r_], writes=[uk])
                        else:
                            P.act(I("activation", out=u.v[:, fc, :n], in_=r_.flat[:, :n], func=AF.Square), reads=[r_], writes=[uk])

                def y_phase(it):
                    fg, t0, n, u, wb = it["fg"], it["t0"], it["n"], it["u"], it["wb"]
                    for dc in range(KC):
                        bank = banks[2 + dc % 4]
                        P.pe([I("matmul", out=bank.ap[:, :n], lhsT=wb.v[:, fc, dc * 128:(dc + 1) * 128], rhs=u.v[:, fc, :n], start=(fc == 0), stop=(fc == 3))
                              for fc in range(4)], reads=[wb, u], writes=[bank])
                        ya = yacc.v[:, dc, t0 - g0:t0 - g0 + n]
                        yk = yacc.sub(dc * 768 + t0 - g0, dc * 768 + t0 - g0 + n)
                        if fg == 0:
                            P.act(I("copy", out=ya, in_=bank.ap[:, :n]), reads=[bank], writes=[yk])
                        else:
                            P.dve(I("tensor_tensor", out=ya, in0=ya, in1=bank.ap[:, :n], op=ALU.add), reads=[bank, yk], writes=[yk])

                cur = {}
                for i in range(len(its) + 1):
                    if i < len(its):
                        u_phase(its[i])
                    if i >= 1:
                        y_phase(its[i - 1])
                for (t0, n) in subs:
                    col = 0 if t0 < L else 1
                    o = t0 - g0
                    ms_block(lambda k: yacc.v[:, k, o:o + n], n, bMS, sqs, [yacc])
                    rstd_from(bMS, n, rs)
                    for k in range(KC):
                        tmp = tmps[k % 2]
                        P.dve(I("scalar_tensor_tensor", out=tmp.flat[:, :n], in0=yacc.v[:, k, o:o + n], scalar=ab_t[:, 5, k, col:col + 1],
                                in1=rs.flat[:, :n], op0=ALU.mult, op1=ALU.mult),
                              reads=[yacc, rs, "ab"], writes=[tmp])
                        P.pool(I("tensor_tensor", out=xT[:, k, t0:t0 + n], in0=xT[:, k, t0:t0 + n], in1=tmp.flat[:, :n], op=ALU.add),
                               reads=[tmp] + xkk(t0, n), writes=xkk(t0, n))
            arena.release(m0)

        def phase_store():
            m = arena.mark()
            stg = [arena.alloc(D, F32) for _ in range(2)]
            outs = []
            for tt in range(NTL):
                s = stg[tt % 2]
                for half in range(2):
                    bank = banks[(tt % 2) * 2 + half]
                    P.pe([I("transpose", out=bank.ap[:, kk * 128:(kk + 1) * 128], in_=xT[:, half * 4 + kk, tt * 128:(tt + 1) * 128], identity=ident32)
                          for kk in range(4)], reads=xkk(tt * 128, 128) + [CONST], writes=[bank])
                    if half == 0:
                        P.act(I("copy", out=s.flat[:, 0:512], in_=bank.ap[:, :]), reads=[bank], writes=[s.sub(0, 512)])
                    else:
                        P.dve(I("tensor_copy", out=s.flat[:, 512:1024], in_=bank.ap[:, :]), reads=[bank], writes=[s.sub(512, 1024)])
                P.dma("sp", I("dma_start", out=out_d[tt * 128:(tt + 1) * 128, :], in_=s.flat), reads=[s], writes=[("out", tt)])
                outs.append(("out", tt))
            P.op("sp", [], reads=outs)
            arena.release(m)

        def assemble():
            phase_load()
            if stop_after == "load":
                dump(xT[:, 0, :], 0, T, xkk(0, T))
                return
            for l in range(depth):
                ctx_out = l < depth - 1
                phase_mod(l)
                if stop_after == "mod":
                    dump(ab_t[:].rearrange("p a k c -> p (a k c)"), 0, 96, ["ab"])
                    return
                phase_norm_h(0, 1)
                if stop_after == "norm1":
                    dump(hT[:, 0, :], 0, T, hk(0, T))
                    return
                base = arena.mark()
                oT_ssd = arena.alloc((3, T), BF16)
                oT_ret = arena.alloc((2, T), BF16)
                oT_att = arena.alloc((3, T), BF16)
                arena.off = oT_ssd.off + oT_ssd.n * 2
                phase_ssd(l, oT_ssd, ctx_out)
                if stop_after == "ssd":
                    dump(oT_ssd.flat, 0, 3 * T, [oT_ssd])
                    return
                arena.off = oT_ret.off + oT_ret.n * 2
                phase_ret(l, oT_ret, ctx_out)
                if stop_after == "ret":
                    dump(oT_ret.flat, 0, 2 * T, [oT_ret])
                    return
                arena.off = oT_att.off + oT_att.n * 2
                phase_att(l, oT_att, ctx_out)
                if stop_after == "att":
                    dump(oT_att.flat, 0, 3 * T, [oT_att])
                    return
                phase_out(l, [(oT_att, 3), (oT_ret, 2), (oT_ssd, 3)], ctx_out)
                arena.release(base)
                if stop_after == "out":
                    dump(xT[:, 0, 0:128], 0, 128, xkk(0, 128))
                    return
                phase_norm_h(3, 4)
                phase_ffn(l, ctx_out)
                if stop_after == "layer":
                    dump(xT[:, 0, 0:128], 0, 128, xkk(0, 128))
                    return
            phase_store()

        try:
            assemble()
        except _Stop:
            pass
        P.emit(st)
    build.arena_hi = arena.hi
    build.n_ops = len(P.ops)
    build.stats = P.stats
    return nc


def make_in_maps(inp, L, LC, depth, nb):
    con = host_consts()
    rope = host_rope(L, LC)
    par = host_params(inp, depth)
    f32 = lambda a: np.ascontiguousarray(np.asarray(a, dtype=np.float32))
    shared = {
        "par": par, "con": con, "rope": rope,
        "w_mod": f32(inp["w_mod"]), "w_in": f32(inp["w_in"]), "w_out": f32(inp["w_out"]),
        "w_ff1": f32(inp["w_ff1"]), "w_ff2": f32(inp["w_ff2"]),
    }
    maps = []
    for b in range(nb):
        cv = np.zeros((128, KC, 2), np.float32)
        cv[:, :, 0] = np.asarray(inp["c"][b]).reshape(KC, 128).T
        cv[:, :, 1] = np.asarray(inp["c_ctx"]).reshape(KC, 128).T
        m = dict(shared)
        m["x"] = f32(inp["x"][b])
        m["ctx"] = f32(inp["ctx"][b])
        m["cv"] = cv.reshape(128, 16)
        maps.append(m)
    return maps


def kernel(**inputs):
    inp = {k: np.asarray(v) for k, v in inputs.items()}
    B, L, _ = inp["x"].shape
    LC = inp["ctx"].shape[1]
    depth = inp["w_mod"].shape[0]
    nc = build(L, LC, depth)
    maps = make_in_maps(inp, L, LC, depth, B)
    res = run_bass_kernel_spmd(nc, maps, core_ids=list(range(B)))
    return np.stack([np.asarray(r["out"], dtype=np.float32) for r in res.results], axis=0)
```

```python
import numpy as np
from contextlib import ExitStack
import concourse.bass as bass
import concourse.mybir as mybir
from concourse.bass_utils import run_bass_kernel_spmd

F32 = mybir.dt.float32
BF16 = mybir.dt.bfloat16
AF = mybir.ActivationFunctionType
ALU = mybir.AluOpType

D = 1024
KC = 8
D_IN = 2956
D_FF = 4096
EPS = 1e-6
GRID_W = 64
ROPE_THETA = 10000.0

ENGINES = ("pe", "act", "dve", "pool", "sp")
SEM_CHUNK = 1000
DMA_POOL = 8


class Buf:
    def __init__(self, ap, keys):
        self.ap = ap
        self.keys = tuple(keys)


def _keys(items):
    out = []
    for it in items:
        if isinstance(it, Buf):
            out.extend(it.keys)
        elif isinstance(it, (list,)):
            out.extend(_keys(it))
        else:
            out.append(it)
    return out


class Op:
    __slots__ = ("eng", "fn", "dma", "deps", "has_dep", "ms", "dma_idx", "idx")


class Prog:
    def __init__(self, nc):
        self.nc = nc
        self.ops = []
        self.last_writer = {}
        self.readers = {}

    def op(self, eng, fn, reads=(), writes=(), dma=False):
        if isinstance(fn, tuple):
            fn = [fn]
        o = Op()
        o.eng, o.fn, o.dma = eng, fn, dma
        o.has_dep, o.ms, o.dma_idx = False, None, None
        o.idx = len(self.ops)
        rk = _keys(reads)
        wk = _keys(writes)
        pr = [k for k in rk if isinstance(k, tuple) and k and k[0] == "ps"]
        if pr:
            rk = [k for k in rk if not (isinstance(k, tuple) and k and k[0] == "ps")]
            wk = wk + pr
        deps = set()
        lw, rd = self.last_writer, self.readers
        for r in rk:
            w = lw.get(r)
            if w is not None:
                deps.add(w)
        for r in wk:
            w = lw.get(r)
            if w is not None:
                deps.add(w)
            x = rd.get(r)
            if x:
                deps.update(x)
        for r in rk:
            rd.setdefault(r, []).append(o.idx)
        for r in wk:
            lw[r] = o.idx
            rd[r] = []
        deps.discard(o.idx)
        if eng == "pe":
            deps = {d for d in deps if self.ops[d].eng != "pe"}
        o.deps = deps
        self.ops.append(o)
        return o

    def pe(self, fn, reads=(), writes=()):
        return self.op("pe", fn, reads, writes)

    def act(self, fn, reads=(), writes=()):
        return self.op("act", fn, reads, writes)

    def dve(self, fn, reads=(), writes=()):
        return self.op("dve", fn, reads, writes)

    def pool(self, fn, reads=(), writes=()):
        return self.op("pool", fn, reads, writes)

    def dma(self, eng, fn, reads=(), writes=()):
        return self.op(eng, fn, reads, writes, dma=True)

    def emit(self, stack):
        nc = self.nc
        ops = self.ops
        for o in ops:
            for d in o.deps:
                ops[d].has_dep = True
        cnt = {e: 0 for e in ENGINES}
        dcnt = {e: 0 for e in ENGINES}
        for o in ops:
            if o.dma:
                o.dma_idx = dcnt[o.eng]
                dcnt[o.eng] += 1
            elif o.has_dep:
                cnt[o.eng] += 1
                o.ms = cnt[o.eng]
        self.stats = dict(ms=dict(cnt), dma=dict(dcnt), n_ops={e: sum(1 for o in ops if o.eng == e) for e in ENGINES})
        esems, dsems = {}, {}
        for e in ENGINES:
            n = (cnt[e] + SEM_CHUNK - 1) // SEM_CHUNK
            esems[e] = [stack.enter_context(nc.semaphore(f"c_{e}_{i}")) for i in range(n)]
            n = min(DMA_POOL, dcnt[e])
            dsems[e] = [stack.enter_context(nc.semaphore(f"d_{e}_{i}")) for i in range(n)]

        def sem_of(o):
            if o.dma:
                return dsems[o.eng][o.dma_idx % DMA_POOL], 16 * (o.dma_idx // DMA_POOL + 1)
            m = o.ms - 1
            return esems[o.eng][m // SEM_CHUNK], (m % SEM_CHUNK) + 1

        by_eng = {e: [o for o in ops if o.eng == e] for e in ENGINES}
        block = stack.enter_context(nc.Block())

        def run(e, eng):
            waited = {}
            for o in by_eng[e]:
                need = {}
                for d in o.deps:
                    s, v = sem_of(ops[d])
                    k = id(s)
                    if need.get(k, (None, 0))[1] < v:
                        need[k] = (s, v)
                if o.dma and o.dma_idx >= DMA_POOL:
                    s = dsems[e][o.dma_idx % DMA_POOL]
                    v = 16 * (o.dma_idx // DMA_POOL)
                    k = id(s)
                    if need.get(k, (None, 0))[1] < v:
                        need[k] = (s, v)
                for k, (s, v) in need.items():
                    if waited.get(k, 0) < v:
                        eng.wait_ge(s, v)
                        waited[k] = v
                inst = None
                for (mname, kw) in o.fn:
                    inst = getattr(eng, mname)(**kw)
                if inst is None:
                    continue
                if o.dma:
                    inst.then_inc(sem_of(o)[0], 16)
                elif o.ms is not None:
                    inst.then_inc(sem_of(o)[0], 1)

        if by_eng["pe"]:
            block.tensor(lambda eng: run("pe", eng))
        if by_eng["act"]:
            block.scalar(lambda eng: run("act", eng))
        if by_eng["dve"]:
            block.vector(lambda eng: run("dve", eng))
        if by_eng["pool"]:
            block.gpsimd(lambda eng: run("pool", eng))
        if by_eng["sp"]:
            block.sync(lambda eng: run("sp", eng))


SLOT = 512


class ABuf(Buf):
    def __init__(self, flat, off, esz, shape):
        self.flat = flat
        self.off = off
        self.esz = esz
        self.shape = tuple(shape)
        n = int(np.prod(shape))
        self.n = n
        keys = [("A", s) for s in range(off // SLOT, (off + n * esz - 1) // SLOT + 1)]
        if len(shape) == 1:
            v = flat
        elif len(shape) == 2:
            v = flat.rearrange("p (a b) -> p a b", a=shape[0])
        elif len(shape) == 3:
            v = flat.rearrange("p (a b c) -> p a b c", a=shape[0], b=shape[1])
        elif len(shape) == 4:
            v = flat.rearrange("p (a b c d) -> p a b c d", a=shape[0], b=shape[1], c=shape[2])
        else:
            raise ValueError(shape)
        self.v = v
        Buf.__init__(self, v, keys)

    def sub(self, lo, hi):
        b0 = self.off + lo * self.esz
        b1 = self.off + hi * self.esz
        keys = [("A", s) for s in range(b0 // SLOT, (b1 - 1) // SLOT + 1)]
        return Buf(self.flat[:, lo:hi], keys)


class Arena:
    def __init__(self, nc, st, nbytes):
        self.nbytes = nbytes
        self.t = st.enter_context(nc.sbuf_tensor("arena", [128, nbytes // 2], BF16))
        self.off = 0
        self.hi = 0

    def alloc(self, shape, dt):
        if isinstance(shape, int):
            shape = (shape,)
        esz = 4 if dt == F32 else 2
        n = int(np.prod(shape))
        nb = n * esz
        off = (self.off + 63) // 64 * 64
        assert off + nb <= self.nbytes, f"arena overflow: need {off + nb} of {self.nbytes}"
        self.off = off + nb
        self.hi = max(self.hi, self.off)
        flat = self.t[:, off // 2:(off + nb) // 2]
        if dt == F32:
            flat = flat.bitcast(F32)
        return ABuf(flat, off, esz, shape)

    def mark(self):
        return self.off

    def release(self, m):
        self.off = m


C_ID, C_UF, C_UB, C_ONE, C_BLK, C_ROT, C_OND, C_MF, C_MB, C_HM = range(10)
NCON = 10 * 128


def host_consts():
    c = np.zeros((128, NCON), np.float32)
    i = np.arange(128)
    c[:, C_ID * 128:(C_ID + 1) * 128] = np.eye(128)
    c[:, C_UF * 128:(C_UF + 1) * 128] = (i[:, None] <= i[None, :])
    c[:, C_UB * 128:(C_UB + 1) * 128] = (i[:, None] >= i[None, :])
    c[:, C_ONE * 128:(C_ONE + 1) * 128] = 1.0
    blk = (i[:, None] // 64 == i[None, :] // 64).astype(np.float32) / 64.0
    c[:, C_BLK * 128:(C_BLK + 1) * 128] = blk
    rot = np.zeros((128, 128), np.float32)
    for d in range(128):
        if d % 32 < 16:
            rot[d + 16, d] = -1.0
        else:
            rot[d - 16, d] = 1.0
    c[:, C_ROT * 128:(C_ROT + 1) * 128] = rot
    c[:, C_OND * 128:(C_OND + 1) * 128] = 1.0 / 1024.0
    c[:, C_MF * 128:(C_MF + 1) * 128] = (i[None, :] >= i[:, None])
    c[:, C_MB * 128:(C_MB + 1) * 128] = (i[None, :] < i[:, None])
    hm = np.zeros((128, 128), np.float32)
    hm[:64, 0] = 1.0
    hm[64:, 1] = 1.0
    c[:, C_HM * 128:(C_HM + 1) * 128] = hm
    return c


def host_rope(L, LC):
    T = L + LC
    rows = L // GRID_W
    row = np.broadcast_to(np.arange(rows)[:, None], (rows, GRID_W)).reshape(L)
    col = np.broadcast_to(np.arange(GRID_W)[None, :], (rows, GRID_W)).reshape(L)
    half = 32
    inv_freq = (ROPE_THETA ** (-np.arange(0, half, 2, dtype=np.float32) / half)).astype(np.float32)
    ang = np.stack([row, col], axis=-1).astype(np.float32)[:, :, None] * inv_freq
    ang = np.concatenate([ang, ang], axis=-1).reshape(L, 64)
    cos = np.ones((64, T), np.float32)
    sin = np.zeros((64, T), np.float32)
    cos[:, :L] = np.cos(ang).T
    sin[:, :L] = np.sin(ang).T
    tab = np.zeros((128, 2, T), np.float32)
    tab[:64, 0] = cos
    tab[64:, 0] = cos
    tab[:64, 1] = sin
    tab[64:, 1] = sin
    return tab.reshape(128, 2 * T)


PO_BMOD = 0
PO_NG = PO_BMOD + 48
PO_QG = PO_NG + 32
PO_KG = PO_QG + 1
PO_RGG = PO_KG + 1
PO_RGB = PO_RGG + 2
PO_RLOG = PO_RGB + 2
PO_CW = PO_RLOG + 8
PO_CB = PO_CW + 35
PO_DTB = PO_CB + 7
PO_ALOG = PO_DTB + 12
PO_SD = PO_ALOG + 12
PO_SNG = PO_SD + 6
NPAR = PO_SNG + 3


def host_params(inp, depth):
    par = np.zeros((128, depth, NPAR), np.float32)
    fm = lambda v: np.ascontiguousarray(v.reshape(-1, 128).T)
    for l in range(depth):
        p = par[:, l]
        p[:, PO_BMOD:PO_BMOD + 48] = fm(inp["b_mod"][l])
        p[:, PO_NG:PO_NG + 32] = fm(inp["norm_g"][l].reshape(-1))
        p[:, PO_QG] = np.tile(inp["q_norm_g"][l], 2)
        p[:, PO_KG] = np.tile(inp["k_norm_g"][l], 2)
        p[:, PO_RGG:PO_RGG + 2] = fm(inp["ret_gn_g"][l])
        p[:, PO_RGB:PO_RGB + 2] = fm(inp["ret_gn_b"][l])
        p[:, PO_RLOG:PO_RLOG + 8] = np.broadcast_to(inp["ret_decay_logit"][l].reshape(1, 8), (128, 8))
        cw = inp["ssd_conv_w"][l]
        for cc in range(7):
            p[:, PO_CW + cc * 5:PO_CW + cc * 5 + 5] = cw[:, cc * 128:(cc + 1) * 128].T
        p[:, PO_CB:PO_CB + 7] = fm(inp["ssd_conv_b"][l])
        p[:, PO_DTB:PO_DTB + 12] = np.broadcast_to(inp["ssd_dt_bias"][l].reshape(1, 12), (128, 12))
        p[:, PO_ALOG:PO_ALOG + 12] = np.broadcast_to(inp["ssd_a_log"][l].reshape(1, 12), (128, 12))
        p[:, PO_SD:PO_SD + 6] = np.broadcast_to(inp["ssd_d"][l].reshape(1, 6), (128, 6))
        p[:, PO_SNG:PO_SNG + 3] = fm(inp["ssd_norm_g"][l])
    return par.reshape(128, depth * NPAR)


CO_QA, CO_KA, CO_VA = 0, 384, 512
CO_QR, CO_KR, CO_VR, CO_GR = 640, 896, 1152, 1408
CO_Z, CO_XBC, CO_DT = 1664, 2048, 2944


def I(m, **kw):
    return (m, kw)


class _Stop(Exception):
    pass


def build(L, LC, depth, stop_after=None, dbg=None):
    T = L + LC
    NT = T // 128
    NTL = L // 128
    blocks = [(i * 512, 512) for i in range(L // 512)] + [(L, LC)]
    lat_chunks = list(range(NTL))
    ctx_chunks = list(range(NTL, NT))
    fwd_order = ctx_chunks + lat_chunks
    bwd_order = ctx_chunks[::-1] + lat_chunks[::-1]

    nc = bass.Bass("TRN2", target_bir_lowering=False)
    dt_in = lambda name, shape: nc.dram_tensor(name, shape, F32, kind="ExternalInput").ap()
    x_d = dt_in("x", [L, D])
    ctx_d = dt_in("ctx", [LC, D])
    cv_d = dt_in("cv", [128, 16])
    par_d = dt_in("par", [128, depth * NPAR])
    con_d = dt_in("con", [128, NCON])
    rope_d = dt_in("rope", [128, 2 * T])
    wmod_d = dt_in("w_mod", [depth, D, 6 * D])
    win_d = dt_in("w_in", [depth, D, D_IN])
    wout_d = dt_in("w_out", [depth, D, D])
    w1_d = dt_in("w_ff1", [depth, D, D_FF])
    w2_d = dt_in("w_ff2", [depth, D_FF, D])
    out_d = nc.dram_tensor("out", [L, D], F32, kind="ExternalOutput").ap()
    dbg_d = None
    if dbg is not None:
        dbg_d = nc.dram_tensor("dbg", [128, dbg], F32, kind="ExternalOutput").ap()

    P = Prog(nc)
    st = ExitStack()
    with st:
        sbt = lambda name, shape, dt: st.enter_context(nc.sbuf_tensor(name, shape, dt))
        xT = sbt("xT", [128, KC, T], F32)
        hT = sbt("hT", [128, KC, T], BF16)
        c32_t = sbt("c32", [128, 4, 128], F32)
        cb_t = sbt("cb", [128, 6, 128], BF16)
        idb_t = sbt("idb", [128, 128], BF16)
        one_b_t = sbt("oneb", [128, 128], BF16)
        par_t = sbt("par_sb", [128, depth, NPAR], F32)
        ab_t = sbt("ab", [128, 6, KC, 2], F32)
        dI_t = sbt("dI", [128, 6, 128], BF16)
        sml_t = sbt("sml", [128, 64], F32)
        arena = Arena(nc, st, (nc.sbuf_bytes_remaining - 1024) // 64 * 64)
        banks = [Buf(st.enter_context(nc.psum_tensor(f"ps{i}", [128, 512], F32)), [("ps", i)]) for i in range(8)]

        ident32 = c32_t[:, 0, :]
        Uf32 = c32_t[:, 1, :]
        Ub32 = c32_t[:, 2, :]
        ones32 = c32_t[:, 3, :]
        blk64b = cb_t[:, 0, :]
        rotb = cb_t[:, 1, :]
        onesDb = cb_t[:, 2, :]
        mFb = cb_t[:, 3, :]
        mBb = cb_t[:, 4, :]
        hmb = cb_t[:, 5, 0:2]
        identb = idb_t[:, :]
        ones1b = one_b_t[:, :]
        CONST = "const"

        def hk(t0, n):
            return [("hT", t) for t in range(t0 // 128, (t0 + n + 127) // 128)]

        def xkk(t0, n):
            r = []
            for t in range(t0 // 128, (t0 + n + 127) // 128):
                r += [("xT", t, 0), ("xT", t, 1)]
            return r

        def b16(bank):
            return bank.ap[:].bitcast(BF16)

        P.dma("sp", I("dma_start", out=c32_t[:, 0:3, :], in_=con_d[:, 0:384].rearrange("p (a b) -> p a b", a=3)), writes=[CONST])
        P.dma("sp", I("dma_start", out=c32_t[:, 3, :], in_=con_d[:, C_ONE * 128:(C_ONE + 1) * 128]), writes=[CONST])
        P.dma("sp", I("dma_start", out=par_t[:], in_=par_d.rearrange("p (l n) -> p l n", l=depth)), writes=["par"])
        P.dma("pool", I("dma_start", out=cb_t[:], in_=con_d[:, C_BLK * 128:(C_HM + 1) * 128].rearrange("p (a b) -> p a b", a=6)), writes=[CONST])
        P.dma("pool", I("dma_start", out=idb_t[:], in_=con_d[:, C_ID * 128:(C_ID + 1) * 128]), writes=[CONST])
        P.dma("pool", I("dma_start", out=one_b_t[:], in_=con_d[:, C_ONE * 128:(C_ONE + 1) * 128]), writes=[CONST])

        def dump(ap_, col0, ncols, reads):
            if dbg_d is None:
                return
            if ap_.dtype != F32:
                tmp = arena.alloc(ncols, F32)
                P.dve(I("tensor_copy", out=tmp.flat, in_=ap_), reads=reads, writes=[tmp])
                ap_, reads = tmp.flat, [tmp]
            P.dma("sp", I("dma_start", out=dbg_d[:, col0:col0 + ncols], in_=ap_), reads=reads, writes=[("dbg", col0)])
            P.op("sp", [], reads=[("dbg", col0)])

        def phase_load():
            m = arena.mark()
            stg = [arena.alloc(D, F32) for _ in range(2)]
            for tt in range(NT):
                s = stg[tt % 2]
                src = x_d[tt * 128:(tt + 1) * 128, :] if tt < NTL else ctx_d[(tt - NTL) * 128:(tt - NTL + 1) * 128, :]
                P.dma("sp", I("dma_start", out=s.flat, in_=src), writes=[s])
                for half in range(2):
                    bank = banks[(tt % 2) * 2 + half]
                    P.pe([I("transpose", out=bank.ap[:, kk * 128:(kk + 1) * 128], in_=s.flat[:, (half * 4 + kk) * 128:(half * 4 + kk + 1) * 128], identity=ident32)
                          for kk in range(4)], reads=[s, CONST], writes=[bank])
                    dst = xT[:, half * 4:(half + 1) * 4, tt * 128:(tt + 1) * 128]
                    srcp = bank.ap[:].rearrange("p (k n) -> p k n", k=4)
                    if half == 0:
                        P.act(I("copy", out=dst, in_=srcp), reads=[bank], writes=[("xT", tt, half)])
                    else:
                        P.dve(I("tensor_copy", out=dst, in_=srcp), reads=[bank], writes=[("xT", tt, half)])
            arena.release(m)

        def phase_mod(l):
            m = arena.mark()
            cv32 = arena.alloc((KC, 2), F32)
            cvb = arena.alloc((KC, 2), BF16)
            P.dma("sp", I("dma_start", out=cv32.flat, in_=cv_d), writes=[cv32])
            P.act(I("activation", out=cvb.flat, in_=cv32.flat, func=AF.Silu), reads=[cv32], writes=[cvb])
            wm = [arena.alloc((KC, 768), BF16) for _ in range(2)]
            bank = banks[7]
            for piece in range(8):
                w = wm[piece % 2]
                P.dma("pool", I("dma_start", out=w.v, in_=wmod_d[l][:, piece * 768:(piece + 1) * 768].rearrange("(k p) n -> p k n", p=128)), writes=[w])
                ins = []
                for jj in range(6):
                    j = piece * 6 + jj
                    for k in range(KC):
                        ins.append(I("matmul", out=bank.ap[:, 2 * j:2 * j + 2], lhsT=w.v[:, k, jj * 128:(jj + 1) * 128], rhs=cvb.v[:, k, :],
                                     start=(k == 0), stop=(k == KC - 1)))
                P.pe(ins, reads=[w, cvb], writes=[bank])
            modT = arena.alloc((48, 2), F32)
            par = par_t[:, l, :]
            P.dve(I("tensor_tensor", out=modT.v, in0=bank.ap[:, 0:96].rearrange("p (j c) -> p j c", c=2),
                    in1=par[:, PO_BMOD:PO_BMOD + 48].unsqueeze(2).to_broadcast([128, 48, 2]), op=ALU.add),
                  reads=[bank, "par"], writes=[modT])
            ng = lambda f: par[:, PO_NG + f * 8:PO_NG + f * 8 + 8].unsqueeze(2).to_broadcast([128, KC, 2])
            mv = lambda i: modT.v[:, i * 8:(i + 1) * 8, :]
            P.dve([I("scalar_tensor_tensor", out=ab_t[:, 0], in0=mv(1), scalar=1.0, in1=ng(0), op0=ALU.add, op1=ALU.mult),
                   I("tensor_copy", out=ab_t[:, 1], in_=mv(0)),
                   I("tensor_tensor", out=ab_t[:, 2], in0=mv(2), in1=ng(1), op=ALU.mult),
                   I("scalar_tensor_tensor", out=ab_t[:, 3], in0=mv(4), scalar=1.0, in1=ng(2), op0=ALU.add, op1=ALU.mult),
                   I("tensor_copy", out=ab_t[:, 4], in_=mv(3)),
                   I("tensor_tensor", out=ab_t[:, 5], in0=mv(5), in1=ng(3), op=ALU.mult)],
                  reads=[modT, "par"], writes=["ab"])
            arena.release(m)

        def ms_block(src_k, n, bank, sqs, reads):
            for k in range(KC):
                sq = sqs[k % 2]
                P.act(I("activation", out=sq.flat[:, :n], in_=src_k(k), func=AF.Square), reads=reads, writes=[sq])
                P.pe(I("matmul", out=bank.ap[:, :n], lhsT=onesDb, rhs=sq.flat[:, :n], start=(k == 0), stop=(k == KC - 1)),
                     reads=[sq, CONST], writes=[bank])

        def rstd_from(bank, n, rs, scale=None):
            P.act(I("activation", out=rs.flat[:, :n], in_=bank.ap[:, :n], func=AF.Ln, bias=EPS, scale=(1.0 if scale is None else scale)),
                  reads=[bank], writes=[rs])
            P.act(I("activation", out=rs.flat[:, :n], in_=rs.flat[:, :n], func=AF.Exp, scale=-0.5), reads=[rs], writes=[rs])

        def sigmoid_from(bank, n, dst):
            P.act(I("activation", out=dst.flat[:, :n], in_=bank.ap[:, :n], func=AF.Exp, scale=-1.0), reads=[bank], writes=[dst])
            P.act(I("activation", out=dst.flat[:, :n], in_=dst.flat[:, :n], func=AF.Ln, bias=1.0, scale=1.0), reads=[dst], writes=[dst])
            P.act(I("activation", out=dst.flat[:, :n], in_=dst.flat[:, :n], func=AF.Exp, scale=-1.0), reads=[dst], writes=[dst])

        def phase_norm_h(ai, bi_):
            m = arena.mark()
            sqs = [arena.alloc(512, BF16) for _ in range(2)]
            rs = arena.alloc(512, F32)
            tmps = [arena.alloc(512, F32) for _ in range(2)]
            bank = banks[6]
            for (t0, n) in blocks:
                col = 0 if t0 < L else 1
                ms_block(lambda k: xT[:, k, t0:t0 + n], n, bank, sqs, xkk(t0, n))
                rstd_from(bank, n, rs)
                for k in range(KC):
                    tmp = tmps[k % 2]
                    P.dve(I("scalar_tensor_tensor", out=tmp.flat[:, :n], in0=xT[:, k, t0:t0 + n], scalar=ab_t[:, ai, k, col:col + 1],
                            in1=rs.flat[:, :n], op0=ALU.mult, op1=ALU.mult),
                          reads=xkk(t0, n) + [rs, "ab"], writes=[tmp])
                    P.act(I("activation", out=hT[:, k, t0:t0 + n], in_=tmp.flat[:, :n], func=AF.Identity,
                            bias=ab_t[:, bi_, k, col:col + 1], scale=1.0),
                          reads=[tmp, "ab"], writes=hk(t0, n))
            arena.release(m)

        def load_w(dram_ap, buf):
            P.dma("pool", I("dma_start", out=buf.v, in_=dram_ap.rearrange("(k p) n -> p k n", p=128)), writes=[buf])

        def proj_fm(bank, w, c0, M, t0, n):
            P.pe([I("matmul", out=bank.ap[0:M, :n], lhsT=w.v[:, k, c0:c0 + M], rhs=hT[:, k, t0:t0 + n], start=(k == 0), stop=(k == KC - 1))
                  for k in range(KC)], reads=[w] + hk(t0, n), writes=[bank])

        def scan(H, hg, la_of, dt_of, kt_chunk, n_kt, kidx, qg_of, vtok_of, extra_terms, extra_reads, finish, dbuf, shared=None, skip_out=(), pre=None):
            NG = H // hg
            HP = H * 64
            units = [(0, min(H, 4))] + ([(4, H - 4)] if H > 4 else [])
            NU = len(units)
            NPAR = 2 if dbuf else 1
            bS, bG, bT, bY = banks[0], banks[3], banks[4], banks[5]
            pbanks = [banks[1], banks[2], banks[6], banks[7]]
            bZ, bF = bG, bS
            m = arena.mark()
            sb_store = arena.alloc((NT, HP), BF16)
            Sst = [arena.alloc(HP, F32) for _ in range(2)]
            sfb = arena.alloc(HP, BF16)
            mk2 = lambda shape, dt: [[arena.alloc(shape, dt) for _ in range(2)] for _ in range(NPAR)]
            pt = mk2(2 * H, F32)
            dif = mk2(H, F32)
            wv_ = mk2(H, F32)
            dec = mk2(H, F32)
            cw = mk2(H, F32)
            Vdt = mk2(HP, BF16) if dt_of(0) is not None else None
            E = mk2((H, 128), BF16)
            Ebc = mk2((H, 128), BF16)
            Gm = mk2((NG, 128), BF16)
            MT, QsT = E, Ebc
            Vw = shared if shared is not None else arena.alloc(HP, BF16)
            ktok = arena.alloc((n_kt, 128), BF16)
            R = [[arena.alloc((hn, 128), F32) for (_, hn) in units] for _ in range(2)]
            Ud = [Uf32, Ub32]
            md = [mFb, mBb]
            hq = lambda ap_: ap_.rearrange("p (h q) -> p h q", h=H)
            g4 = lambda ap_: ap_.rearrange("p (g a) i -> p g a i", g=NG)

            def small(c, d, p):
                lap, lar = la_of(c)
                P.pe([I("matmul", out=bS.ap[:, 0:H], lhsT=Ud[d], rhs=lap[:, d, :], start=True, stop=True),
                      I("matmul", out=bS.ap[:, H:2 * H], lhsT=ones32, rhs=lap[:, d, :], start=True, stop=True)],
                     reads=[CONST] + lar, writes=[bS])
                P.act(I("copy", out=pt[p][d].flat, in_=bS.ap[:, 0:2 * H]), reads=[bS], writes=[pt[p][d]])
                P.dve(I("tensor_tensor", out=dif[p][d].flat, in0=pt[p][d].flat[:, H:2 * H], in1=pt[p][d].flat[:, 0:H], op=ALU.subtract),
                      reads=[pt[p][d]], writes=[dif[p][d]])
                P.act(I("activation", out=wv_[p][d].flat, in_=dif[p][d].flat, func=AF.Exp), reads=[dif[p][d]], writes=[wv_[p][d]])
                P.act(I("activation", out=dec[p][d].flat, in_=pt[p][d].flat[:, H:2 * H], func=AF.Exp), reads=[pt[p][d]], writes=[dec[p][d]])
                dtv = dt_of(c)
                if dtv is not None:
                    P.dve(I("tensor_tensor", out=cw[p][d].flat, in0=wv_[p][d].flat, in1=dtv[0][:, d, :], op=ALU.mult),
                          reads=[wv_[p][d]] + dtv[1], writes=[cw[p][d]])
                else:
                    P.dve(I("tensor_copy", out=cw[p][d].flat, in_=wv_[p][d].flat), reads=[wv_[p][d]], writes=[cw[p][d]])

            def dstate(c, d, S, p):
                vt, vr = vtok_of(c)
                P.dve(I("tensor_tensor", out=hq(Vw.flat), in0=hq(vt), in1=cw[p][d].flat.unsqueeze(2).to_broadcast([128, H, 64]), op=ALU.mult),
                      reads=vr + [cw[p][d]], writes=[Vw])
                bt16 = b16(bT)
                P.pe([I("transpose", out=bt16[:, i * 128:(i + 1) * 128], in_=kt_chunk(i, c)[0], identity=identb) for i in range(n_kt)],
                     reads=[CONST] + kt_chunk(0, c)[1], writes=[bT])
                P.act(I("copy", out=ktok.flat, in_=bt16[:, 0:n_kt * 128]), reads=[bT], writes=[ktok])
                P.pe([I("matmul", out=bT.ap[:, g * hg * 64:(g + 1) * hg * 64], lhsT=ktok.v[:, kidx(g), :], rhs=Vw.flat[:, g * hg * 64:(g + 1) * hg * 64],
                        start=True, stop=True) for g in range(NG)], reads=[ktok, Vw], writes=[bT])
                P.dve(I("tensor_tensor", out=hq(S.flat), in0=hq(S.flat), in1=dec[p][d].flat.unsqueeze(2).to_broadcast([128, H, 64]), op=ALU.mult),
                      reads=[S, dec[p][d]], writes=[S])
                P.dve(I("tensor_tensor", out=S.flat, in0=S.flat, in1=bT.ap[:, 0:HP], op=ALU.add), reads=[S, bT], writes=[S])

            P.pool(I("memset", ap=Sst[1].flat, constant=0.0), writes=[Sst[1]])
            P.pool(I("memset", ap=Sst[0].flat, constant=0.0), writes=[Sst[0]])
            for c in bwd_order:
                P.act(I("copy", out=sb_store.v[:, c, :], in_=Sst[1].flat), reads=[Sst[1]], writes=[sb_store.sub(c * HP, (c + 1) * HP)])
                small(c, 1, 0)
                dstate(c, 1, Sst[1], 0)

            loc = {}

            def local_part(c, p):
                qg, qgr = qg_of(c, p)
                lap, lar = la_of(c)
                loc[c] = (qg, qgr)
                P.pe([I("matmul", out=bG.ap[:, g * 128:(g + 1) * 128], lhsT=kt_chunk(kidx(g), c)[0], rhs=qg[:, g, :], start=True, stop=True)
                      for g in range(NG)], reads=kt_chunk(0, c)[1] + qgr, writes=[bG])
                for d in range(2):
                    P.dve(I("tensor_tensor", out=Gm[p][d].v, in0=bG.ap[:, 0:NG * 128].rearrange("p (g i) -> p g i", g=NG),
                            in1=md[d].unsqueeze(1).to_broadcast([128, NG, 128]), op=ALU.mult),
                          reads=[bG, CONST], writes=[Gm[p][d]])
                for d in range(2):
                    small(c, d, p)
                chains = [(d, ui, h0, hn) for d in range(2) for ui, (h0, hn) in enumerate(units)]
                bk = lambda d, ui: pbanks[(d * NU + ui) % 4]
                for (d, ui, h0, hn) in chains:
                    P.dve(I("tensor_tensor", out=R[d][ui].v, in0=Ud[d].unsqueeze(1).to_broadcast([128, hn, 128]),
                            in1=lap[:, d, h0:h0 + hn].unsqueeze(2).to_broadcast([128, hn, 128]), op=ALU.mult),
                          reads=[CONST] + lar, writes=[R[d][ui]])
                for (d, ui, h0, hn) in chains:
                    P.pe(I("matmul", out=bk(d, ui).ap[:, 0:hn * 128], lhsT=ones32, rhs=R[d][ui].flat, start=True, stop=True),
                         reads=[CONST, R[d][ui]], writes=[bk(d, ui)])
                for (d, ui, h0, hn) in chains:
                    P.act(I("activation", out=Ebc[p][d].flat[:, h0 * 128:(h0 + hn) * 128], in_=bk(d, ui).ap[:, 0:hn * 128], func=AF.Exp),
                          reads=[bk(d, ui)], writes=[Ebc[p][d].sub(h0 * 128, (h0 + hn) * 128)])
                    P.dve(I("tensor_tensor", out=R[d][ui].v, in0=bk(d, ui).ap[:, 0:hn * 128].rearrange("p (h i) -> p h i", h=hn),
                            in1=pt[p][d].flat[:, h0:h0 + hn].unsqueeze(2).to_broadcast([128, hn, 128]), op=ALU.subtract),
                          reads=[bk(d, ui), pt[p][d]], writes=[R[d][ui]])
                for (d, ui, h0, hn) in chains:
                    P.dve(I("tensor_tensor", out=R[d][ui].v, in0=R[d][ui].v, in1=md[d].unsqueeze(1).to_broadcast([128, hn, 128]), op=ALU.mult),
                          reads=[R[d][ui], CONST], writes=[R[d][ui]])
                for (d, ui, h0, hn) in chains:
                    P.act(I("activation", out=E[p][d].flat[:, h0 * 128:(h0 + hn) * 128], in_=R[d][ui].flat, func=AF.Exp),
                          reads=[R[d][ui]], writes=[E[p][d].sub(h0 * 128, (h0 + hn) * 128)])
                vt, vr = vtok_of(c)
                dtv = dt_of(c)
                for d in range(2):
                    P.dve(I("tensor_tensor", out=g4(QsT[p][d].v), in0=g4(Ebc[p][d].v), in1=qg.unsqueeze(2).to_broadcast([128, NG, hg, 128]), op=ALU.mult),
                          reads=[Ebc[p][d]] + qgr, writes=[QsT[p][d]])
                    if dtv is not None:
                        P.pool(I("tensor_tensor", out=hq(Vdt[p][d].flat), in0=hq(vt), in1=dtv[0][:, d, :].unsqueeze(2).to_broadcast([128, H, 64]), op=ALU.mult),
                               reads=vr + dtv[1], writes=[Vdt[p][d]])
                for d in range(2):
                    P.dve(I("tensor_tensor", out=g4(MT[p][d].v), in0=g4(E[p][d].v), in1=Gm[p][d].v.unsqueeze(2).to_broadcast([128, NG, hg, 128]), op=ALU.mult),
                          reads=[E[p][d], Gm[p][d]], writes=[MT[p][d]])
                if pre is not None:
                    pre(c, p, bG)

            def state_part(c, p):
                vt, vr = vtok_of(c)
                dtv = dt_of(c)
                ins = []
                for h in range(H):
                    out = bY.ap[(h % 2) * 64:(h % 2) * 64 + 64, (h // 2) * 128:(h // 2 + 1) * 128]
                    terms = []
                    for d in range(2):
                        lhs = Vdt[p][d].flat[:, h * 64:(h + 1) * 64] if dtv is not None else vt[:, h * 64:(h + 1) * 64]
                        terms.append((lhs, MT[p][d].v[:, h, :]))
                    terms.append((sfb.flat[:, h * 64:(h + 1) * 64], QsT[p][0].v[:, h, :]))
                    terms.append((sb_store.v[:, c, h * 64:(h + 1) * 64], QsT[p][1].v[:, h, :]))
                    terms += extra_terms(c, h)
                    for i, (lh, rh) in enumerate(terms):
                        ins.append(I("matmul", out=out, lhsT=lh, rhs=rh, start=(i == 0), stop=(i == len(terms) - 1)))
                P.pe(ins, reads=([Vdt[p][0], Vdt[p][1]] if dtv is not None else []) + [MT[p][0], MT[p][1], QsT[p][0], QsT[p][1], sfb, sb_store.sub(c * HP, (c + 1) * HP)] + vr + extra_reads,
                     writes=[bY])
                dstate(c, 0, Sst[0], p)
                P.act(I("copy", out=sfb.flat, in_=Sst[0].flat), reads=[Sst[0]], writes=[sfb])
                finish(c, bY, bF, p)

            order = []
            for c in fwd_order:
                if c in skip_out:
                    small(c, 0, 0)
                    dstate(c, 0, Sst[0], 0)
                else:
                    order.append(c)
            P.act(I("copy", out=sfb.flat, in_=Sst[0].flat), reads=[Sst[0]], writes=[sfb])
            if dbuf:
                local_part(order[0], 0)
                for i, c in enumerate(order):
                    if i + 1 < len(order):
                        local_part(order[i + 1], (i + 1) % 2)
                    state_part(c, i % 2)
            else:
                for c in order:
                    local_part(c, 0)
                    state_part(c, 0)
            arena.release(m)

        def phase_ssd(l, oT_ssd, ctx_out):
            par = par_t[:, l, :]
            m0 = arena.mark()
            BT = arena.alloc((2, T), BF16)
            CT = arena.alloc((2, T), BF16)
            Xtok = arena.alloc((NT, 384), BF16)
            dtv = arena.alloc((NT, 2, 6), F32)
            lav = arena.alloc((NT, 2, 6), F32)
            wz = arena.alloc((KC, 384), BF16)
            load_w(win_d[l][:, CO_Z:CO_Z + 384], wz)
            m1 = arena.mark()
            wdt = arena.alloc((KC, 12), BF16)
            load_w(win_d[l][:, CO_DT:CO_DT + 12], wdt)
            bank = banks[0]
            P.pe([I("matmul", out=bank.ap[:, tt * 12:(tt + 1) * 12], lhsT=hT[:, k, tt * 128:(tt + 1) * 128], rhs=wdt.v[:, k, :],
                    start=(k == 0), stop=(k == KC - 1)) for tt in range(NT) for k in range(KC)],
                 reads=[wdt] + hk(0, T), writes=[bank])
            d3 = lambda b: b.flat.rearrange("p (t c) -> p t c", c=12)
            P.dve(I("tensor_tensor", out=d3(dtv), in0=bank.ap[:, 0:NT * 12].rearrange("p (t c) -> p t c", c=12),
                    in1=par[:, PO_DTB:PO_DTB + 12].unsqueeze(1).to_broadcast([128, NT, 12]), op=ALU.add),
                  reads=[bank, "par"], writes=[dtv])
            aexp = sml_t[:, 0:12]
            P.act(I("activation", out=dtv.flat, in_=dtv.flat, func=AF.Exp), reads=[dtv], writes=[dtv])
            P.act(I("activation", out=dtv.flat, in_=dtv.flat, func=AF.Ln, bias=1.0, scale=1.0), reads=[dtv], writes=[dtv])
            P.act(I("activation", out=aexp, in_=par[:, PO_ALOG:PO_ALOG + 12], func=AF.Exp), reads=["par"], writes=["aexp"])
            P.dve(I("scalar_tensor_tensor", out=d3(lav), in0=d3(dtv), scalar=-1.0, in1=aexp.unsqueeze(1).to_broadcast([128, NT, 12]),
                    op0=ALU.mult, op1=ALU.mult), reads=[dtv, "aexp"], writes=[lav])
            if stop_after == "ssd_a":
                dump(lav.flat, 0, NT * 12, [lav])
                raise _Stop()
            P.dve([I("tensor_scalar", out=dI_t[:, h, :], in0=identb, scalar1=par[:, PO_SD + h:PO_SD + h + 1], scalar2=None, op0=ALU.mult) for h in range(6)],
                  reads=[CONST, "par"], writes=["dI"])
            rawp = arena.alloc(T + 8, F32)
            acc = arena.alloc(T, F32)
            xs = [arena.alloc(T, BF16) for _ in range(2)]
            wx = [arena.alloc((KC, 128), BF16) for _ in range(2)]
            P.pool(I("memset", ap=rawp.flat, constant=0.0), writes=[rawp])
            roff = lambda t0: (2 + t0) if t0 < L else (t0 + 6)
            for cc in range(7):
                w = wx[cc % 2]
                load_w(win_d[l][:, CO_XBC + cc * 128:CO_XBC + (cc + 1) * 128], w)
                for bi, (t0, n) in enumerate(blocks):
                    bank = banks[1 + bi % 2]
                    proj_fm(bank, w, 0, 128, t0, n)
                    P.act(I("copy", out=rawp.flat[:, roff(t0):roff(t0) + n], in_=bank.ap[:, :n]), reads=[bank], writes=[rawp])
                for (s0, sn, ro) in [(0, L, 2), (L, LC, L + 6)]:
                    sa = acc.sub(s0, s0 + sn)
                    P.dve(I("tensor_scalar", out=sa.ap, in0=rawp.flat[:, ro - 2:ro - 2 + sn], scalar1=par[:, PO_CW + cc * 5:PO_CW + cc * 5 + 1],
                            scalar2=par[:, PO_CB + cc:PO_CB + cc + 1], op0=ALU.mult, op1=ALU.add), reads=[rawp, "par"], writes=[sa])
                    for j in range(1, 5):
                        P.dve(I("scalar_tensor_tensor", out=sa.ap, in0=rawp.flat[:, ro - 2 + j:ro - 2 + j + sn],
                                scalar=par[:, PO_CW + cc * 5 + j:PO_CW + cc * 5 + j + 1], in1=sa.ap, op0=ALU.mult, op1=ALU.add),
                              reads=[rawp, "par", sa], writes=[sa])
                if cc < 3:
                    dst = xs[cc % 2]
                    dflat_ = dst.flat
                elif cc < 5:
                    dst = BT.sub((cc - 3) * T, (cc - 2) * T)
                    dflat_ = dst.ap
                else:
                    dst = CT.sub((cc - 5) * T, (cc - 4) * T)
                    dflat_ = dst.ap
                P.act(I("activation", out=dflat_, in_=acc.flat, func=AF.Silu), reads=[acc], writes=[dst])
                if cc < 3:
                    for t8 in range(0, NT, 8):
                        nt8 = min(8, NT - t8)
                        bank = banks[3 + (t8 // 8) % 2]
                        P.pe([I("transpose", out=b16(bank)[:, i * 128:(i + 1) * 128], in_=dflat_[:, (t8 + i) * 128:(t8 + i + 1) * 128], identity=identb)
                              for i in range(nt8)], reads=[dst, CONST], writes=[bank])
                        P.dve(I("tensor_copy", out=Xtok.v[:, t8:t8 + nt8, cc * 128:(cc + 1) * 128],
                                in_=b16(bank)[:, 0:nt8 * 128].rearrange("p (t f) -> p t f", f=128)),
                              reads=[bank], writes=[Xtok])
            if stop_after == "ssd_b":
                dump(Xtok.flat[:, 0:768], 0, 768, [Xtok])
                raise _Stop()
            arena.release(m1)
            sz = arena.alloc(384, F32)
            vv = sz
            sq = arena.alloc(384, BF16)
            rs = arena.alloc(128, F32)

            def pre(c, p, bZ):
                P.pe([I("matmul", out=bZ.ap[:, pc * 128:(pc + 1) * 128], lhsT=wz.v[:, k, pc * 128:(pc + 1) * 128], rhs=hT[:, k, c * 128:(c + 1) * 128],
                        start=(k == 0), stop=(k == KC - 1)) for pc in range(3) for k in range(KC)],
                     reads=[wz] + hk(c * 128, 128), writes=[bZ])
                sigmoid_from(bZ, 384, sz)
                P.dve(I("tensor_tensor", out=sz.flat, in0=bZ.ap[:, 0:384], in1=sz.flat, op=ALU.mult), reads=[bZ, sz], writes=[sz])

            def finish(c, bY, bF, p):
                P.dve(I("tensor_tensor", out=vv.flat, in0=bY.ap[:, 0:384], in1=sz.flat, op=ALU.mult), reads=[bY, sz], writes=[vv])
                P.act(I("activation", out=sq.flat, in_=vv.flat, func=AF.Square), reads=[vv], writes=[sq])
                P.pe([I("matmul", out=bF.ap[:, 0:128], lhsT=ones1b, rhs=sq.flat[:, pc * 128:(pc + 1) * 128], start=(pc == 0), stop=(pc == 2)) for pc in range(3)],
                     reads=[sq, CONST], writes=[bF])
                rstd_from(bF, 128, rs, scale=1.0 / 384.0)
                P.dve([I("scalar_tensor_tensor", out=oT_ssd.v[:, pc, c * 128:(c + 1) * 128], in0=vv.flat[:, pc * 128:(pc + 1) * 128],
                         scalar=par[:, PO_SNG + pc:PO_SNG + pc + 1], in1=rs.flat, op0=ALU.mult, op1=ALU.mult) for pc in range(3)],
                      reads=[vv, rs, "par"], writes=[oT_ssd.sub(pc * T + c * 128, pc * T + (c + 1) * 128) for pc in range(3)])

            scan(H=6, hg=3,
                 la_of=lambda c: (lav.v[:, c], [lav]),
                 dt_of=lambda c: (dtv.v[:, c], [dtv]),
                 kt_chunk=lambda i, c: (BT.v[:, i, c * 128:(c + 1) * 128], [BT]),
                 n_kt=2, kidx=lambda g: g,
                 qg_of=lambda c, p: (CT.v[:, :, c * 128:(c + 1) * 128], [CT]),
                 vtok_of=lambda c: (Xtok.v[:, c, :], [Xtok]),
                 extra_terms=lambda c, h: [(Xtok.v[:, c, h * 64:(h + 1) * 64], dI_t[:, h, :])],
                 extra_reads=["dI"],
                 finish=finish, dbuf=False, shared=sq, skip_out=(() if ctx_out else tuple(ctx_chunks)), pre=pre)
            arena.release(m0)

        def rope_store(src, src_reads, n, t0, dst_ap, dst_buf, ropeb, scr, scale=1.0, dsts=None):
            qb, t1, t2 = scr
            bR = banks[7]
            if scale == 1.0:
                P.act(I("copy", out=qb.flat[:, :n], in_=src), reads=src_reads, writes=[qb])
            else:
                P.act(I("mul", out=qb.flat[:, :n], in_=src, mul=scale), reads=src_reads, writes=[qb])
            P.pe(I("matmul", out=bR.ap[:, :n], lhsT=rotb, rhs=qb.flat[:, :n], start=True, stop=True), reads=[qb, CONST], writes=[bR])
            P.dve(I("tensor_tensor", out=t1.flat[:, :n], in0=qb.flat[:, :n], in1=ropeb.v[:, 0, t0:t0 + n], op=ALU.mult), reads=[qb, ropeb], writes=[t1])
            P.dve(I("tensor_tensor", out=t2.flat[:, :n], in0=bR.ap[:, :n], in1=ropeb.v[:, 1, t0:t0 + n], op=ALU.mult), reads=[bR, ropeb], writes=[t2])
            if dsts is None:
                dsts = [(0, 128, dst_ap, dst_buf)]
            for (p0, p1, d_ap, d_buf) in dsts:
                P.pool(I("tensor_tensor", out=d_ap, in0=t1.flat[p0:p1, :n], in1=t2.flat[p0:p1, :n], op=ALU.add), reads=[t1, t2], writes=[d_buf])

        def load_rope():
            ropeb = arena.alloc((2, T), BF16)
            P.dma("pool", I("dma_start", out=ropeb.v, in_=rope_d.rearrange("p (a t) -> p a t", a=2)), writes=[ropeb])
            return ropeb

        def phase_ret(l, oT_ret, ctx_out):
            par = par_t[:, l, :]
            m0 = arena.mark()
            QTr = arena.alloc((2, T), BF16)
            KTr = arena.alloc((2, T), BF16)
            Vtok = arena.alloc((NT, 256), BF16)
            wg = arena.alloc((KC, 256), BF16)
            load_w(win_d[l][:, CO_GR:CO_GR + 256], wg)
            lar = arena.alloc((2, 4), F32)
            m1 = arena.mark()
            ropeb = load_rope()
            wq = arena.alloc((KC, 256), BF16)
            wk_ = arena.alloc((KC, 256), BF16)
            wv = arena.alloc((KC, 256), BF16)
            load_w(win_d[l][:, CO_QR:CO_QR + 256], wq)
            load_w(win_d[l][:, CO_KR:CO_KR + 256], wk_)
            load_w(win_d[l][:, CO_VR:CO_VR + 256], wv)
            scr = (arena.alloc(512, BF16), arena.alloc(512, F32), arena.alloc(512, F32))
            tl = sml_t[:, 16:24]
            P.act(I("activation", out=tl, in_=par[:, PO_RLOG:PO_RLOG + 8], func=AF.Exp, scale=-1.0), reads=["par"], writes=["tl"])
            P.act(I("activation", out=tl, in_=tl, func=AF.Ln, bias=1.0, scale=1.0), reads=["tl"], writes=["tl"])
            P.dve(I("tensor_scalar", out=lar.flat, in0=tl, scalar1=-1.0, scalar2=None, op0=ALU.mult), reads=["tl"], writes=[lar])
            for (w, dstT, scale) in [(wq, QTr, 1.0), (wk_, KTr, 0.125)]:
                for pc in range(2):
                    for bi, (t0, n) in enumerate(blocks):
                        bank = banks[1 + bi % 2]
                        proj_fm(bank, w, pc * 128, 128, t0, n)
                        rope_store(bank.ap[:, :n], [bank], n, t0, dstT.v[:, pc, t0:t0 + n], dstT.sub(pc * T + t0, pc * T + t0 + n), ropeb, scr, scale)
            for tt in range(0, NT, 2):
                n2 = min(2, NT - tt)
                bank = banks[3 + (tt // 2) % 2]
                P.pe([I("matmul", out=bank.ap[:, i * 256:(i + 1) * 256], lhsT=hT[:, k, (tt + i) * 128:(tt + i + 1) * 128], rhs=wv.v[:, k, :],
                        start=(k == 0), stop=(k == KC - 1)) for i in range(n2) for k in range(KC)],
                     reads=[wv] + hk(tt * 128, n2 * 128), writes=[bank])
                P.act(I("copy", out=Vtok.v[:, tt:tt + n2, :], in_=bank.ap[:, 0:n2 * 256].rearrange("p (t f) -> p t f", f=256)),
                      reads=[bank], writes=[Vtok])
            arena.release(m1)
            Qz = [arena.alloc((4, 128), BF16) for _ in range(2)]
            y32 = arena.alloc(256, F32)
            yb = arena.alloc(256, BF16)
            yc = arena.alloc(256, F32)
            sq = arena.alloc(256, BF16)
            rs = arena.alloc(256, F32)
            sgs = [arena.alloc(256, F32) for _ in range(2)]

            def qg_of(c, p):
                Qz_ = Qz[p]
                P.dve(I("tensor_tensor", out=Qz_.flat.rearrange("p (a b i) -> p a b i", a=2, b=2),
                        in0=QTr.v[:, :, c * 128:(c + 1) * 128].unsqueeze(2).to_broadcast([128, 2, 2, 128]),
                        in1=hmb.unsqueeze(1).unsqueeze(3).to_broadcast([128, 2, 2, 128]), op=ALU.mult),
                      reads=[QTr, CONST], writes=[Qz_])
                return (Qz_.v, [Qz_])

            def pre(c, p, bZ):
                sg = sgs[p]
                P.pe([I("matmul", out=bZ.ap[:, pc * 128:(pc + 1) * 128], lhsT=wg.v[:, k, pc * 128:(pc + 1) * 128], rhs=hT[:, k, c * 128:(c + 1) * 128],
                        start=(k == 0), stop=(k == KC - 1)) for pc in range(2) for k in range(KC)],
                     reads=[wg] + hk(c * 128, 128), writes=[bZ])
                sigmoid_from(bZ, 256, sg)
                P.dve(I("tensor_tensor", out=sg.flat, in0=bZ.ap[:, 0:256], in1=sg.flat, op=ALU.mult), reads=[bZ, sg], writes=[sg])

            def finish(c, bY, bF, p):
                sg = sgs[p]
                P.act(I("copy", out=y32.flat, in_=bY.ap[:, 0:256]), reads=[bY], writes=[y32])
                P.dve(I("tensor_copy", out=yb.flat, in_=y32.flat), reads=[y32], writes=[yb])
                P.pe(I("matmul", out=bF.ap[:, 0:256], lhsT=blk64b, rhs=yb.flat, start=True, stop=True), reads=[yb, CONST], writes=[bF])
                P.dve(I("tensor_tensor", out=yc.flat, in0=y32.flat, in1=bF.ap[:, 0:256], op=ALU.subtract), reads=[y32, bF], writes=[yc])
                P.act(I("activation", out=sq.flat, in_=yc.flat, func=AF.Square), reads=[yc], writes=[sq])
                P.pe(I("matmul", out=bF.ap[:, 0:256], lhsT=blk64b, rhs=sq.flat, start=True, stop=True), reads=[sq, CONST], writes=[bF])
                rstd_from(bF, 256, rs)
                P.dve(I("tensor_tensor", out=yc.flat, in0=yc.flat, in1=rs.flat, op=ALU.mult), reads=[yc, rs], writes=[yc])
                P.act([I("activation", out=yc.flat[:, pc * 128:(pc + 1) * 128], in_=yc.flat[:, pc * 128:(pc + 1) * 128], func=AF.Identity,
                         bias=par[:, PO_RGB + pc:PO_RGB + pc + 1], scale=par[:, PO_RGG + pc:PO_RGG + pc + 1]) for pc in range(2)],
                      reads=[yc, "par"], writes=[yc])
                P.dve(I("tensor_tensor", out=oT_ret.v[:, :, c * 128:(c + 1) * 128], in0=yc.flat.rearrange("p (a i) -> p a i", a=2),
                        in1=sg.flat.rearrange("p (a i) -> p a i", a=2), op=ALU.mult),
                      reads=[yc, sg], writes=[oT_ret.sub(pc * T + c * 128, pc * T + (c + 1) * 128) for pc in range(2)])

            scan(H=4, hg=1,
                 la_of=lambda c: (lar.v, [lar]),
                 dt_of=lambda c: None,
                 kt_chunk=lambda i, c: (KTr.v[:, i, c * 128:(c + 1) * 128], [KTr]),
                 n_kt=2, kidx=lambda g: g // 2,
                 qg_of=qg_of,
                 vtok_of=lambda c: (Vtok.v[:, c, :], [Vtok]),
                 extra_terms=lambda c, h: [],
                 extra_reads=[],
                 finish=finish, dbuf=True, skip_out=(() if ctx_out else tuple(ctx_chunks)), pre=pre)
            arena.release(m0)

        def phase_att(l, oT_att, ctx_out):
            par = par_t[:, l, :]
            m0 = arena.mark()
            ropeb = load_rope()
            kTd = arena.alloc((2, T), BF16)
            Vext = arena.alloc((NT, 2, 128), BF16)
            qz = [arena.alloc(T, BF16) for _ in range(2)]
            mA = arena.mark()
            scr = (arena.alloc(512, BF16), arena.alloc(512, F32), arena.alloc(512, F32))
            sqb = arena.alloc(512, BF16)
            rs = arena.alloc(512, F32)
            qn = arena.alloc(512, F32)
            mB = arena.mark()
            gq8 = sml_t[:, 32:33]
            P.dve(I("tensor_scalar", out=gq8, in0=par[:, PO_QG:PO_QG + 1], scalar1=0.125, scalar2=None, op0=ALU.mult), reads=["par"], writes=["gq8"])
            bN = banks[6]

            def normrope(bank, n, t0, gcol, greads, dsts):
                P.act(I("activation", out=sqb.flat[:, :n], in_=bank.ap[:, :n], func=AF.Square), reads=[bank], writes=[sqb])
                P.pe(I("matmul", out=bN.ap[:, :n], lhsT=blk64b, rhs=sqb.flat[:, :n], start=True, stop=True), reads=[sqb, CONST], writes=[bN])
                rstd_from(bN, n, rs)
                P.dve(I("scalar_tensor_tensor", out=qn.flat[:, :n], in0=bank.ap[:, :n], scalar=gcol, in1=rs.flat[:, :n], op0=ALU.mult, op1=ALU.mult),
                      reads=[bank, rs] + greads, writes=[qn])
                rope_store(qn.flat[:, :n], [qn], n, t0, None, None, ropeb, scr, dsts=dsts)

            wkd = arena.alloc((2, KC, 128), BF16)
            wv = arena.alloc((KC, 128), BF16)
            for g in range(2):
                for hh in range(2):
                    P.dma("pool", I("dma_start", out=wkd.v[:, g, :, hh * 64:(hh + 1) * 64],
                                    in_=win_d[l][:, CO_KA + g * 64:CO_KA + (g + 1) * 64].rearrange("(k p) n -> p k n", p=128)), writes=[wkd])
            load_w(win_d[l][:, CO_VA:CO_VA + 128], wv)
            P.pool(I("memset", ap=Vext.v[:, :, :, 64:65], constant=1.0), writes=[Vext])
            P.pool(I("memset", ap=qz[0].flat[64:128, :], constant=0.0), writes=[qz[0]])
            P.pool(I("memset", ap=qz[1].flat[0:64, :], constant=0.0), writes=[qz[1]])
            for g in range(2):
                for bi, (t0, n) in enumerate(blocks):
                    bank = banks[bi % 2]
                    P.pe([I("matmul", out=bank.ap[:, :n], lhsT=wkd.v[:, g, k, :], rhs=hT[:, k, t0:t0 + n], start=(k == 0), stop=(k == KC - 1))
                          for k in range(KC)], reads=[wkd] + hk(t0, n), writes=[bank])
                    normrope(bank, n, t0, par[:, PO_KG:PO_KG + 1], ["par"], [(0, 128, kTd.v[:, g, t0:t0 + n], kTd.sub(g * T + t0, g * T + t0 + n))])
            for tt in range(0, NT, 4):
                n4 = min(4, NT - tt)
                bank = banks[2 + (tt // 4) % 2]
                P.pe([I("matmul", out=bank.ap[:, i * 128:(i + 1) * 128], lhsT=hT[:, k, (tt + i) * 128:(tt + i + 1) * 128], rhs=wv.v[:, k, :],
                        start=(k == 0), stop=(k == KC - 1)) for i in range(n4) for k in range(KC)],
                     reads=[wv] + hk(tt * 128, n4 * 128), writes=[bank])
                bv = bank.ap[:, 0:n4 * 128].rearrange("p (t g d) -> p t g d", g=2, d=64)
                P.act(I("copy", out=Vext.v[:, tt:tt + n4, :, 0:64], in_=bv), reads=[bank], writes=[Vext])
                P.dve(I("tensor_copy", out=Vext.v[:, tt:tt + n4, :, 65:128], in_=bv[:, :, :, 0:63]), reads=[bank], writes=[Vext])
            arena.release(mB)
            wq = arena.alloc((KC, 128), BF16)
            end_off = arena.off
            arena.off = mA
            PT = [arena.alloc(512, BF16) for _ in range(4)]
            rr = arena.alloc(512, F32)
            bcs = arena.alloc(512, F32)
            tn = arena.alloc(512, BF16)
            assert arena.off <= mB
            arena.off = end_off
            pti = 0
            sbanks = [banks[0], banks[1], banks[2], banks[5]]
            obanks = [banks[3], banks[4]]
            bBc = banks[7]
            si = 0
            oi = 0
            for qc in range(3):
                w = wq
                load_w(win_d[l][:, CO_QA + qc * 128:CO_QA + (qc + 1) * 128], w)
                for bi, (t0, n) in enumerate(blocks):
                    bank = banks[bi % 2]
                    proj_fm(bank, w, 0, 128, t0, n)
                    normrope(bank, n, t0, gq8, ["gq8"], [(0, 64, qz[0].flat[0:64, t0:t0 + n], qz[0].sub(t0, t0 + n)),
                                                         (64, 128, qz[1].flat[64:128, t0:t0 + n], qz[1].sub(t0, t0 + n))])
                its = []
                for (t0, n) in blocks:
                    is_ctx = t0 >= L
                    if is_ctx and not ctx_out:
                        continue
                    kcs = ctx_chunks if is_ctx else list(range(NT))
                    for hh in range(2):
                        bO = obanks[oi % 2]
                        oi += 1
                        for ki, kc in enumerate(kcs):
                            its.append(dict(t0=t0, n=n, hh=hh, kc=kc, first=(ki == 0), last=(ki == len(kcs) - 1), bO=bO,
                                            bS=sbanks[si % 4], pt=PT[pti % 4]))
                            si += 1
                            pti += 1

                def emit_S(it):
                    t0, n, hh, kc, bSx, pt_ = it["t0"], it["n"], it["hh"], it["kc"], it["bS"], it["pt"]
                    g = (2 * qc + hh) // 3
                    P.pe(I("matmul", out=bSx.ap[:, :n], lhsT=kTd.v[:, g, kc * 128:(kc + 1) * 128], rhs=qz[hh].flat[:, t0:t0 + n], start=True, stop=True),
                         reads=[kTd.sub(g * T + kc * 128, g * T + (kc + 1) * 128), qz[hh].sub(t0, t0 + n)], writes=[bSx])
                    P.act(I("activation", out=pt_.flat[:, :n], in_=bSx.ap[:, :n], func=AF.Exp), reads=[bSx], writes=[pt_])

                def emit_PV(it):
                    t0, n, hh, kc, bO, pt_ = it["t0"], it["n"], it["hh"], it["kc"], it["bO"], it["pt"]
                    first, last = it["first"], it["last"]
                    g = (2 * qc + hh) // 3
                    P.pe(I("matmul", out=bO.ap[:, :n], lhsT=Vext.v[:, kc, g, :], rhs=pt_.flat[:, :n], start=first, stop=last),
                         reads=[Vext, pt_], writes=[bO])
                    if not last:
                        return
                    P.dve(I("reciprocal", out=rr.flat[64:65, :n], in_=bO.ap[64:65, :n]), reads=[bO], writes=[rr])
                    P.pe(I("matmul", out=bBc.ap[0:64, :n], lhsT=ones32[64:65, 0:64], rhs=rr.flat[64:65, :n], start=True, stop=True),
                         reads=[rr, CONST], writes=[bBc])
                    P.act(I("copy", out=bcs.flat[0:64, :n], in_=bBc.ap[0:64, :n]), reads=[bBc], writes=[bcs])
                    dst = oT_att.sub(qc * T + t0, qc * T + t0 + n)
                    if hh == 0:
                        P.dve(I("tensor_tensor", out=oT_att.v[0:64, qc, t0:t0 + n], in0=bO.ap[0:64, :n], in1=bcs.flat[0:64, :n], op=ALU.mult),
                              reads=[bO, bcs], writes=[dst])
                    else:
                        P.dve(I("tensor_tensor", out=tn.flat[0:64, :n], in0=bO.ap[0:64, :n], in1=bcs.flat[0:64, :n], op=ALU.mult),
                              reads=[bO, bcs], writes=[tn])
                        P.pe(I("matmul", out=bBc.ap[64:128, :n], lhsT=identb[0:64, 0:64], rhs=tn.flat[0:64, :n], start=True, stop=True),
                             reads=[tn, CONST], writes=[bBc])
                        P.act(I("copy", out=oT_att.v[64:128, qc, t0:t0 + n], in_=bBc.ap[64:128, :n]), reads=[bBc], writes=[dst])

                LOOK = 2
                for i in range(len(its) + LOOK):
                    if i < len(its):
                        emit_S(its[i])
                    if i >= LOOK:
                        emit_PV(its[i - LOOK])
            arena.release(m0)

        def phase_out(l, oT_parts, ctx_out):
            m0 = arena.mark()
            wo = arena.alloc((KC, D), BF16)
            load_w(wout_d[l], wo)
            yv = arena.alloc((KC, 512), F32)
            sqs = [arena.alloc(512, BF16) for _ in range(2)]
            rs = arena.alloc(512, F32)
            tmps = [arena.alloc(512, F32) for _ in range(2)]
            bMS = banks[6]
            pieces = []
            for (ob, nch) in oT_parts:
                for i in range(nch):
                    pieces.append((ob, i))
            for (t0, n) in blocks:
                if t0 >= L and not ctx_out:
                    continue
                col = 0 if t0 < L else 1
                for dc in range(KC):
                    bank = banks[dc % 4]
                    P.pe([I("matmul", out=bank.ap[:, :n], lhsT=wo.v[:, k, dc * 128:(dc + 1) * 128], rhs=ob.v[:, i, t0:t0 + n], start=(k == 0), stop=(k == KC - 1))
                          for k, (ob, i) in enumerate(pieces)],
                         reads=[wo] + [ob.sub(i * T + t0, i * T + t0 + n) for (ob, i) in pieces], writes=[bank])
                    P.act(I("copy", out=yv.v[:, dc, :n], in_=bank.ap[:, :n]), reads=[bank], writes=[yv.sub(dc * 512, dc * 512 + 512)])
                ms_block(lambda k: yv.v[:, k, :n], n, bMS, sqs, [yv])
                rstd_from(bMS, n, rs)
                for k in range(KC):
                    tmp = tmps[k % 2]
                    P.dve(I("scalar_tensor_tensor", out=tmp.flat[:, :n], in0=yv.v[:, k, :n], scalar=ab_t[:, 2, k, col:col + 1],
                            in1=rs.flat[:, :n], op0=ALU.mult, op1=ALU.mult),
                          reads=[yv.sub(k * 512, k * 512 + 512), rs, "ab"], writes=[tmp])
                    P.pool(I("tensor_tensor", out=xT[:, k, t0:t0 + n], in0=xT[:, k, t0:t0 + n], in1=tmp.flat[:, :n], op=ALU.add),
                           reads=[tmp] + xkk(t0, n), writes=xkk(t0, n))
            arena.release(m0)

        def phase_ffn(l, ctx_out):
            m0 = arena.mark()
            Tf = T if ctx_out else L
            groups = []
            t = 0
            while t < Tf:
                gn = min(768, Tf - t)
                groups.append((t, gn))
                t += gn
            yacc = arena.alloc((KC, 768), F32)
            w1 = [arena.alloc((KC, 512), BF16) for _ in range(2)]
            w2 = [arena.alloc((4, D), BF16) for _ in range(2)]
            uT = [arena.alloc((4, 512), BF16) for _ in range(2)]
            rl = [arena.alloc(512, BF16) for _ in range(2)]
            sqs = [arena.alloc(512, BF16) for _ in range(2)]
            rs = arena.alloc(512, F32)
            tmps = [arena.alloc(512, F32) for _ in range(2)]
            bMS = banks[7]
            ui = 0
            wi = 0
            for (g0, gn) in groups:
                subs = []
                t = g0
                while t < g0 + gn:
                    n = min(512, g0 + gn - t)
                    subs.append((t, n))
                    t += n
                its = []
                for fg in range(8):
                    for (t0, n) in subs:
                        its.append(dict(fg=fg, t0=t0, n=n, first_of_fg=((t0, n) == subs[0])))

                def u_phase(it):
                    nonlocal ui, wi
                    fg, t0, n = it["fg"], it["t0"], it["n"]
                    if it["first_of_fg"]:
                        it["wa"], it["wb"] = w1[wi % 2], w2[wi % 2]
                        wi += 1
                        load_w(w1_d[l][:, fg * 512:(fg + 1) * 512], it["wa"])
                        load_w(w2_d[l][fg * 512:(fg + 1) * 512, :], it["wb"])
                        cur["wa"], cur["wb"] = it["wa"], it["wb"]
                    else:
                        it["wa"], it["wb"] = cur["wa"], cur["wb"]
                    u = uT[ui % 2]
                    ui += 1
                    it["u"] = u
                    for fc in range(4):
                        bank = banks[fc % 2]
                        proj_fm(bank, it["wa"], fc * 128, 128, t0, n)
                        uk = u.sub(fc * 512, fc * 512 + 512)
                        r_ = rl[fc % 2]
                        P.act(I("activation", out=r_.flat[:, :n], in_=bank.ap[:, :n], func=AF.Relu), reads=[bank], writes=[r_])
                        if fc % 2 == 0:
                            P.dve(I("tensor_tensor", out=u.v[:, fc, :n], in0=r_.flat[:, :n], in1=r_.flat[:, :n], op=ALU.mult), reads=[r_], writes=[uk])
                        else:
                            P.act(I("activation", out=u.v[:, fc, :n], in_=r_.flat[:, :n], func=AF.Square), reads=[r_], writes=[uk])

                def y_phase(it):
                    fg, t0, n, u, wb = it["fg"], it["t0"], it["n"], it["u"], it["wb"]
                    for dc in range(KC):
                        bank = banks[2 + dc % 4]
                        P.pe([I("matmul", out=bank.ap[:, :n], lhsT=wb.v[:, fc, dc * 128:(dc + 1) * 128], rhs=u.v[:, fc, :n], start=(fc == 0), stop=(fc == 3))
                              for fc in range(4)], reads=[wb, u], writes=[bank])
                        ya = yacc.v[:, dc, t0 - g0:t0 - g0 + n]
                        yk = yacc.sub(dc * 768 + t0 - g0, dc * 768 + t0 - g0 + n)
                        if fg == 0:
                            P.act(I("copy", out=ya, in_=bank.ap[:, :n]), reads=[bank], writes=[yk])
                        else:
                            P.dve(I("tensor_tensor", out=ya, in0=ya, in1=bank.ap[:, :n], op=ALU.add), reads=[bank, yk], writes=[yk])

                cur = {}
                for i in range(len(its) + 1):
                    if i < len(its):
                        u_phase(its[i])
                    if i >= 1:
                        y_phase(its[i - 1])
                for (t0, n) in subs:
                    col = 0 if t0 < L else 1
                    o = t0 - g0
                    ms_block(lambda k: yacc.v[:, k, o:o + n], n, bMS, sqs, [yacc])
                    rstd_from(bMS, n, rs)
                    for k in range(KC):
                        tmp = tmps[k % 2]
                        P.dve(I("scalar_tensor_tensor", out=tmp.flat[:, :n], in0=yacc.v[:, k, o:o + n], scalar=ab_t[:, 5, k, col:col + 1],
                                in1=rs.flat[:, :n], op0=ALU.mult, op1=ALU.mult),
                              reads=[yacc, rs, "ab"], writes=[tmp])
                        P.pool(I("tensor_tensor", out=xT[:, k, t0:t0 + n], in0=xT[:, k, t0:t0 + n], in1=tmp.flat[:, :n], op=ALU.add),
                               reads=[tmp] + xkk(t0, n), writes=xkk(t0, n))
            arena.release(m0)

        def phase_store():
            m = arena.mark()
            stg = [arena.alloc(D, F32) for _ in range(2)]
            outs = []
            for tt in range(NTL):
                s = stg[tt % 2]
                for half in range(2):
                    bank = banks[(tt % 2) * 2 + half]
                    P.pe([I("transpose", out=bank.ap[:, kk * 128:(kk + 1) * 128], in_=xT[:, half * 4 + kk, tt * 128:(tt + 1) * 128], identity=ident32)
                          for kk in range(4)], reads=xkk(tt * 128, 128) + [CONST], writes=[bank])
                    if half == 0:
                        P.act(I("copy", out=s.flat[:, 0:512], in_=bank.ap[:, :]), reads=[bank], writes=[s.sub(0, 512)])
                    else:
                        P.dve(I("tensor_copy", out=s.flat[:, 512:1024], in_=bank.ap[:, :]), reads=[bank], writes=[s.sub(512, 1024)])
                P.dma("sp", I("dma_start", out=out_d[tt * 128:(tt + 1) * 128, :], in_=s.flat), reads=[s], writes=[("out", tt)])
                outs.append(("out", tt))
            P.op("sp", [], reads=outs)
            arena.release(m)

        def assemble():
            phase_load()
            if stop_after == "load":
                dump(xT[:, 0, :], 0, T, xkk(0, T))
                return
            for l in range(depth):
                ctx_out = l < depth - 1
                phase_mod(l)
                if stop_after == "mod":
                    dump(ab_t[:].rearrange("p a k c -> p (a k c)"), 0, 96, ["ab"])
                    return
                phase_norm_h(0, 1)
                if stop_after == "norm1":
                    dump(hT[:, 0, :], 0, T, hk(0, T))
                    return
                base = arena.mark()
                oT_ssd = arena.alloc((3, T), BF16)
                oT_ret = arena.alloc((2, T), BF16)
                oT_att = arena.alloc((3, T), BF16)
                arena.off = oT_ssd.off + oT_ssd.n * 2
                phase_ssd(l, oT_ssd, ctx_out)
                if stop_after == "ssd":
                    dump(oT_ssd.flat, 0, 3 * T, [oT_ssd])
                    return
                arena.off = oT_ret.off + oT_ret.n * 2
                phase_ret(l, oT_ret, ctx_out)
                if stop_after == "ret":
                    dump(oT_ret.flat, 0, 2 * T, [oT_ret])
                    return
                arena.off = oT_att.off + oT_att.n * 2
                phase_att(l, oT_att, ctx_out)
                if stop_after == "att":
                    dump(oT_att.flat, 0, 3 * T, [oT_att])
                    return
                phase_out(l, [(oT_att, 3), (oT_ret, 2), (oT_ssd, 3)], ctx_out)
                arena.release(base)
                if stop_after == "out":
                    dump(xT[:, 0, 0:128], 0, 128, xkk(0, 128))
                    return
                phase_norm_h(3, 4)
                phase_ffn(l, ctx_out)
                if stop_after == "layer":
                    dump(xT[:, 0, 0:128], 0, 128, xkk(0, 128))
                    return
            phase_store()

        try:
            assemble()
        except _Stop:
            pass
        P.emit(st)
    build.arena_hi = arena.hi
    build.n_ops = len(P.ops)
    build.stats = P.stats
    return nc


def make_in_maps(inp, L, LC, depth, nb):
    con = host_consts()
    rope = host_rope(L, LC)
    par = host_params(inp, depth)
    f32 = lambda a: np.ascontiguousarray(np.asarray(a, dtype=np.float32))
    shared = {
        "par": par, "con": con, "rope": rope,
        "w_mod": f32(inp["w_mod"]), "w_in": f32(inp["w_in"]), "w_out": f32(inp["w_out"]),
        "w_ff1": f32(inp["w_ff1"]), "w_ff2": f32(inp["w_ff2"]),
    }
    maps = []
    for b in range(nb):
        cv = np.zeros((128, KC, 2), np.float32)
        cv[:, :, 0] = np.asarray(inp["c"][b]).reshape(KC, 128).T
        cv[:, :, 1] = np.asarray(inp["c_ctx"]).reshape(KC, 128).T
        m = dict(shared)
        m["x"] = f32(inp["x"][b])
        m["ctx"] = f32(inp["ctx"][b])
        m["cv"] = cv.reshape(128, 16)
        maps.append(m)
    return maps


def kernel(**inputs):
    inp = {k: np.asarray(v) for k, v in inputs.items()}
    B, L, _ = inp["x"].shape
    LC = inp["ctx"].shape[1]
    depth = inp["w_mod"].shape[0]
    nc = build(L, LC, depth)
    maps = make_in_maps(inp, L, LC, depth, B)
    res = run_bass_kernel_spmd(nc, maps, core_ids=list(range(B)))
    return np.stack([np.asarray(r["out"], dtype=np.float32) for r in res.results], axis=0)
```

```python
import numpy as np
from contextlib import ExitStack
import concourse.bass as bass
import concourse.mybir as mybir
from concourse.bass_utils import run_bass_kernel_spmd

F32 = mybir.dt.float32
BF16 = mybir.dt.bfloat16
AF = mybir.ActivationFunctionType
ALU = mybir.AluOpType

D = 1024
KC = 8
D_IN = 2956
D_FF = 4096
EPS = 1e-6
GRID_W = 64
ROPE_THETA = 10000.0

ENGINES = ("pe", "act", "dve", "pool", "sp")
SEM_CHUNK = 1000
DMA_POOL = 8


class Buf:
    def __init__(self, ap, keys):
        self.ap = ap
        self.keys = tuple(keys)


def _keys(items):
    out = []
    for it in items:
        if isinstance(it, Buf):
            out.extend(it.keys)
        elif isinstance(it, (list,)):
            out.extend(_keys(it))
        else:
            out.append(it)
    return out


class Op:
    __slots__ = ("eng", "fn", "dma", "deps", "has_dep", "ms", "dma_idx", "idx")


class Prog:
    def __init__(self, nc):
        self.nc = nc
        self.ops = []
        self.last_writer = {}
        self.readers = {}

    def op(self, eng, fn, reads=(), writes=(), dma=False):
        if isinstance(fn, tuple):
            fn = [fn]
        o = Op()
        o.eng, o.fn, o.dma = eng, fn, dma
        o.has_dep, o.ms, o.dma_idx = False, None, None
        o.idx = len(self.ops)
        rk = _keys(reads)
        wk = _keys(writes)
        pr = [k for k in rk if isinstance(k, tuple) and k and k[0] == "ps"]
        if pr:
            rk = [k for k in rk if not (isinstance(k, tuple) and k and k[0] == "ps")]
            wk = wk + pr
        deps = set()
        lw, rd = self.last_writer, self.readers
        for r in rk:
            w = lw.get(r)
            if w is not None:
                deps.add(w)
        for r in wk:
            w = lw.get(r)
            if w is not None:
                deps.add(w)
            x = rd.get(r)
            if x:
                deps.update(x)
        for r in rk:
            rd.setdefault(r, []).append(o.idx)
        for r in wk:
            lw[r] = o.idx
            rd[r] = []
        deps.discard(o.idx)
        if eng == "pe":
            deps = {d for d in deps if self.ops[d].eng != "pe"}
        o.deps = deps
        self.ops.append(o)
        return o

    def pe(self, fn, reads=(), writes=()):
        return self.op("pe", fn, reads, writes)

    def act(self, fn, reads=(), writes=()):
        return self.op("act", fn, reads, writes)

    def dve(self, fn, reads=(), writes=()):
        return self.op("dve", fn, reads, writes)

    def pool(self, fn, reads=(), writes=()):
        return self.op("pool", fn, reads, writes)

    def dma(self, eng, fn, reads=(), writes=()):
        return self.op(eng, fn, reads, writes, dma=True)

    def emit(self, stack):
        nc = self.nc
        ops = self.ops
        for o in ops:
            for d in o.deps:
                ops[d].has_dep = True
        cnt = {e: 0 for e in ENGINES}
        dcnt = {e: 0 for e in ENGINES}
        for o in ops:
            if o.dma:
                o.dma_idx = dcnt[o.eng]
                dcnt[o.eng] += 1
            elif o.has_dep:
                cnt[o.eng] += 1
                o.ms = cnt[o.eng]
        self.stats = dict(ms=dict(cnt), dma=dict(dcnt), n_ops={e: sum(1 for o in ops if o.eng == e) for e in ENGINES})
        esems, dsems = {}, {}
        for e in ENGINES:
            n = (cnt[e] + SEM_CHUNK - 1) // SEM_CHUNK
            esems[e] = [stack.enter_context(nc.semaphore(f"c_{e}_{i}")) for i in range(n)]
            n = min(DMA_POOL, dcnt[e])
            dsems[e] = [stack.enter_context(nc.semaphore(f"d_{e}_{i}")) for i in range(n)]

        def sem_of(o):
            if o.dma:
                return dsems[o.eng][o.dma_idx % DMA_POOL], 16 * (o.dma_idx // DMA_POOL + 1)
            m = o.ms - 1
            return esems[o.eng][m // SEM_CHUNK], (m % SEM_CHUNK) + 1

        by_eng = {e: [o for o in ops if o.eng == e] for e in ENGINES}
        block = stack.enter_context(nc.Block())

        def run(e, eng):
            waited = {}
            for o in by_eng[e]:
                need = {}
                for d in o.deps:
                    s, v = sem_of(ops[d])
                    k = id(s)
                    if need.get(k, (None, 0))[1] < v:
                        need[k] = (s, v)
                if o.dma and o.dma_idx >= DMA_POOL:
                    s = dsems[e][o.dma_idx % DMA_POOL]
                    v = 16 * (o.dma_idx // DMA_POOL)
                    k = id(s)
                    if need.get(k, (None, 0))[1] < v:
                        need[k] = (s, v)
                for k, (s, v) in need.items():
                    if waited.get(k, 0) < v:
                        eng.wait_ge(s, v)
                        waited[k] = v
                inst = None
                for (mname, kw) in o.fn:
                    inst = getattr(eng, mname)(**kw)
                if inst is None:
                    continue
                if o.dma:
                    inst.then_inc(sem_of(o)[0], 16)
                elif o.ms is not None:
                    inst.then_inc(sem_of(o)[0], 1)

        if by_eng["pe"]:
            block.tensor(lambda eng: run("pe", eng))
        if by_eng["act"]:
            block.scalar(lambda eng: run("act", eng))
        if by_eng["dve"]:
            block.vector(lambda eng: run("dve", eng))
        if by_eng["pool"]:
            block.gpsimd(lambda eng: run("pool", eng))
        if by_eng["sp"]:
            block.sync(lambda eng: run("sp", eng))


SLOT = 512


class ABuf(Buf):
    def __init__(self, flat, off, esz, shape):
        self.flat = flat
        self.off = off
        self.esz = esz
        self.shape = tuple(shape)
        n = int(np.prod(shape))
        self.n = n
        keys = [("A", s) for s in range(off // SLOT, (off + n * esz - 1) // SLOT + 1)]
        if len(shape) == 1:
            v = flat
        elif len(shape) == 2:
            v = flat.rearrange("p (a b) -> p a b", a=shape[0])
        elif len(shape) == 3:
            v = flat.rearrange("p (a b c) -> p a b c", a=shape[0], b=shape[1])
        elif len(shape) == 4:
            v = flat.rearrange("p (a b c d) -> p a b c d", a=shape[0], b=shape[1], c=shape[2])
        else:
            raise ValueError(shape)
        self.v = v
        Buf.__init__(self, v, keys)

    def sub(self, lo, hi):
        b0 = self.off + lo * self.esz
        b1 = self.off + hi * self.esz
        keys = [("A", s) for s in range(b0 // SLOT, (b1 - 1) // SLOT + 1)]
        return Buf(self.flat[:, lo:hi], keys)


class Arena:
    def __init__(self, nc, st, nbytes):
        self.nbytes = nbytes
        self.t = st.enter_context(nc.sbuf_tensor("arena", [128, nbytes // 2], BF16))
        self.off = 0
        self.hi = 0

    def alloc(self, shape, dt):
        if isinstance(shape, int):
            shape = (shape,)
        esz = 4 if dt == F32 else 2
        n = int(np.prod(shape))
        nb = n * esz
        off = (self.off + 63) // 64 * 64
        assert off + nb <= self.nbytes, f"arena overflow: need {off + nb} of {self.nbytes}"
        self.off = off + nb
        self.hi = max(self.hi, self.off)
        flat = self.t[:, off // 2:(off + nb) // 2]
        if dt == F32:
            flat = flat.bitcast(F32)
        return ABuf(flat, off, esz, shape)

    def mark(self):
        return self.off

    def release(self, m):
        self.off = m


C_ID, C_UF, C_UB, C_ONE, C_BLK, C_ROT, C_OND, C_MF, C_MB, C_HM = range(10)
NCON = 10 * 128


def host_consts():
    c = np.zeros((128, NCON), np.float32)
    i = np.arange(128)
    c[:, C_ID * 128:(C_ID + 1) * 128] = np.eye(128)
    c[:, C_UF * 128:(C_UF + 1) * 128] = (i[:, None] <= i[None, :])
    c[:, C_UB * 128:(C_UB + 1) * 128] = (i[:, None] >= i[None, :])
    c[:, C_ONE * 128:(C_ONE + 1) * 128] = 1.0
    blk = (i[:, None] // 64 == i[None, :] // 64).astype(np.float32) / 64.0
    c[:, C_BLK * 128:(C_BLK + 1) * 128] = blk
    rot = np.zeros((128, 128), np.float32)
    for d in range(128):
        if d % 32 < 16:
            rot[d + 16, d] = -1.0
        else:
            rot[d - 16, d] = 1.0
    c[:, C_ROT * 128:(C_ROT + 1) * 128] = rot
    c[:, C_OND * 128:(C_OND + 1) * 128] = 1.0 / 1024.0
    c[:, C_MF * 128:(C_MF + 1) * 128] = (i[None, :] >= i[:, None])
    c[:, C_MB * 128:(C_MB + 1) * 128] = (i[None, :] < i[:, None])
    hm = np.zeros((128, 128), np.float32)
    hm[:64, 0] = 1.0
    hm[64:, 1] = 1.0
    c[:, C_HM * 128:(C_HM + 1) * 128] = hm
    return c


def host_rope(L, LC):
    T = L + LC
    rows = L // GRID_W
    row = np.broadcast_to(np.arange(rows)[:, None], (rows, GRID_W)).reshape(L)
    col = np.broadcast_to(np.arange(GRID_W)[None, :], (rows, GRID_W)).reshape(L)
    half = 32
    inv_freq = (ROPE_THETA ** (-np.arange(0, half, 2, dtype=np.float32) / half)).astype(np.float32)
    ang = np.stack([row, col], axis=-1).astype(np.float32)[:, :, None] * inv_freq
    ang = np.concatenate([ang, ang], axis=-1).reshape(L, 64)
    cos = np.ones((64, T), np.float32)
    sin = np.zeros((64, T), np.float32)
    cos[:, :L] = np.cos(ang).T
    sin[:, :L] = np.sin(ang).T
    tab = np.zeros((128, 2, T), np.float32)
    tab[:64, 0] = cos
    tab[64:, 0] = cos
    tab[:64, 1] = sin
    tab[64:, 1] = sin
    return tab.reshape(128, 2 * T)


PO_BMOD = 0
PO_NG = PO_BMOD + 48
PO_QG = PO_NG + 32
PO_KG = PO_QG + 1
PO_RGG = PO_KG + 1
PO_RGB = PO_RGG + 2
PO_RLOG = PO_RGB + 2
PO_CW = PO_RLOG + 8
PO_CB = PO_CW + 35
PO_DTB = PO_CB + 7
PO_ALOG = PO_DTB + 12
PO_SD = PO_ALOG + 12
PO_SNG = PO_SD + 6
NPAR = PO_SNG + 3


def host_params(inp, depth):
    par = np.zeros((128, depth, NPAR), np.float32)
    fm = lambda v: np.ascontiguousarray(v.reshape(-1, 128).T)
    for l in range(depth):
        p = par[:, l]
        p[:, PO_BMOD:PO_BMOD + 48] = fm(inp["b_mod"][l])
        p[:, PO_NG:PO_NG + 32] = fm(inp["norm_g"][l].reshape(-1))
        p[:, PO_QG] = np.tile(inp["q_norm_g"][l], 2)
        p[:, PO_KG] = np.tile(inp["k_norm_g"][l], 2)
        p[:, PO_RGG:PO_RGG + 2] = fm(inp["ret_gn_g"][l])
        p[:, PO_RGB:PO_RGB + 2] = fm(inp["ret_gn_b"][l])
        p[:, PO_RLOG:PO_RLOG + 8] = np.broadcast_to(inp["ret_decay_logit"][l].reshape(1, 8), (128, 8))
        cw = inp["ssd_conv_w"][l]
        for cc in range(7):
            p[:, PO_CW + cc * 5:PO_CW + cc * 5 + 5] = cw[:, cc * 128:(cc + 1) * 128].T
        p[:, PO_CB:PO_CB + 7] = fm(inp["ssd_conv_b"][l])
        p[:, PO_DTB:PO_DTB + 12] = np.broadcast_to(inp["ssd_dt_bias"][l].reshape(1, 12), (128, 12))
        p[:, PO_ALOG:PO_ALOG + 12] = np.broadcast_to(inp["ssd_a_log"][l].reshape(1, 12), (128, 12))
        p[:, PO_SD:PO_SD + 6] = np.broadcast_to(inp["ssd_d"][l].reshape(1, 6), (128, 6))
        p[:, PO_SNG:PO_SNG + 3] = fm(inp["ssd_norm_g"][l])
    return par.reshape(128, depth * NPAR)


CO_QA, CO_KA, CO_VA = 0, 384, 512
CO_QR, CO_KR, CO_VR, CO_GR = 640, 896, 1152, 1408
CO_Z, CO_XBC, CO_DT = 1664, 2048, 2944


def I(m, **kw):
    return (m, kw)


class _Stop(Exception):
    pass


def build(L, LC, depth, stop_after=None, dbg=None):
    T = L + LC
    NT = T // 128
    NTL = L // 128
    blocks = [(i * 512, 512) for i in range(L // 512)] + [(L, LC)]
    lat_chunks = list(range(NTL))
    ctx_chunks = list(range(NTL, NT))
    fwd_order = ctx_chunks + lat_chunks
    bwd_order = ctx_chunks[::-1] + lat_chunks[::-1]

    nc = bass.Bass("TRN2", target_bir_lowering=False)
    dt_in = lambda name, shape: nc.dram_tensor(name, shape, F32, kind="ExternalInput").ap()
    x_d = dt_in("x", [L, D])
    ctx_d = dt_in("ctx", [LC, D])
    cv_d = dt_in("cv", [128, 16])
    par_d = dt_in("par", [128, depth * NPAR])
    con_d = dt_in("con", [128, NCON])
    rope_d = dt_in("rope", [128, 2 * T])
    wmod_d = dt_in("w_mod", [depth, D, 6 * D])
    win_d = dt_in("w_in", [depth, D, D_IN])
    wout_d = dt_in("w_out", [depth, D, D])
    w1_d = dt_in("w_ff1", [depth, D, D_FF])
    w2_d = dt_in("w_ff2", [depth, D_FF, D])
    out_d = nc.dram_tensor("out", [L, D], F32, kind="ExternalOutput").ap()
    dbg_d = None
    if dbg is not None:
        dbg_d = nc.dram_tensor("dbg", [128, dbg], F32, kind="ExternalOutput").ap()

    P = Prog(nc)
    st = ExitStack()
    with st:
        sbt = lambda name, shape, dt: st.enter_context(nc.sbuf_tensor(name, shape, dt))
        xT = sbt("xT", [128, KC, T], F32)
        hT = sbt("hT", [128, KC, T], BF16)
        c32_t = sbt("c32", [128, 4, 128], F32)
        cb_t = sbt("cb", [128, 6, 128], BF16)
        idb_t = sbt("idb", [128, 128], BF16)
        one_b_t = sbt("oneb", [128, 128], BF16)
        par_t = sbt("par_sb", [128, depth, NPAR], F32)
        ab_t = sbt("ab", [128, 6, KC, 2], F32)
        dI_t = sbt("dI", [128, 6, 128], BF16)
        sml_t = sbt("sml", [128, 64], F32)
        arena = Arena(nc, st, (nc.sbuf_bytes_remaining - 1024) // 64 * 64)
        banks = [Buf(st.enter_context(nc.psum_tensor(f"ps{i}", [128, 512], F32)), [("ps", i)]) for i in range(8)]

        ident32 = c32_t[:, 0, :]
        Uf32 = c32_t[:, 1, :]
        Ub32 = c32_t[:, 2, :]
        ones32 = c32_t[:, 3, :]
        blk64b = cb_t[:, 0, :]
        rotb = cb_t[:, 1, :]
        onesDb = cb_t[:, 2, :]
        mFb = cb_t[:, 3, :]
        mBb = cb_t[:, 4, :]
        hmb = cb_t[:, 5, 0:2]
        identb = idb_t[:, :]
        ones1b = one_b_t[:, :]
        CONST = "const"

        def hk(t0, n):
            return [("hT", t) for t in range(t0 // 128, (t0 + n + 127) // 128)]

        def xkk(t0, n):
            r = []
            for t in range(t0 // 128, (t0 + n + 127) // 128):
                r += [("xT", t, 0), ("xT", t, 1)]
            return r

        def b16(bank):
            return bank.ap[:].bitcast(BF16)

        P.dma("sp", I("dma_start", out=c32_t[:, 0:3, :], in_=con_d[:, 0:384].rearrange("p (a b) -> p a b", a=3)), writes=[CONST])
        P.dma("sp", I("dma_start", out=c32_t[:, 3, :], in_=con_d[:, C_ONE * 128:(C_ONE + 1) * 128]), writes=[CONST])
        P.dma("sp", I("dma_start", out=par_t[:], in_=par_d.rearrange("p (l n) -> p l n", l=depth)), writes=["par"])
        P.dma("pool", I("dma_start", out=cb_t[:], in_=con_d[:, C_BLK * 128:(C_HM + 1) * 128].rearrange("p (a b) -> p a b", a=6)), writes=[CONST])
        P.dma("pool", I("dma_start", out=idb_t[:], in_=con_d[:, C_ID * 128:(C_ID + 1) * 128]), writes=[CONST])
        P.dma("pool", I("dma_start", out=one_b_t[:], in_=con_d[:, C_ONE * 128:(C_ONE + 1) * 128]), writes=[CONST])

        def dump(ap_, col0, ncols, reads):
            if dbg_d is None:
                return
            if ap_.dtype != F32:
                tmp = arena.alloc(ncols, F32)
                P.dve(I("tensor_copy", out=tmp.flat, in_=ap_), reads=reads, writes=[tmp])
                ap_, reads = tmp.flat, [tmp]
            P.dma("sp", I("dma_start", out=dbg_d[:, col0:col0 + ncols], in_=ap_), reads=reads, writes=[("dbg", col0)])
            P.op("sp", [], reads=[("dbg", col0)])

        def phase_load():
            m = arena.mark()
            stg = [arena.alloc(D, F32) for _ in range(2)]
            for tt in range(NT):
                s = stg[tt % 2]
                src = x_d[tt * 128:(tt + 1) * 128, :] if tt < NTL else ctx_d[(tt - NTL) * 128:(tt - NTL + 1) * 128, :]
                P.dma("sp", I("dma_start", out=s.flat, in_=src), writes=[s])
                for half in range(2):
                    bank = banks[(tt % 2) * 2 + half]
                    P.pe([I("transpose", out=bank.ap[:, kk * 128:(kk + 1) * 128], in_=s.flat[:, (half * 4 + kk) * 128:(half * 4 + kk + 1) * 128], identity=ident32)
                          for kk in range(4)], reads=[s, CONST], writes=[bank])
                    dst = xT[:, half * 4:(half + 1) * 4, tt * 128:(tt + 1) * 128]
                    srcp = bank.ap[:].rearrange("p (k n) -> p k n", k=4)
                    if half == 0:
                        P.act(I("copy", out=dst, in_=srcp), reads=[bank], writes=[("xT", tt, half)])
                    else:
                        P.dve(I("tensor_copy", out=dst, in_=srcp), reads=[bank], writes=[("xT", tt, half)])
            arena.release(m)

        def phase_mod(l):
            m = arena.mark()
            cv32 = arena.alloc((KC, 2), F32)
            cvb = arena.alloc((KC, 2), BF16)
            P.dma("sp", I("dma_start", out=cv32.flat, in_=cv_d), writes=[cv32])
            P.act(I("activation", out=cvb.flat, in_=cv32.flat, func=AF.Silu), reads=[cv32], writes=[cvb])
            wm = [arena.alloc((KC, 768), BF16) for _ in range(2)]
            bank = banks[7]
            for piece in range(8):
                w = wm[piece % 2]
                P.dma("pool", I("dma_start", out=w.v, in_=wmod_d[l][:, piece * 768:(piece + 1) * 768].rearrange("(k p) n -> p k n", p=128)), writes=[w])
                ins = []
                for jj in range(6):
                    j = piece * 6 + jj
                    for k in range(KC):
                        ins.append(I("matmul", out=bank.ap[:, 2 * j:2 * j + 2], lhsT=w.v[:, k, jj * 128:(jj + 1) * 128], rhs=cvb.v[:, k, :],
                                     start=(k == 0), stop=(k == KC - 1)))
                P.pe(ins, reads=[w, cvb], writes=[bank])
            modT = arena.alloc((48, 2), F32)
            par = par_t[:, l, :]
            P.dve(I("tensor_tensor", out=modT.v, in0=bank.ap[:, 0:96].rearrange("p (j c) -> p j c", c=2),
                    in1=par[:, PO_BMOD:PO_BMOD + 48].unsqueeze(2).to_broadcast([128, 48, 2]), op=ALU.add),
                  reads=[bank, "par"], writes=[modT])
            ng = lambda f: par[:, PO_NG + f * 8:PO_NG + f * 8 + 8].unsqueeze(2).to_broadcast([128, KC, 2])
            mv = lambda i: modT.v[:, i * 8:(i + 1) * 8, :]
            P.dve([I("scalar_tensor_tensor", out=ab_t[:, 0], in0=mv(1), scalar=1.0, in1=ng(0), op0=ALU.add, op1=ALU.mult),
                   I("tensor_copy", out=ab_t[:, 1], in_=mv(0)),
                   I("tensor_tensor", out=ab_t[:, 2], in0=mv(2), in1=ng(1), op=ALU.mult),
                   I("scalar_tensor_tensor", out=ab_t[:, 3], in0=mv(4), scalar=1.0, in1=ng(2), op0=ALU.add, op1=ALU.mult),
                   I("tensor_copy", out=ab_t[:, 4], in_=mv(3)),
                   I("tensor_tensor", out=ab_t[:, 5], in0=mv(5), in1=ng(3), op=ALU.mult)],
                  reads=[modT, "par"], writes=["ab"])
            arena.release(m)

        def ms_block(src_k, n, bank, sqs, reads):
            for k in range(KC):
                sq = sqs[k % 2]
                P.act(I("activation", out=sq.flat[:, :n], in_=src_k(k), func=AF.Square), reads=reads, writes=[sq])
                P.pe(I("matmul", out=bank.ap[:, :n], lhsT=onesDb, rhs=sq.flat[:, :n], start=(k == 0), stop=(k == KC - 1)),
                     reads=[sq, CONST], writes=[bank])

        def rstd_from(bank, n, rs, scale=None):
            P.act(I("activation", out=rs.flat[:, :n], in_=bank.ap[:, :n], func=AF.Ln, bias=EPS, scale=(1.0 if scale is None else scale)),
                  reads=[bank], writes=[rs])
            P.act(I("activation", out=rs.flat[:, :n], in_=rs.flat[:, :n], func=AF.Exp, scale=-0.5), reads=[rs], writes=[rs])

        def sigmoid_from(bank, n, dst):
            P.act(I("activation", out=dst.flat[:, :n], in_=bank.ap[:, :n], func=AF.Exp, scale=-1.0), reads=[bank], writes=[dst])
            P.act(I("activation", out=dst.flat[:, :n], in_=dst.flat[:, :n], func=AF.Ln, bias=1.0, scale=1.0), reads=[dst], writes=[dst])
            P.act(I("activation", out=dst.flat[:, :n], in_=dst.flat[:, :n], func=AF.Exp, scale=-1.0), reads=[dst], writes=[dst])

        def phase_norm_h(ai, bi_):
            m = arena.mark()
            sqs = [arena.alloc(512, BF16) for _ in range(2)]
            rs = arena.alloc(512, F32)
            tmps = [arena.alloc(512, F32) for _ in range(2)]
            bank = banks[6]
            for (t0, n) in blocks:
                col = 0 if t0 < L else 1
                ms_block(lambda k: xT[:, k, t0:t0 + n], n, bank, sqs, xkk(t0, n))
                rstd_from(bank, n, rs)
                for k in range(KC):
                    tmp = tmps[k % 2]
                    P.dve(I("scalar_tensor_tensor", out=tmp.flat[:, :n], in0=xT[:, k, t0:t0 + n], scalar=ab_t[:, ai, k, col:col + 1],
                            in1=rs.flat[:, :n], op0=ALU.mult, op1=ALU.mult),
                          reads=xkk(t0, n) + [rs, "ab"], writes=[tmp])
                    P.act(I("activation", out=hT[:, k, t0:t0 + n], in_=tmp.flat[:, :n], func=AF.Identity,
                            bias=ab_t[:, bi_, k, col:col + 1], scale=1.0),
                          reads=[tmp, "ab"], writes=hk(t0, n))
            arena.release(m)

        def load_w(dram_ap, buf):
            P.dma("pool", I("dma_start", out=buf.v, in_=dram_ap.rearrange("(k p) n -> p k n", p=128)), writes=[buf])

        def proj_fm(bank, w, c0, M, t0, n):
            P.pe([I("matmul", out=bank.ap[0:M, :n], lhsT=w.v[:, k, c0:c0 + M], rhs=hT[:, k, t0:t0 + n], start=(k == 0), stop=(k == KC - 1))
                  for k in range(KC)], reads=[w] + hk(t0, n), writes=[bank])

        def scan(H, hg, la_of, dt_of, kt_chunk, n_kt, kidx, qg_of, vtok_of, extra_terms, extra_reads, finish, dbuf, shared=None, skip_out=(), pre=None):
            NG = H // hg
            HP = H * 64
            units = [(0, min(H, 4))] + ([(4, H - 4)] if H > 4 else [])
            NU = len(units)
            NPAR = 2 if dbuf else 1
            bS, bG, bT, bY = banks[0], banks[3], banks[4], banks[5]
            pbanks = [banks[1], banks[2], banks[6], banks[7]]
            bZ, bF = bG, bS
            m = arena.mark()
            sb_store = arena.alloc((NT, HP), BF16)
            Sst = [arena.alloc(HP, F32) for _ in range(2)]
            sfb = arena.alloc(HP, BF16)
            mk2 = lambda shape, dt: [[arena.alloc(shape, dt) for _ in range(2)] for _ in range(NPAR)]
            class _Pair:
                def __init__(self, buf, aps):
                    self.buf, self.aps = buf, aps

                def __getitem__(self, d):
                    return Buf(self.aps[d], self.buf.keys)
            ptc = [arena.alloc((4, H), F32) for _ in range(NPAR)]
            difc = [arena.alloc((2, H), F32) for _ in range(NPAR)]
            wvc = [arena.alloc((2, H), F32) for _ in range(NPAR)]
            decc = [arena.alloc((2, H), F32) for _ in range(NPAR)]
            cwc = [arena.alloc((2, H), F32) for _ in range(NPAR)]
            Vdt = mk2(HP, BF16) if dt_of(0) is not None else None
            E = mk2((H, 128), BF16)
            Ebc = mk2((H, 128), BF16)
            Gm = mk2((NG, 128), BF16)
            MT, QsT = E, Ebc
            Vw = shared if shared is not None else arena.alloc(HP, BF16)
            ktok = arena.alloc((n_kt, 128), BF16)
            R = [[arena.alloc((hn, 128), F32) for (_, hn) in units] for _ in range(2)]
            Ud = [Uf32, Ub32]
            md = [mFb, mBb]
            hq = lambda ap_: ap_.rearrange("p (h q) -> p h q", h=H)
            g4 = lambda ap_: ap_.rearrange("p (g a) i -> p g a i", g=NG)

            def small(c, d, p):
                lap, lar = la_of(c)
                P.pe([I("matmul", out=bS.ap[:, 0:H], lhsT=Ud[d], rhs=lap[:, d, :], start=True, stop=True),
                      I("matmul", out=bS.ap[:, H:2 * H], lhsT=ones32, rhs=lap[:, d, :], start=True, stop=True)],
                     reads=[CONST] + lar, writes=[bS])
                P.act([I("copy", out=ptc[p].v[:, d, :], in_=bS.ap[:, 0:H]), I("copy", out=ptc[p].v[:, 2 + d, :], in_=bS.ap[:, H:2 * H])],
                      reads=[bS], writes=[ptc[p]])
                P.dve(I("tensor_tensor", out=difc[p].v[:, d, :], in0=ptc[p].v[:, 2 + d, :], in1=ptc[p].v[:, d, :], op=ALU.subtract),
                      reads=[ptc[p]], writes=[difc[p]])
                P.act(I("activation", out=wvc[p].v[:, d, :], in_=difc[p].v[:, d, :], func=AF.Exp), reads=[difc[p]], writes=[wvc[p]])
                P.act(I("activation", out=decc[p].v[:, d, :], in_=ptc[p].v[:, 2 + d, :], func=AF.Exp), reads=[ptc[p]], writes=[decc[p]])
                dtv = dt_of(c)
                if dtv is not None:
                    P.dve(I("tensor_tensor", out=cwc[p].v[:, d, :], in0=wvc[p].v[:, d, :], in1=dtv[0][:, d, :], op=ALU.mult),
                          reads=[wvc[p]] + dtv[1], writes=[cwc[p]])
                else:
                    P.dve(I("tensor_copy", out=cwc[p].v[:, d, :], in_=wvc[p].v[:, d, :]), reads=[wvc[p]], writes=[cwc[p]])

            def small2(c, p):
                lap, lar = la_of(c)
                P.pe([I("matmul", out=bS.ap[:, 0:H], lhsT=Ud[0], rhs=lap[:, 0, :], start=True, stop=True),
                      I("matmul", out=bS.ap[:, H:2 * H], lhsT=Ud[1], rhs=lap[:, 1, :], start=True, stop=True),
                      I("matmul", out=bS.ap[:, 2 * H:4 * H], lhsT=ones32, rhs=lap.rearrange("p d h -> p (d h)"), start=True, stop=True)],
                     reads=[CONST] + lar, writes=[bS])
                P.act(I("copy", out=ptc[p].flat, in_=bS.ap[:, 0:4 * H]), reads=[bS], writes=[ptc[p]])
                P.dve(I("tensor_tensor", out=difc[p].flat, in0=ptc[p].flat[:, 2 * H:4 * H], in1=ptc[p].flat[:, 0:2 * H], op=ALU.subtract),
                      reads=[ptc[p]], writes=[difc[p]])
                P.act(I("activation", out=wvc[p].flat, in_=difc[p].flat, func=AF.Exp), reads=[difc[p]], writes=[wvc[p]])
                P.act(I("activation", out=decc[p].flat, in_=ptc[p].flat[:, 2 * H:4 * H], func=AF.Exp), reads=[ptc[p]], writes=[decc[p]])
                dtv = dt_of(c)
                if dtv is not None:
                    P.dve(I("tensor_tensor", out=cwc[p].flat, in0=wvc[p].flat, in1=dtv[0].rearrange("p d h -> p (d h)"), op=ALU.mult),
                          reads=[wvc[p]] + dtv[1], writes=[cwc[p]])
                else:
                    P.dve(I("tensor_copy", out=cwc[p].flat, in_=wvc[p].flat), reads=[wvc[p]], writes=[cwc[p]])

            pt = [_Pair(ptc[p], [ptc[p].v[:, 0, :], ptc[p].v[:, 1, :]]) for p in range(NPAR)]
            dec = [_Pair(decc[p], [decc[p].v[:, 0, :], decc[p].v[:, 1, :]]) for p in range(NPAR)]
            cw = [_Pair(cwc[p], [cwc[p].v[:, 0, :], cwc[p].v[:, 1, :]]) for p in range(NPAR)]

            def dstate(c, d, S, p):
                vt, vr = vtok_of(c)
                P.dve(I("tensor_tensor", out=hq(Vw.flat), in0=hq(vt), in1=cw[p][d].ap.unsqueeze(2).to_broadcast([128, H, 64]), op=ALU.mult),
                      reads=vr + [cw[p][d]], writes=[Vw])
                bt16 = b16(bT)
                P.pe([I("transpose", out=bt16[:, i * 128:(i + 1) * 128], in_=kt_chunk(i, c)[0], identity=identb) for i in range(n_kt)],
                     reads=[CONST] + kt_chunk(0, c)[1], writes=[bT])
                P.act(I("copy", out=ktok.flat, in_=bt16[:, 0:n_kt * 128]), reads=[bT], writes=[ktok])
                P.pe([I("matmul", out=bT.ap[:, g * hg * 64:(g + 1) * hg * 64], lhsT=ktok.v[:, kidx(g), :], rhs=Vw.flat[:, g * hg * 64:(g + 1) * hg * 64],
                        start=True, stop=True) for g in range(NG)], reads=[ktok, Vw], writes=[bT])
                P.dve(I("tensor_tensor", out=hq(S.flat), in0=hq(S.flat), in1=dec[p][d].ap.unsqueeze(2).to_broadcast([128, H, 64]), op=ALU.mult),
                      reads=[S, dec[p][d]], writes=[S])
                P.dve(I("tensor_tensor", out=S.flat, in0=S.flat, in1=bT.ap[:, 0:HP], op=ALU.add), reads=[S, bT], writes=[S])

            P.pool(I("memset", ap=Sst[1].flat, constant=0.0), writes=[Sst[1]])
            P.pool(I("memset", ap=Sst[0].flat, constant=0.0), writes=[Sst[0]])
            for c in bwd_order:
                P.act(I("copy", out=sb_store.v[:, c, :], in_=Sst[1].flat), reads=[Sst[1]], writes=[sb_store.sub(c * HP, (c + 1) * HP)])
                small(c, 1, 0)
                dstate(c, 1, Sst[1], 0)

            loc = {}

            def local_part(c, p):
                qg, qgr = qg_of(c, p)
                lap, lar = la_of(c)
                loc[c] = (qg, qgr)
                P.pe([I("matmul", out=bG.ap[:, g * 128:(g + 1) * 128], lhsT=kt_chunk(kidx(g), c)[0], rhs=qg[:, g, :], start=True, stop=True)
                      for g in range(NG)], reads=kt_chunk(0, c)[1] + qgr, writes=[bG])
                for d in range(2):
                    P.dve(I("tensor_tensor", out=Gm[p][d].v, in0=bG.ap[:, 0:NG * 128].rearrange("p (g i) -> p g i", g=NG),
                            in1=md[d].unsqueeze(1).to_broadcast([128, NG, 128]), op=ALU.mult),
                          reads=[bG, CONST], writes=[Gm[p][d]])
                small2(c, p)
                chains = [(d, ui, h0, hn) for d in range(2) for ui, (h0, hn) in enumerate(units)]
                bk = lambda d, ui: pbanks[(d * NU + ui) % 4]
                for (d, ui, h0, hn) in chains:
                    P.dve(I("tensor_tensor", out=R[d][ui].v, in0=Ud[d].unsqueeze(1).to_broadcast([128, hn, 128]),
                            in1=lap[:, d, h0:h0 + hn].unsqueeze(2).to_broadcast([128, hn, 128]), op=ALU.mult),
                          reads=[CONST] + lar, writes=[R[d][ui]])
                for (d, ui, h0, hn) in chains:
                    P.pe(I("matmul", out=bk(d, ui).ap[:, 0:hn * 128], lhsT=ones32, rhs=R[d][ui].flat, start=True, stop=True),
                         reads=[CONST, R[d][ui]], writes=[bk(d, ui)])
                for (d, ui, h0, hn) in chains:
                    P.act(I("activation", out=Ebc[p][d].flat[:, h0 * 128:(h0 + hn) * 128], in_=bk(d, ui).ap[:, 0:hn * 128], func=AF.Exp),
                          reads=[bk(d, ui)], writes=[Ebc[p][d].sub(h0 * 128, (h0 + hn) * 128)])
                    P.dve(I("tensor_tensor", out=R[d][ui].v, in0=bk(d, ui).ap[:, 0:hn * 128].rearrange("p (h i) -> p h i", h=hn),
                            in1=pt[p][d].ap[:, h0:h0 + hn].unsqueeze(2).to_broadcast([128, hn, 128]), op=ALU.subtract),
                          reads=[bk(d, ui), pt[p][d]], writes=[R[d][ui]])
                for (d, ui, h0, hn) in chains:
                    P.dve(I("tensor_tensor", out=R[d][ui].v, in0=R[d][ui].v, in1=md[d].unsqueeze(1).to_broadcast([128, hn, 128]), op=ALU.mult),
                          reads=[R[d][ui], CONST], writes=[R[d][ui]])
                for (d, ui, h0, hn) in chains:
                    P.act(I("activation", out=E[p][d].flat[:, h0 * 128:(h0 + hn) * 128], in_=R[d][ui].flat, func=AF.Exp),
                          reads=[R[d][ui]], writes=[E[p][d].sub(h0 * 128, (h0 + hn) * 128)])
                vt, vr = vtok_of(c)
                dtv = dt_of(c)
                for d in range(2):
                    P.dve(I("tensor_tensor", out=g4(QsT[p][d].v), in0=g4(Ebc[p][d].v), in1=qg.unsqueeze(2).to_broadcast([128, NG, hg, 128]), op=ALU.mult),
                          reads=[Ebc[p][d]] + qgr, writes=[QsT[p][d]])
                    if dtv is not None:
                        P.pool(I("tensor_tensor", out=hq(Vdt[p][d].flat), in0=hq(vt), in1=dtv[0][:, d, :].unsqueeze(2).to_broadcast([128, H, 64]), op=ALU.mult),
                               reads=vr + dtv[1], writes=[Vdt[p][d]])
                for d in range(2):
                    P.dve(I("tensor_tensor", out=g4(MT[p][d].v), in0=g4(E[p][d].v), in1=Gm[p][d].v.unsqueeze(2).to_broadcast([128, NG, hg, 128]), op=ALU.mult),
                          reads=[E[p][d], Gm[p][d]], writes=[MT[p][d]])
                if pre is not None:
                    pre(c, p, bG)

            def state_part(c, p):
                vt, vr = vtok_of(c)
                dtv = dt_of(c)
                ins = []
                for h in range(H):
                    out = bY.ap[(h % 2) * 64:(h % 2) * 64 + 64, (h // 2) * 128:(h // 2 + 1) * 128]
                    terms = []
                    for d in range(2):
                        lhs = Vdt[p][d].flat[:, h * 64:(h + 1) * 64] if dtv is not None else vt[:, h * 64:(h + 1) * 64]
                        terms.append((lhs, MT[p][d].v[:, h, :]))
                    terms.append((sfb.flat[:, h * 64:(h + 1) * 64], QsT[p][0].v[:, h, :]))
                    terms.append((sb_store.v[:, c, h * 64:(h + 1) * 64], QsT[p][1].v[:, h, :]))
                    terms += extra_terms(c, h)
                    for i, (lh, rh) in enumerate(terms):
                        ins.append(I("matmul", out=out, lhsT=lh, rhs=rh, start=(i == 0), stop=(i == len(terms) - 1)))
                P.pe(ins, reads=([Vdt[p][0], Vdt[p][1]] if dtv is not None else []) + [MT[p][0], MT[p][1], QsT[p][0], QsT[p][1], sfb, sb_store.sub(c * HP, (c + 1) * HP)] + vr + extra_reads,
                     writes=[bY])
                dstate(c, 0, Sst[0], p)
                P.act(I("copy", out=sfb.flat, in_=Sst[0].flat), reads=[Sst[0]], writes=[sfb])
                finish(c, bY, bF, p)

            order = []
            for c in fwd_order:
                if c in skip_out:
                    small(c, 0, 0)
                    dstate(c, 0, Sst[0], 0)
                else:
                    order.append(c)
            P.act(I("copy", out=sfb.flat, in_=Sst[0].flat), reads=[Sst[0]], writes=[sfb])
            if dbuf:
                local_part(order[0], 0)
                for i, c in enumerate(order):
                    if i + 1 < len(order):
                        local_part(order[i + 1], (i + 1) % 2)
                    state_part(c, i % 2)
            else:
                for c in order:
                    local_part(c, 0)
                    state_part(c, 0)
            arena.release(m)

        def phase_ssd(l, oT_ssd, ctx_out):
            par = par_t[:, l, :]
            m0 = arena.mark()
            BT = arena.alloc((2, T), BF16)
            CT = arena.alloc((2, T), BF16)
            Xtok = arena.alloc((NT, 384), BF16)
            dtv = arena.alloc((NT, 2, 6), F32)
            lav = arena.alloc((NT, 2, 6), F32)
            wz = arena.alloc((KC, 384), BF16)
            load_w(win_d[l][:, CO_Z:CO_Z + 384], wz)
            m1 = arena.mark()
            wdt = arena.alloc((KC, 12), BF16)
            load_w(win_d[l][:, CO_DT:CO_DT + 12], wdt)
            bank = banks[0]
            P.pe([I("matmul", out=bank.ap[:, tt * 12:(tt + 1) * 12], lhsT=hT[:, k, tt * 128:(tt + 1) * 128], rhs=wdt.v[:, k, :],
                    start=(k == 0), stop=(k == KC - 1)) for tt in range(NT) for k in range(KC)],
                 reads=[wdt] + hk(0, T), writes=[bank])
            d3 = lambda b: b.flat.rearrange("p (t c) -> p t c", c=12)
            P.dve(I("tensor_tensor", out=d3(dtv), in0=bank.ap[:, 0:NT * 12].rearrange("p (t c) -> p t c", c=12),
                    in1=par[:, PO_DTB:PO_DTB + 12].unsqueeze(1).to_broadcast([128, NT, 12]), op=ALU.add),
                  reads=[bank, "par"], writes=[dtv])
            aexp = sml_t[:, 0:12]
            P.act(I("activation", out=dtv.flat, in_=dtv.flat, func=AF.Exp), reads=[dtv], writes=[dtv])
            P.act(I("activation", out=dtv.flat, in_=dtv.flat, func=AF.Ln, bias=1.0, scale=1.0), reads=[dtv], writes=[dtv])
            P.act(I("activation", out=aexp, in_=par[:, PO_ALOG:PO_ALOG + 12], func=AF.Exp), reads=["par"], writes=["aexp"])
            P.dve(I("scalar_tensor_tensor", out=d3(lav), in0=d3(dtv), scalar=-1.0, in1=aexp.unsqueeze(1).to_broadcast([128, NT, 12]),
                    op0=ALU.mult, op1=ALU.mult), reads=[dtv, "aexp"], writes=[lav])
            if stop_after == "ssd_a":
                dump(lav.flat, 0, NT * 12, [lav])
                raise _Stop()
            P.dve([I("tensor_scalar", out=dI_t[:, h, :], in0=identb, scalar1=par[:, PO_SD + h:PO_SD + h + 1], scalar2=None, op0=ALU.mult) for h in range(6)],
                  reads=[CONST, "par"], writes=["dI"])
            rawp = arena.alloc(T + 8, F32)
            acc = arena.alloc(T, F32)
            xs = [arena.alloc(T, BF16) for _ in range(2)]
            wx = [arena.alloc((KC, 128), BF16) for _ in range(2)]
            P.pool(I("memset", ap=rawp.flat, constant=0.0), writes=[rawp])
            roff = lambda t0: (2 + t0) if t0 < L else (t0 + 6)
            for cc in range(7):
                w = wx[cc % 2]
                load_w(win_d[l][:, CO_XBC + cc * 128:CO_XBC + (cc + 1) * 128], w)
                for bi, (t0, n) in enumerate(blocks):
                    bank = banks[1 + bi % 2]
                    proj_fm(bank, w, 0, 128, t0, n)
                    P.act(I("copy", out=rawp.flat[:, roff(t0):roff(t0) + n], in_=bank.ap[:, :n]), reads=[bank], writes=[rawp])
                for (s0, sn, ro) in [(0, L, 2), (L, LC, L + 6)]:
                    sa = acc.sub(s0, s0 + sn)
                    P.dve(I("tensor_scalar", out=sa.ap, in0=rawp.flat[:, ro - 2:ro - 2 + sn], scalar1=par[:, PO_CW + cc * 5:PO_CW + cc * 5 + 1],
                            scalar2=par[:, PO_CB + cc:PO_CB + cc + 1], op0=ALU.mult, op1=ALU.add), reads=[rawp, "par"], writes=[sa])
                    for j in range(1, 5):
                        P.dve(I("scalar_tensor_tensor", out=sa.ap, in0=rawp.flat[:, ro - 2 + j:ro - 2 + j + sn],
                                scalar=par[:, PO_CW + cc * 5 + j:PO_CW + cc * 5 + j + 1], in1=sa.ap, op0=ALU.mult, op1=ALU.add),
                              reads=[rawp, "par", sa], writes=[sa])
                if cc < 3:
                    dst = xs[cc % 2]
                    dflat_ = dst.flat
                elif cc < 5:
                    dst = BT.sub((cc - 3) * T, (cc - 2) * T)
                    dflat_ = dst.ap
                else:
                    dst = CT.sub((cc - 5) * T, (cc - 4) * T)
                    dflat_ = dst.ap
                P.act(I("activation", out=dflat_, in_=acc.flat, func=AF.Silu), reads=[acc], writes=[dst])
                if cc < 3:
                    for t8 in range(0, NT, 8):
                        nt8 = min(8, NT - t8)
                        bank = banks[3 + (t8 // 8) % 2]
                        P.pe([I("transpose", out=b16(bank)[:, i * 128:(i + 1) * 128], in_=dflat_[:, (t8 + i) * 128:(t8 + i + 1) * 128], identity=identb)
                              for i in range(nt8)], reads=[dst, CONST], writes=[bank])
                        P.dve(I("tensor_copy", out=Xtok.v[:, t8:t8 + nt8, cc * 128:(cc + 1) * 128],
                                in_=b16(bank)[:, 0:nt8 * 128].rearrange("p (t f) -> p t f", f=128)),
                              reads=[bank], writes=[Xtok])
            if stop_after == "ssd_b":
                dump(Xtok.flat[:, 0:768], 0, 768, [Xtok])
                raise _Stop()
            arena.release(m1)
            sz = arena.alloc(384, F32)
            vv = sz
            sq = arena.alloc(384, BF16)
            rs = arena.alloc(128, F32)

            def pre(c, p, bZ):
                P.pe([I("matmul", out=bZ.ap[:, pc * 128:(pc + 1) * 128], lhsT=wz.v[:, k, pc * 128:(pc + 1) * 128], rhs=hT[:, k, c * 128:(c + 1) * 128],
                        start=(k == 0), stop=(k == KC - 1)) for pc in range(3) for k in range(KC)],
                     reads=[wz] + hk(c * 128, 128), writes=[bZ])
                sigmoid_from(bZ, 384, sz)
                P.dve(I("tensor_tensor", out=sz.flat, in0=bZ.ap[:, 0:384], in1=sz.flat, op=ALU.mult), reads=[bZ, sz], writes=[sz])

            def finish(c, bY, bF, p):
                P.dve(I("tensor_tensor", out=vv.flat, in0=bY.ap[:, 0:384], in1=sz.flat, op=ALU.mult), reads=[bY, sz], writes=[vv])
                P.act(I("activation", out=sq.flat, in_=vv.flat, func=AF.Square), reads=[vv], writes=[sq])
                P.pe([I("matmul", out=bF.ap[:, 0:128], lhsT=ones1b, rhs=sq.flat[:, pc * 128:(pc + 1) * 128], start=(pc == 0), stop=(pc == 2)) for pc in range(3)],
                     reads=[sq, CONST], writes=[bF])
                rstd_from(bF, 128, rs, scale=1.0 / 384.0)
                P.dve([I("scalar_tensor_tensor", out=oT_ssd.v[:, pc, c * 128:(c + 1) * 128], in0=vv.flat[:, pc * 128:(pc + 1) * 128],
                         scalar=par[:, PO_SNG + pc:PO_SNG + pc + 1], in1=rs.flat, op0=ALU.mult, op1=ALU.mult) for pc in range(3)],
                      reads=[vv, rs, "par"], writes=[oT_ssd.sub(pc * T + c * 128, pc * T + (c + 1) * 128) for pc in range(3)])

            scan(H=6, hg=3,
                 la_of=lambda c: (lav.v[:, c], [lav]),
                 dt_of=lambda c: (dtv.v[:, c], [dtv]),
                 kt_chunk=lambda i, c: (BT.v[:, i, c * 128:(c + 1) * 128], [BT]),
                 n_kt=2, kidx=lambda g: g,
                 qg_of=lambda c, p: (CT.v[:, :, c * 128:(c + 1) * 128], [CT]),
                 vtok_of=lambda c: (Xtok.v[:, c, :], [Xtok]),
                 extra_terms=lambda c, h: [(Xtok.v[:, c, h * 64:(h + 1) * 64], dI_t[:, h, :])],
                 extra_reads=["dI"],
                 finish=finish, dbuf=False, shared=sq, skip_out=(() if ctx_out else tuple(ctx_chunks)), pre=pre)
            arena.release(m0)

        def rope_store(src, src_reads, n, t0, dst_ap, dst_buf, ropeb, scr, scale=1.0, dsts=None):
            qb, t1, t2 = scr
            bR = banks[7]
            if scale == 1.0:
                P.act(I("copy", out=qb.flat[:, :n], in_=src), reads=src_reads, writes=[qb])
            else:
                P.act(I("mul", out=qb.flat[:, :n], in_=src, mul=scale), reads=src_reads, writes=[qb])
            P.pe(I("matmul", out=bR.ap[:, :n], lhsT=rotb, rhs=qb.flat[:, :n], start=True, stop=True), reads=[qb, CONST], writes=[bR])
            P.dve(I("tensor_tensor", out=t1.flat[:, :n], in0=qb.flat[:, :n], in1=ropeb.v[:, 0, t0:t0 + n], op=ALU.mult), reads=[qb, ropeb], writes=[t1])
            P.dve(I("tensor_tensor", out=t2.flat[:, :n], in0=bR.ap[:, :n], in1=ropeb.v[:, 1, t0:t0 + n], op=ALU.mult), reads=[bR, ropeb], writes=[t2])
            if dsts is None:
                dsts = [(0, 128, dst_ap, dst_buf)]
            for (p0, p1, d_ap, d_buf) in dsts:
                P.pool(I("tensor_tensor", out=d_ap, in0=t1.flat[p0:p1, :n], in1=t2.flat[p0:p1, :n], op=ALU.add), reads=[t1, t2], writes=[d_buf])

        def load_rope():
            ropeb = arena.alloc((2, T), BF16)
            P.dma("pool", I("dma_start", out=ropeb.v, in_=rope_d.rearrange("p (a t) -> p a t", a=2)), writes=[ropeb])
            return ropeb

        def phase_ret(l, oT_ret, ctx_out):
            par = par_t[:, l, :]
            m0 = arena.mark()
            QTr = arena.alloc((2, T), BF16)
            KTr = arena.alloc((2, T), BF16)
            Vtok = arena.alloc((NT, 256), BF16)
            wg = arena.alloc((KC, 256), BF16)
            load_w(win_d[l][:, CO_GR:CO_GR + 256], wg)
            lar = arena.alloc((2, 4), F32)
            m1 = arena.mark()
            ropeb = load_rope()
            wq = arena.alloc((KC, 256), BF16)
            wk_ = arena.alloc((KC, 256), BF16)
            wv = arena.alloc((KC, 256), BF16)
            load_w(win_d[l][:, CO_QR:CO_QR + 256], wq)
            load_w(win_d[l][:, CO_KR:CO_KR + 256], wk_)
            load_w(win_d[l][:, CO_VR:CO_VR + 256], wv)
            scr = (arena.alloc(512, BF16), arena.alloc(512, F32), arena.alloc(512, F32))
            tl = sml_t[:, 16:24]
            P.act(I("activation", out=tl, in_=par[:, PO_RLOG:PO_RLOG + 8], func=AF.Exp, scale=-1.0), reads=["par"], writes=["tl"])
            P.act(I("activation", out=tl, in_=tl, func=AF.Ln, bias=1.0, scale=1.0), reads=["tl"], writes=["tl"])
            P.dve(I("tensor_scalar", out=lar.flat, in0=tl, scalar1=-1.0, scalar2=None, op0=ALU.mult), reads=["tl"], writes=[lar])
            for (w, dstT, scale) in [(wq, QTr, 1.0), (wk_, KTr, 0.125)]:
                for pc in range(2):
                    for bi, (t0, n) in enumerate(blocks):
                        bank = banks[1 + bi % 2]
                        proj_fm(bank, w, pc * 128, 128, t0, n)
                        rope_store(bank.ap[:, :n], [bank], n, t0, dstT.v[:, pc, t0:t0 + n], dstT.sub(pc * T + t0, pc * T + t0 + n), ropeb, scr, scale)
            for tt in range(0, NT, 2):
                n2 = min(2, NT - tt)
                bank = banks[3 + (tt // 2) % 2]
                P.pe([I("matmul", out=bank.ap[:, i * 256:(i + 1) * 256], lhsT=hT[:, k, (tt + i) * 128:(tt + i + 1) * 128], rhs=wv.v[:, k, :],
                        start=(k == 0), stop=(k == KC - 1)) for i in range(n2) for k in range(KC)],
                     reads=[wv] + hk(tt * 128, n2 * 128), writes=[bank])
                P.act(I("copy", out=Vtok.v[:, tt:tt + n2, :], in_=bank.ap[:, 0:n2 * 256].rearrange("p (t f) -> p t f", f=256)),
                      reads=[bank], writes=[Vtok])
            arena.release(m1)
            Qz = [arena.alloc((4, 128), BF16) for _ in range(2)]
            y32 = arena.alloc(256, F32)
            yb = arena.alloc(256, BF16)
            yc = arena.alloc(256, F32)
            sq = arena.alloc(256, BF16)
            rs = arena.alloc(256, F32)
            sgs = [arena.alloc(256, F32) for _ in range(2)]

            def qg_of(c, p):
                Qz_ = Qz[p]
                P.dve(I("tensor_tensor", out=Qz_.flat.rearrange("p (a b i) -> p a b i", a=2, b=2),
                        in0=QTr.v[:, :, c * 128:(c + 1) * 128].unsqueeze(2).to_broadcast([128, 2, 2, 128]),
                        in1=hmb.unsqueeze(1).unsqueeze(3).to_broadcast([128, 2, 2, 128]), op=ALU.mult),
                      reads=[QTr, CONST], writes=[Qz_])
                return (Qz_.v, [Qz_])

            def pre(c, p, bZ):
                sg = sgs[p]
                P.pe([I("matmul", out=bZ.ap[:, pc * 128:(pc + 1) * 128], lhsT=wg.v[:, k, pc * 128:(pc + 1) * 128], rhs=hT[:, k, c * 128:(c + 1) * 128],
                        start=(k == 0), stop=(k == KC - 1)) for pc in range(2) for k in range(KC)],
                     reads=[wg] + hk(c * 128, 128), writes=[bZ])
                sigmoid_from(bZ, 256, sg)
                P.dve(I("tensor_tensor", out=sg.flat, in0=bZ.ap[:, 0:256], in1=sg.flat, op=ALU.mult), reads=[bZ, sg], writes=[sg])

            def finish(c, bY, bF, p):
                sg = sgs[p]
                P.act(I("copy", out=y32.flat, in_=bY.ap[:, 0:256]), reads=[bY], writes=[y32])
                P.dve(I("tensor_copy", out=yb.flat, in_=y32.flat), reads=[y32], writes=[yb])
                P.pe(I("matmul", out=bF.ap[:, 0:256], lhsT=blk64b, rhs=yb.flat, start=True, stop=True), reads=[yb, CONST], writes=[bF])
                P.dve(I("tensor_tensor", out=yc.flat, in0=y32.flat, in1=bF.ap[:, 0:256], op=ALU.subtract), reads=[y32, bF], writes=[yc])
                P.act(I("activation", out=sq.flat, in_=yc.flat, func=AF.Square), reads=[yc], writes=[sq])
                P.pe(I("matmul", out=bF.ap[:, 0:256], lhsT=blk64b, rhs=sq.flat, start=True, stop=True), reads=[sq, CONST], writes=[bF])
                rstd_from(bF, 256, rs)
                P.dve(I("tensor_tensor", out=yc.flat, in0=yc.flat, in1=rs.flat, op=ALU.mult), reads=[yc, rs], writes=[yc])
                P.act([I("activation", out=yc.flat[:, pc * 128:(pc + 1) * 128], in_=yc.flat[:, pc * 128:(pc + 1) * 128], func=AF.Identity,
                         bias=par[:, PO_RGB + pc:PO_RGB + pc + 1], scale=par[:, PO_RGG + pc:PO_RGG + pc + 1]) for pc in range(2)],
                      reads=[yc, "par"], writes=[yc])
                P.dve(I("tensor_tensor", out=oT_ret.v[:, :, c * 128:(c + 1) * 128], in0=yc.flat.rearrange("p (a i) -> p a i", a=2),
                        in1=sg.flat.rearrange("p (a i) -> p a i", a=2), op=ALU.mult),
                      reads=[yc, sg], writes=[oT_ret.sub(pc * T + c * 128, pc * T + (c + 1) * 128) for pc in range(2)])

            scan(H=4, hg=1,
                 la_of=lambda c: (lar.v, [lar]),
                 dt_of=lambda c: None,
                 kt_chunk=lambda i, c: (KTr.v[:, i, c * 128:(c + 1) * 128], [KTr]),
                 n_kt=2, kidx=lambda g: g // 2,
                 qg_of=qg_of,
                 vtok_of=lambda c: (Vtok.v[:, c, :], [Vtok]),
                 extra_terms=lambda c, h: [],
                 extra_reads=[],
                 finish=finish, dbuf=True, skip_out=(() if ctx_out else tuple(ctx_chunks)), pre=pre)
            arena.release(m0)

        def phase_att(l, oT_att, ctx_out):
            par = par_t[:, l, :]
            m0 = arena.mark()
            ropeb = load_rope()
            kTd = arena.alloc((2, T), BF16)
            Vext = arena.alloc((NT, 2, 128), BF16)
            qz = [arena.alloc(T, BF16) for _ in range(2)]
            mA = arena.mark()
            scr = (arena.alloc(512, BF16), arena.alloc(512, F32), arena.alloc(512, F32))
            sqb = arena.alloc(512, BF16)
            rs = arena.alloc(512, F32)
            qn = arena.alloc(512, F32)
            mB = arena.mark()
            gq8 = sml_t[:, 32:33]
            P.dve(I("tensor_scalar", out=gq8, in0=par[:, PO_QG:PO_QG + 1], scalar1=0.125, scalar2=None, op0=ALU.mult), reads=["par"], writes=["gq8"])
            bN = banks[6]

            def normrope(bank, n, t0, gcol, greads, dsts):
                P.act(I("activation", out=sqb.flat[:, :n], in_=bank.ap[:, :n], func=AF.Square), reads=[bank], writes=[sqb])
                P.pe(I("matmul", out=bN.ap[:, :n], lhsT=blk64b, rhs=sqb.flat[:, :n], start=True, stop=True), reads=[sqb, CONST], writes=[bN])
                rstd_from(bN, n, rs)
                P.dve(I("scalar_tensor_tensor", out=qn.flat[:, :n], in0=bank.ap[:, :n], scalar=gcol, in1=rs.flat[:, :n], op0=ALU.mult, op1=ALU.mult),
                      reads=[bank, rs] + greads, writes=[qn])
                rope_store(qn.flat[:, :n], [qn], n, t0, None, None, ropeb, scr, dsts=dsts)

            wkd = arena.alloc((2, KC, 128), BF16)
            wv = arena.alloc((KC, 128), BF16)
            for g in range(2):
                for hh in range(2):
                    P.dma("pool", I("dma_start", out=wkd.v[:, g, :, hh * 64:(hh + 1) * 64],
                                    in_=win_d[l][:, CO_KA + g * 64:CO_KA + (g + 1) * 64].rearrange("(k p) n -> p k n", p=128)), writes=[wkd])
            load_w(win_d[l][:, CO_VA:CO_VA + 128], wv)
            P.pool(I("memset", ap=Vext.v[:, :, :, 64:65], constant=1.0), writes=[Vext])
            P.pool(I("memset", ap=qz[0].flat[64:128, :], constant=0.0), writes=[qz[0]])
            P.pool(I("memset", ap=qz[1].flat[0:64, :], constant=0.0), writes=[qz[1]])
            for g in range(2):
                for bi, (t0, n) in enumerate(blocks):
                    bank = banks[bi % 2]
                    P.pe([I("matmul", out=bank.ap[:, :n], lhsT=wkd.v[:, g, k, :], rhs=hT[:, k, t0:t0 + n], start=(k == 0), stop=(k == KC - 1))
                          for k in range(KC)], reads=[wkd] + hk(t0, n), writes=[bank])
                    normrope(bank, n, t0, par[:, PO_KG:PO_KG + 1], ["par"], [(0, 128, kTd.v[:, g, t0:t0 + n], kTd.sub(g * T + t0, g * T + t0 + n))])
            for tt in range(0, NT, 4):
                n4 = min(4, NT - tt)
                bank = banks[2 + (tt // 4) % 2]
                P.pe([I("matmul", out=bank.ap[:, i * 128:(i + 1) * 128], lhsT=hT[:, k, (tt + i) * 128:(tt + i + 1) * 128], rhs=wv.v[:, k, :],
                        start=(k == 0), stop=(k == KC - 1)) for i in range(n4) for k in range(KC)],
                     reads=[wv] + hk(tt * 128, n4 * 128), writes=[bank])
                bv = bank.ap[:, 0:n4 * 128].rearrange("p (t g d) -> p t g d", g=2, d=64)
                P.act(I("copy", out=Vext.v[:, tt:tt + n4, :, 0:64], in_=bv), reads=[bank], writes=[Vext])
                P.dve(I("tensor_copy", out=Vext.v[:, tt:tt + n4, :, 65:128], in_=bv[:, :, :, 0:63]), reads=[bank], writes=[Vext])
            arena.release(mB)
            wq = arena.alloc((KC, 128), BF16)
            end_off = arena.off
            arena.off = mA
            PT = [arena.alloc(512, BF16) for _ in range(4)]
            rr = arena.alloc(512, F32)
            bcs = arena.alloc(512, F32)
            tn = arena.alloc(512, BF16)
            assert arena.off <= mB
            arena.off = end_off
            pti = 0
            sbanks = [banks[0], banks[1], banks[2], banks[5]]
            obanks = [banks[3], banks[4]]
            bBc = banks[7]
            si = 0
            oi = 0
            for qc in range(3):
                w = wq
                load_w(win_d[l][:, CO_QA + qc * 128:CO_QA + (qc + 1) * 128], w)
                for bi, (t0, n) in enumerate(blocks):
                    bank = banks[bi % 2]
                    proj_fm(bank, w, 0, 128, t0, n)
                    normrope(bank, n, t0, gq8, ["gq8"], [(0, 64, qz[0].flat[0:64, t0:t0 + n], qz[0].sub(t0, t0 + n)),
                                                         (64, 128, qz[1].flat[64:128, t0:t0 + n], qz[1].sub(t0, t0 + n))])
                its = []
                for (t0, n) in blocks:
                    is_ctx = t0 >= L
                    if is_ctx and not ctx_out:
                        continue
                    kcs = ctx_chunks if is_ctx else list(range(NT))
                    for hh in range(2):
                        bO = obanks[oi % 2]
                        oi += 1
                        for ki, kc in enumerate(kcs):
                            its.append(dict(t0=t0, n=n, hh=hh, kc=kc, first=(ki == 0), last=(ki == len(kcs) - 1), bO=bO,
                                            bS=sbanks[si % 4], pt=PT[pti % 4]))
                            si += 1
                            pti += 1

                def emit_S(it):
                    t0, n, hh, kc, bSx, pt_ = it["t0"], it["n"], it["hh"], it["kc"], it["bS"], it["pt"]
                    g = (2 * qc + hh) // 3
                    P.pe(I("matmul", out=bSx.ap[:, :n], lhsT=kTd.v[:, g, kc * 128:(kc + 1) * 128], rhs=qz[hh].flat[:, t0:t0 + n], start=True, stop=True),
                         reads=[kTd.sub(g * T + kc * 128, g * T + (kc + 1) * 128), qz[hh].sub(t0, t0 + n)], writes=[bSx])
                    P.act(I("activation", out=pt_.flat[:, :n], in_=bSx.ap[:, :n], func=AF.Exp), reads=[bSx], writes=[pt_])

                def emit_PV(it):
                    t0, n, hh, kc, bO, pt_ = it["t0"], it["n"], it["hh"], it["kc"], it["bO"], it["pt"]
                    first, last = it["first"], it["last"]
                    g = (2 * qc + hh) // 3
                    P.pe(I("matmul", out=bO.ap[:, :n], lhsT=Vext.v[:, kc, g, :], rhs=pt_.flat[:, :n], start=first, stop=last),
                         reads=[Vext, pt_], writes=[bO])
                    if not last:
                        return
                    P.dve(I("reciprocal", out=rr.flat[64:65, :n], in_=bO.ap[64:65, :n]), reads=[bO], writes=[rr])
                    P.pe(I("matmul", out=bBc.ap[0:64, :n], lhsT=ones32[64:65, 0:64], rhs=rr.flat[64:65, :n], start=True, stop=True),
                         reads=[rr, CONST], writes=[bBc])
                    P.act(I("copy", out=bcs.flat[0:64, :n], in_=bBc.ap[0:64, :n]), reads=[bBc], writes=[bcs])
                    dst = oT_att.sub(qc * T + t0, qc * T + t0 + n)
                    if hh == 0:
                        P.dve(I("tensor_tensor", out=oT_att.v[0:64, qc, t0:t0 + n], in0=bO.ap[0:64, :n], in1=bcs.flat[0:64, :n], op=ALU.mult),
                              reads=[bO, bcs], writes=[dst])
                    else:
                        P.dve(I("tensor_tensor", out=tn.flat[0:64, :n], in0=bO.ap[0:64, :n], in1=bcs.flat[0:64, :n], op=ALU.mult),
                              reads=[bO, bcs], writes=[tn])
                        P.pe(I("matmul", out=bBc.ap[64:128, :n], lhsT=identb[0:64, 0:64], rhs=tn.flat[0:64, :n], start=True, stop=True),
                             reads=[tn, CONST], writes=[bBc])
                        P.act(I("copy", out=oT_att.v[64:128, qc, t0:t0 + n], in_=bBc.ap[64:128, :n]), reads=[bBc], writes=[dst])

                LOOK = 2
                for i in range(len(its) + LOOK):
                    if i < len(its):
                        emit_S(its[i])
                    if i >= LOOK:
                        emit_PV(its[i - LOOK])
            arena.release(m0)

        def phase_out(l, oT_parts, ctx_out):
            m0 = arena.mark()
            wo = arena.alloc((KC, D), BF16)
            load_w(wout_d[l], wo)
            yv = arena.alloc((KC, 512), F32)
            sqs = [arena.alloc(512, BF16) for _ in range(2)]
            rs = arena.alloc(512, F32)
            tmps = [arena.alloc(512, F32) for _ in range(2)]
            bMS = banks[6]
            pieces = []
            for (ob, nch) in oT_parts:
                for i in range(nch):
                    pieces.append((ob, i))
            for (t0, n) in blocks:
                if t0 >= L and not ctx_out:
                    continue
                col = 0 if t0 < L else 1
                for dc in range(KC):
                    bank = banks[dc % 4]
                    P.pe([I("matmul", out=bank.ap[:, :n], lhsT=wo.v[:, k, dc * 128:(dc + 1) * 128], rhs=ob.v[:, i, t0:t0 + n], start=(k == 0), stop=(k == KC - 1))
                          for k, (ob, i) in enumerate(pieces)],
                         reads=[wo] + [ob.sub(i * T + t0, i * T + t0 + n) for (ob, i) in pieces], writes=[bank])
                    P.act(I("copy", out=yv.v[:, dc, :n], in_=bank.ap[:, :n]), reads=[bank], writes=[yv.sub(dc * 512, dc * 512 + 512)])
                ms_block(lambda k: yv.v[:, k, :n], n, bMS, sqs, [yv])
                rstd_from(bMS, n, rs)
                for k in range(KC):
                    tmp = tmps[k % 2]
                    P.dve(I("scalar_tensor_tensor", out=tmp.flat[:, :n], in0=yv.v[:, k, :n], scalar=ab_t[:, 2, k, col:col + 1],
                            in1=rs.flat[:, :n], op0=ALU.mult, op1=ALU.mult),
                          reads=[yv.sub(k * 512, k * 512 + 512), rs, "ab"], writes=[tmp])
                    P.pool(I("tensor_tensor", out=xT[:, k, t0:t0 + n], in0=xT[:, k, t0:t0 + n], in1=tmp.flat[:, :n], op=ALU.add),
                           reads=[tmp] + xkk(t0, n), writes=xkk(t0, n))
            arena.release(m0)

        def phase_ffn(l, ctx_out):
            m0 = arena.mark()
            Tf = T if ctx_out else L
            groups = []
            t = 0
            while t < Tf:
                gn = min(768, Tf - t)
                groups.append((t, gn))
                t += gn
            yacc = arena.alloc((KC, 768), F32)
            w1 = [arena.alloc((KC, 512), BF16) for _ in range(2)]
            w2 = [arena.alloc((4, D), BF16) for _ in range(2)]
            uT = [arena.alloc((4, 512), BF16) for _ in range(2)]
            rl = [arena.alloc(512, BF16) for _ in range(2)]
            sqs = [arena.alloc(512, BF16) for _ in range(2)]
            rs = arena.alloc(512, F32)
            tmps = [arena.alloc(512, F32) for _ in range(2)]
            bMS = banks[7]
            ui = 0
            wi = 0
            for (g0, gn) in groups:
                subs = []
                t = g0
                while t < g0 + gn:
                    n = min(512, g0 + gn - t)
                    subs.append((t, n))
                    t += n
                its = []
                for fg in range(8):
                    for (t0, n) in subs:
                        its.append(dict(fg=fg, t0=t0, n=n, first_of_fg=((t0, n) == subs[0])))

                def u_phase(it):
                    nonlocal ui, wi
                    fg, t0, n = it["fg"], it["t0"], it["n"]
                    if it["first_of_fg"]:
                        it["wa"], it["wb"] = w1[wi % 2], w2[wi % 2]
                        wi += 1
                        load_w(w1_d[l][:, fg * 512:(fg + 1) * 512], it["wa"])
                        load_w(w2_d[l][fg * 512:(fg + 1) * 512, :], it["wb"])
                        cur["wa"], cur["wb"] = it["wa"], it["wb"]
                    else:
                        it["wa"], it["wb"] = cur["wa"], cur["wb"]
                    u = uT[ui % 2]
                    ui += 1
                    it["u"] = u
                    for fc in range(4):
                        bank = banks[fc % 2]
                        proj_fm(bank, it["wa"], fc * 128, 128, t0, n)
                        uk = u.sub(fc * 512, fc * 512 + 512)
                        r_ = rl[fc % 2]
                        P.act(I("activation", out=r_.flat[:, :n], in_=bank.ap[:, :n], func=AF.Relu), reads=[bank], writes=[r_])
                        if fc % 2 == 0:
                            P.dve(I("tensor_tensor", out=u.v[:, fc, :n], in0=r_.flat[:, :n], in1=r_.flat[:, :n], op=ALU.mult), reads=[r_], writes=[uk])
                        else:
                            P.act(I("activation", out=u.v[:, fc, :n], in_=r_.flat[:, :n], func=AF.Square), reads=[r_], writes=[uk])

                def y_phase(it):
                    fg, t0, n, u, wb = it["fg"], it["t0"], it["n"], it["u"], it["wb"]
                    for dc in range(KC):
                        bank = banks[2 + dc % 4]
                        P.pe([I("matmul", out=bank.ap[:, :n], lhsT=wb.v[:, fc, dc * 128:(dc + 1) * 128], rhs=u.v[:, fc, :n], start=(fc == 0), stop=(fc == 3))
                              for fc in range(4)], reads=[wb, u], writes=[bank])
                        ya = yacc.v[:, dc, t0 - g0:t0 - g0 + n]
                        yk = yacc.sub(dc * 768 + t0 - g0, dc * 768 + t0 - g0 + n)
                        if fg == 0:
                            P.act(I("copy", out=ya, in_=bank.ap[:, :n]), reads=[bank], writes=[yk])
                        else:
                            P.dve(I("tensor_tensor", out=ya, in0=ya, in1=bank.ap[:, :n], op=ALU.add), reads=[bank, yk], writes=[yk])

                cur = {}
                for i in range(len(its) + 1):
                    if i < len(its):
                        u_phase(its[i])
                    if i >= 1:
                        y_phase(its[i - 1])
                for (t0, n) in subs:
                    col = 0 if t0 < L else 1
                    o = t0 - g0
                    ms_block(lambda k: yacc.v[:, k, o:o + n], n, bMS, sqs, [yacc])
                    rstd_from(bMS, n, rs)
                    for k in range(KC):
                        tmp = tmps[k % 2]
                        P.dve(I("scalar_tensor_tensor", out=tmp.flat[:, :n], in0=yacc.v[:, k, o:o + n], scalar=ab_t[:, 5, k, col:col + 1],
                                in1=rs.flat[:, :n], op0=ALU.mult, op1=ALU.mult),
                              reads=[yacc, rs, "ab"], writes=[tmp])
                        P.pool(I("tensor_tensor", out=xT[:, k, t0:t0 + n], in0=xT[:, k, t0:t0 + n], in1=tmp.flat[:, :n], op=ALU.add),
                               reads=[tmp] + xkk(t0, n), writes=xkk(t0, n))
            arena.release(m0)

        def phase_store():
            m = arena.mark()
            stg = [arena.alloc(D, F32) for _ in range(2)]
            outs = []
            for tt in range(NTL):
                s = stg[tt % 2]
                for half in range(2):
                    bank = banks[(tt % 2) * 2 + half]
                    P.pe([I("transpose", out=bank.ap[:, kk * 128:(kk + 1) * 128], in_=xT[:, half * 4 + kk, tt * 128:(tt + 1) * 128], identity=ident32)
                          for kk in range(4)], reads=xkk(tt * 128, 128) + [CONST], writes=[bank])
                    if half == 0:
                        P.act(I("copy", out=s.flat[:, 0:512], in_=bank.ap[:, :]), reads=[bank], writes=[s.sub(0, 512)])
                    else:
                        P.dve(I("tensor_copy", out=s.flat[:, 512:1024], in_=bank.ap[:, :]), reads=[bank], writes=[s.sub(512, 1024)])
                P.dma("sp", I("dma_start", out=out_d[tt * 128:(tt + 1) * 128, :], in_=s.flat), reads=[s], writes=[("out", tt)])
                outs.append(("out", tt))
            P.op("sp", [], reads=outs)
            arena.release(m)

        def assemble():
            phase_load()
            if stop_after == "load":
                dump(xT[:, 0, :], 0, T, xkk(0, T))
                return
            for l in range(depth):
                ctx_out = l < depth - 1
                phase_mod(l)
                if stop_after == "mod":
                    dump(ab_t[:].rearrange("p a k c -> p (a k c)"), 0, 96, ["ab"])
                    return
                phase_norm_h(0, 1)
                if stop_after == "norm1":
                    dump(hT[:, 0, :], 0, T, hk(0, T))
                    return
                base = arena.mark()
                oT_ssd = arena.alloc((3, T), BF16)
                oT_ret = arena.alloc((2, T), BF16)
                oT_att = arena.alloc((3, T), BF16)
                arena.off = oT_ssd.off + oT_ssd.n * 2
                phase_ssd(l, oT_ssd, ctx_out)
                if stop_after == "ssd":
                    dump(oT_ssd.flat, 0, 3 * T, [oT_ssd])
                    return
                arena.off = oT_ret.off + oT_ret.n * 2
                phase_ret(l, oT_ret, ctx_out)
                if stop_after == "ret":
                    dump(oT_ret.flat, 0, 2 * T, [oT_ret])
                    return
                arena.off = oT_att.off + oT_att.n * 2
                phase_att(l, oT_att, ctx_out)
                if stop_after == "att":
                    dump(oT_att.flat, 0, 3 * T, [oT_att])
                    return
                phase_out(l, [(oT_att, 3), (oT_ret, 2), (oT_ssd, 3)], ctx_out)
                arena.release(base)
                if stop_after == "out":
                    dump(xT[:, 0, 0:128], 0, 128, xkk(0, 128))
                    return
                phase_norm_h(3, 4)
                phase_ffn(l, ctx_out)
                if stop_after == "layer":
                    dump(xT[:, 0, 0:128], 0, 128, xkk(0, 128))
                    return
            phase_store()

        try:
            assemble()
        except _Stop:
            pass
        P.emit(st)
    build.arena_hi = arena.hi
    build.n_ops = len(P.ops)
    build.stats = P.stats
    return nc


def make_in_maps(inp, L, LC, depth, nb):
    con = host_consts()
    rope = host_rope(L, LC)
    par = host_params(inp, depth)
    f32 = lambda a: np.ascontiguousarray(np.asarray(a, dtype=np.float32))
    shared = {
        "par": par, "con": con, "rope": rope,
        "w_mod": f32(inp["w_mod"]), "w_in": f32(inp["w_in"]), "w_out": f32(inp["w_out"]),
        "w_ff1": f32(inp["w_ff1"]), "w_ff2": f32(inp["w_ff2"]),
    }
    maps = []
    for b in range(nb):
        cv = np.zeros((128, KC, 2), np.float32)
        cv[:, :, 0] = np.asarray(inp["c"][b]).reshape(KC, 128).T
        cv[:, :, 1] = np.asarray(inp["c_ctx"]).reshape(KC, 128).T
        m = dict(shared)
        m["x"] = f32(inp["x"][b])
        m["ctx"] = f32(inp["ctx"][b])
        m["cv"] = cv.reshape(128, 16)
        maps.append(m)
    return maps


def kernel(**inputs):
    inp = {k: np.asarray(v) for k, v in inputs.items()}
    B, L, _ = inp["x"].shape
    LC = inp["ctx"].shape[1]
    depth = inp["w_mod"].shape[0]
    nc = build(L, LC, depth)
    maps = make_in_maps(inp, L, LC, depth, B)
    res = run_bass_kernel_spmd(nc, maps, core_ids=list(range(B)))
    return np.stack([np.asarray(r["out"], dtype=np.float32) for r in res.results], axis=0)
```

```python
import numpy as np
from contextlib import ExitStack
import concourse.bass as bass
import concourse.mybir as mybir
from concourse.bass_utils import run_bass_kernel_spmd

F32 = mybir.dt.float32
BF16 = mybir.dt.bfloat16
AF = mybir.ActivationFunctionType
ALU = mybir.AluOpType

D = 1024
KC = 8
D_IN = 2956
D_FF = 4096
EPS = 1e-6
GRID_W = 64
ROPE_THETA = 10000.0

ENGINES = ("pe", "act", "dve", "pool", "sp")
SEM_CHUNK = 1000
DMA_POOL = 8


class Buf:
    def __init__(self, ap, keys):
        self.ap = ap
        self.keys = tuple(keys)


def _keys(items):
    out = []
    for it in items:
        if isinstance(it, Buf):
            out.extend(it.keys)
        elif isinstance(it, (list,)):
            out.extend(_keys(it))
        else:
            out.append(it)
    return out


class Op:
    __slots__ = ("eng", "fn", "dma", "deps", "has_dep", "ms", "dma_idx", "idx")


class Prog:
    def __init__(self, nc):
        self.nc = nc
        self.ops = []
        self.last_writer = {}
        self.readers = {}

    def op(self, eng, fn, reads=(), writes=(), dma=False):
        if isinstance(fn, tuple):
            fn = [fn]
        o = Op()
        o.eng, o.fn, o.dma = eng, fn, dma
        o.has_dep, o.ms, o.dma_idx = False, None, None
        o.idx = len(self.ops)
        rk = _keys(reads)
        wk = _keys(writes)
        pr = [k for k in rk if isinstance(k, tuple) and k and k[0] == "ps"]
        if pr:
            rk = [k for k in rk if not (isinstance(k, tuple) and k and k[0] == "ps")]
            wk = wk + pr
        deps = set()
        lw, rd = self.last_writer, self.readers
        for r in rk:
            w = lw.get(r)
            if w is not None:
                deps.add(w)
        for r in wk:
            w = lw.get(r)
            if w is not None:
                deps.add(w)
            x = rd.get(r)
            if x:
                deps.update(x)
        for r in rk:
            rd.setdefault(r, []).append(o.idx)
        for r in wk:
            lw[r] = o.idx
            rd[r] = []
        deps.discard(o.idx)
        if eng == "pe":
            deps = {d for d in deps if self.ops[d].eng != "pe"}
        o.deps = deps
        self.ops.append(o)
        return o

    def pe(self, fn, reads=(), writes=()):
        return self.op("pe", fn, reads, writes)

    def act(self, fn, reads=(), writes=()):
        return self.op("act", fn, reads, writes)

    def dve(self, fn, reads=(), writes=()):
        return self.op("dve", fn, reads, writes)

    def pool(self, fn, reads=(), writes=()):
        return self.op("pool", fn, reads, writes)

    def dma(self, eng, fn, reads=(), writes=()):
        return self.op(eng, fn, reads, writes, dma=True)

    def emit(self, stack):
        nc = self.nc
        ops = self.ops
        for o in ops:
            for d in o.deps:
                ops[d].has_dep = True
        cnt = {e: 0 for e in ENGINES}
        dcnt = {e: 0 for e in ENGINES}
        for o in ops:
            if o.dma:
                o.dma_idx = dcnt[o.eng]
                dcnt[o.eng] += 1
            elif o.has_dep:
                cnt[o.eng] += 1
                o.ms = cnt[o.eng]
        self.stats = dict(ms=dict(cnt), dma=dict(dcnt), n_ops={e: sum(1 for o in ops if o.eng == e) for e in ENGINES})
        esems, dsems = {}, {}
        for e in ENGINES:
            n = (cnt[e] + SEM_CHUNK - 1) // SEM_CHUNK
            esems[e] = [stack.enter_context(nc.semaphore(f"c_{e}_{i}")) for i in range(n)]
            n = min(DMA_POOL, dcnt[e])
            dsems[e] = [stack.enter_context(nc.semaphore(f"d_{e}_{i}")) for i in range(n)]

        def sem_of(o):
            if o.dma:
                return dsems[o.eng][o.dma_idx % DMA_POOL], 16 * (o.dma_idx // DMA_POOL + 1)
            m = o.ms - 1
            return esems[o.eng][m // SEM_CHUNK], (m % SEM_CHUNK) + 1

        by_eng = {e: [o for o in ops if o.eng == e] for e in ENGINES}
        block = stack.enter_context(nc.Block())

        def run(e, eng):
            waited = {}
            for o in by_eng[e]:
                need = {}
                for d in o.deps:
                    s, v = sem_of(ops[d])
                    k = id(s)
                    if need.get(k, (None, 0))[1] < v:
                        need[k] = (s, v)
                if o.dma and o.dma_idx >= DMA_POOL:
                    s = dsems[e][o.dma_idx % DMA_POOL]
                    v = 16 * (o.dma_idx // DMA_POOL)
                    k = id(s)
                    if need.get(k, (None, 0))[1] < v:
                        need[k] = (s, v)
                for k, (s, v) in need.items():
                    if waited.get(k, 0) < v:
                        eng.wait_ge(s, v)
                        waited[k] = v
                inst = None
                for (mname, kw) in o.fn:
                    inst = getattr(eng, mname)(**kw)
                if inst is None:
                    continue
                if o.dma:
                    inst.then_inc(sem_of(o)[0], 16)
                elif o.ms is not None:
                    inst.then_inc(sem_of(o)[0], 1)

        if by_eng["pe"]:
            block.tensor(lambda eng: run("pe", eng))
        if by_eng["act"]:
            block.scalar(lambda eng: run("act", eng))
        if by_eng["dve"]:
            block.vector(lambda eng: run("dve", eng))
        if by_eng["pool"]:
            block.gpsimd(lambda eng: run("pool", eng))
        if by_eng["sp"]:
            block.sync(lambda eng: run("sp", eng))


SLOT = 512


class ABuf(Buf):
    def __init__(self, flat, off, esz, shape):
        self.flat = flat
        self.off = off
        self.esz = esz
        self.shape = tuple(shape)
        n = int(np.prod(shape))
        self.n = n
        keys = [("A", s) for s in range(off // SLOT, (off + n * esz - 1) // SLOT + 1)]
        if len(shape) == 1:
            v = flat
        elif len(shape) == 2:
            v = flat.rearrange("p (a b) -> p a b", a=shape[0])
        elif len(shape) == 3:
            v = flat.rearrange("p (a b c) -> p a b c", a=shape[0], b=shape[1])
        elif len(shape) == 4:
            v = flat.rearrange("p (a b c d) -> p a b c d", a=shape[0], b=shape[1], c=shape[2])
        else:
            raise ValueError(shape)
        self.v = v
        Buf.__init__(self, v, keys)

    def sub(self, lo, hi):
        b0 = self.off + lo * self.esz
        b1 = self.off + hi * self.esz
        keys = [("A", s) for s in range(b0 // SLOT, (b1 - 1) // SLOT + 1)]
        return Buf(self.flat[:, lo:hi], keys)


class Arena:
    def __init__(self, nc, st, nbytes):
        self.nbytes = nbytes
        self.t = st.enter_context(nc.sbuf_tensor("arena", [128, nbytes // 2], BF16))
        self.off = 0
        self.hi = 0

    def alloc(self, shape, dt):
        if isinstance(shape, int):
            shape = (shape,)
        esz = 4 if dt == F32 else 2
        n = int(np.prod(shape))
        nb = n * esz
        off = (self.off + 63) // 64 * 64
        assert off + nb <= self.nbytes, f"arena overflow: need {off + nb} of {self.nbytes}"
        self.off = off + nb
        self.hi = max(self.hi, self.off)
        flat = self.t[:, off // 2:(off + nb) // 2]
        if dt == F32:
            flat = flat.bitcast(F32)
        return ABuf(flat, off, esz, shape)

    def mark(self):
        return self.off

    def release(self, m):
        self.off = m


C_ID, C_UF, C_UB, C_ONE, C_BLK, C_ROT, C_OND, C_MF, C_MB, C_HM = range(10)
NCON = 10 * 128


def host_consts():
    c = np.zeros((128, NCON), np.float32)
    i = np.arange(128)
    c[:, C_ID * 128:(C_ID + 1) * 128] = np.eye(128)
    c[:, C_UF * 128:(C_UF + 1) * 128] = (i[:, None] <= i[None, :])
    c[:, C_UB * 128:(C_UB + 1) * 128] = (i[:, None] >= i[None, :])
    c[:, C_ONE * 128:(C_ONE + 1) * 128] = 1.0
    blk = (i[:, None] // 64 == i[None, :] // 64).astype(np.float32) / 64.0
    c[:, C_BLK * 128:(C_BLK + 1) * 128] = blk
    rot = np.zeros((128, 128), np.float32)
    for d in range(128):
        if d % 32 < 16:
            rot[d + 16, d] = -1.0
        else:
            rot[d - 16, d] = 1.0
    c[:, C_ROT * 128:(C_ROT + 1) * 128] = rot
    c[:, C_OND * 128:(C_OND + 1) * 128] = 1.0 / 1024.0
    c[:, C_MF * 128:(C_MF + 1) * 128] = (i[None, :] >= i[:, None])
    c[:, C_MB * 128:(C_MB + 1) * 128] = (i[None, :] < i[:, None])
    hm = np.zeros((128, 128), np.float32)
    hm[:64, 0] = 1.0
    hm[64:, 1] = 1.0
    c[:, C_HM * 128:(C_HM + 1) * 128] = hm
    return c


def host_rope(L, LC):
    T = L + LC
    rows = L // GRID_W
    row = np.broadcast_to(np.arange(rows)[:, None], (rows, GRID_W)).reshape(L)
    col = np.broadcast_to(np.arange(GRID_W)[None, :], (rows, GRID_W)).reshape(L)
    half = 32
    inv_freq = (ROPE_THETA ** (-np.arange(0, half, 2, dtype=np.float32) / half)).astype(np.float32)
    ang = np.stack([row, col], axis=-1).astype(np.float32)[:, :, None] * inv_freq
    ang = np.concatenate([ang, ang], axis=-1).reshape(L, 64)
    cos = np.ones((64, T), np.float32)
    sin = np.zeros((64, T), np.float32)
    cos[:, :L] = np.cos(ang).T
    sin[:, :L] = np.sin(ang).T
    tab = np.zeros((128, 2, T), np.float32)
    tab[:64, 0] = cos
    tab[64:, 0] = cos
    tab[:64, 1] = sin
    tab[64:, 1] = sin
    return tab.reshape(128, 2 * T)


PO_BMOD = 0
PO_NG = PO_BMOD + 48
PO_QG = PO_NG + 32
PO_KG = PO_QG + 1
PO_RGG = PO_KG + 1
PO_RGB = PO_RGG + 2
PO_RLOG = PO_RGB + 2
PO_CW = PO_RLOG + 8
PO_CB = PO_CW + 35
PO_DTB = PO_CB + 7
PO_ALOG = PO_DTB + 12
PO_SD = PO_ALOG + 12
PO_SNG = PO_SD + 6
NPAR = PO_SNG + 3


def host_params(inp, depth):
    par = np.zeros((128, depth, NPAR), np.float32)
    fm = lambda v: np.ascontiguousarray(v.reshape(-1, 128).T)
    for l in range(depth):
        p = par[:, l]
        p[:, PO_BMOD:PO_BMOD + 48] = fm(inp["b_mod"][l])
        p[:, PO_NG:PO_NG + 32] = fm(inp["norm_g"][l].reshape(-1))
        p[:, PO_QG] = np.tile(inp["q_norm_g"][l], 2)
        p[:, PO_KG] = np.tile(inp["k_norm_g"][l], 2)
        p[:, PO_RGG:PO_RGG + 2] = fm(inp["ret_gn_g"][l])
        p[:, PO_RGB:PO_RGB + 2] = fm(inp["ret_gn_b"][l])
        p[:, PO_RLOG:PO_RLOG + 8] = np.broadcast_to(inp["ret_decay_logit"][l].reshape(1, 8), (128, 8))
        cw = inp["ssd_conv_w"][l]
        for cc in range(7):
            p[:, PO_CW + cc * 5:PO_CW + cc * 5 + 5] = cw[:, cc * 128:(cc + 1) * 128].T
        p[:, PO_CB:PO_CB + 7] = fm(inp["ssd_conv_b"][l])
        p[:, PO_DTB:PO_DTB + 12] = np.broadcast_to(inp["ssd_dt_bias"][l].reshape(1, 12), (128, 12))
        p[:, PO_ALOG:PO_ALOG + 12] = np.broadcast_to(inp["ssd_a_log"][l].reshape(1, 12), (128, 12))
        p[:, PO_SD:PO_SD + 6] = np.broadcast_to(inp["ssd_d"][l].reshape(1, 6), (128, 6))
        p[:, PO_SNG:PO_SNG + 3] = fm(inp["ssd_norm_g"][l])
    return par.reshape(128, depth * NPAR)


CO_QA, CO_KA, CO_VA = 0, 384, 512
CO_QR, CO_KR, CO_VR, CO_GR = 640, 896, 1152, 1408
CO_Z, CO_XBC, CO_DT = 1664, 2048, 2944


def I(m, **kw):
    return (m, kw)


class _Stop(Exception):
    pass


def build(L, LC, depth, stop_after=None, dbg=None):
    T = L + LC
    NT = T // 128
    NTL = L // 128
    blocks = [(i * 512, 512) for i in range(L // 512)] + [(L, LC)]
    lat_chunks = list(range(NTL))
    ctx_chunks = list(range(NTL, NT))
    fwd_order = ctx_chunks + lat_chunks
    bwd_order = ctx_chunks[::-1] + lat_chunks[::-1]

    nc = bass.Bass("TRN2", target_bir_lowering=False)
    dt_in = lambda name, shape: nc.dram_tensor(name, shape, F32, kind="ExternalInput").ap()
    x_d = dt_in("x", [L, D])
    ctx_d = dt_in("ctx", [LC, D])
    cv_d = dt_in("cv", [128, 16])
    par_d = dt_in("par", [128, depth * NPAR])
    con_d = dt_in("con", [128, NCON])
    rope_d = dt_in("rope", [128, 2 * T])
    wmod_d = dt_in("w_mod", [depth, D, 6 * D])
    win_d = dt_in("w_in", [depth, D, D_IN])
    wout_d = dt_in("w_out", [depth, D, D])
    w1_d = dt_in("w_ff1", [depth, D, D_FF])
    w2_d = dt_in("w_ff2", [depth, D_FF, D])
    out_d = nc.dram_tensor("out", [L, D], F32, kind="ExternalOutput").ap()
    dbg_d = None
    if dbg is not None:
        dbg_d = nc.dram_tensor("dbg", [128, dbg], F32, kind="ExternalOutput").ap()

    P = Prog(nc)
    st = ExitStack()
    with st:
        sbt = lambda name, shape, dt: st.enter_context(nc.sbuf_tensor(name, shape, dt))
        xT = sbt("xT", [128, KC, T], F32)
        hT = sbt("hT", [128, KC, T], BF16)
        c32_t = sbt("c32", [128, 4, 128], F32)
        cb_t = sbt("cb", [128, 6, 128], BF16)
        idb_t = sbt("idb", [128, 128], BF16)
        one_b_t = sbt("oneb", [128, 128], BF16)
        par_t = sbt("par_sb", [128, depth, NPAR], F32)
        ab_t = sbt("ab", [128, 6, KC, 2], F32)
        dI_t = sbt("dI", [128, 6, 128], BF16)
        sml_t = sbt("sml", [128, 64], F32)
        arena = Arena(nc, st, (nc.sbuf_bytes_remaining - 1024) // 64 * 64)
        banks = [Buf(st.enter_context(nc.psum_tensor(f"ps{i}", [128, 512], F32)), [("ps", i)]) for i in range(8)]

        ident32 = c32_t[:, 0, :]
        Uf32 = c32_t[:, 1, :]
        Ub32 = c32_t[:, 2, :]
        ones32 = c32_t[:, 3, :]
        blk64b = cb_t[:, 0, :]
        rotb = cb_t[:, 1, :]
        onesDb = cb_t[:, 2, :]
        mFb = cb_t[:, 3, :]
        mBb = cb_t[:, 4, :]
        hmb = cb_t[:, 5, 0:2]
        identb = idb_t[:, :]
        ones1b = one_b_t[:, :]
        CONST = "const"

        def hk(t0, n):
            return [("hT", t) for t in range(t0 // 128, (t0 + n + 127) // 128)]

        def xkk(t0, n):
            r = []
            for t in range(t0 // 128, (t0 + n + 127) // 128):
                r += [("xT", t, 0), ("xT", t, 1)]
            return r

        def b16(bank):
            return bank.ap[:].bitcast(BF16)

        P.dma("sp", I("dma_start", out=c32_t[:, 0:3, :], in_=con_d[:, 0:384].rearrange("p (a b) -> p a b", a=3)), writes=[CONST])
        P.dma("sp", I("dma_start", out=c32_t[:, 3, :], in_=con_d[:, C_ONE * 128:(C_ONE + 1) * 128]), writes=[CONST])
        P.dma("sp", I("dma_start", out=par_t[:], in_=par_d.rearrange("p (l n) -> p l n", l=depth)), writes=["par"])
        P.dma("pool", I("dma_start", out=cb_t[:], in_=con_d[:, C_BLK * 128:(C_HM + 1) * 128].rearrange("p (a b) -> p a b", a=6)), writes=[CONST])
        P.dma("pool", I("dma_start", out=idb_t[:], in_=con_d[:, C_ID * 128:(C_ID + 1) * 128]), writes=[CONST])
        P.dma("pool", I("dma_start", out=one_b_t[:], in_=con_d[:, C_ONE * 128:(C_ONE + 1) * 128]), writes=[CONST])

        def dump(ap_, col0, ncols, reads):
            if dbg_d is None:
                return
            if ap_.dtype != F32:
                tmp = arena.alloc(ncols, F32)
                P.dve(I("tensor_copy", out=tmp.flat, in_=ap_), reads=reads, writes=[tmp])
                ap_, reads = tmp.flat, [tmp]
            P.dma("sp", I("dma_start", out=dbg_d[:, col0:col0 + ncols], in_=ap_), reads=reads, writes=[("dbg", col0)])
            P.op("sp", [], reads=[("dbg", col0)])

        def phase_load():
            m = arena.mark()
            stg = [arena.alloc(D, F32) for _ in range(2)]
            for tt in range(NT):
                s = stg[tt % 2]
                src = x_d[tt * 128:(tt + 1) * 128, :] if tt < NTL else ctx_d[(tt - NTL) * 128:(tt - NTL + 1) * 128, :]
                P.dma("sp", I("dma_start", out=s.flat, in_=src), writes=[s])
                for half in range(2):
                    bank = banks[(tt % 2) * 2 + half]
                    P.pe([I("transpose", out=bank.ap[:, kk * 128:(kk + 1) * 128], in_=s.flat[:, (half * 4 + kk) * 128:(half * 4 + kk + 1) * 128], identity=ident32)
                          for kk in range(4)], reads=[s, CONST], writes=[bank])
                    dst = xT[:, half * 4:(half + 1) * 4, tt * 128:(tt + 1) * 128]
                    srcp = bank.ap[:].rearrange("p (k n) -> p k n", k=4)
                    if half == 0:
                        P.act(I("copy", out=dst, in_=srcp), reads=[bank], writes=[("xT", tt, half)])
                    else:
                        P.dve(I("tensor_copy", out=dst, in_=srcp), reads=[bank], writes=[("xT", tt, half)])
            arena.release(m)

        def phase_mod(l):
            m = arena.mark()
            cv32 = arena.alloc((KC, 2), F32)
            cvb = arena.alloc((KC, 2), BF16)
            P.dma("sp", I("dma_start", out=cv32.flat, in_=cv_d), writes=[cv32])
            P.act(I("activation", out=cvb.flat, in_=cv32.flat, func=AF.Silu), reads=[cv32], writes=[cvb])
            wm = [arena.alloc((KC, 768), BF16) for _ in range(2)]
            bank = banks[7]
            for piece in range(8):
                w = wm[piece % 2]
                P.dma("pool", I("dma_start", out=w.v, in_=wmod_d[l][:, piece * 768:(piece + 1) * 768].rearrange("(k p) n -> p k n", p=128)), writes=[w])
                ins = []
                for jj in range(6):
                    j = piece * 6 + jj
                    for k in range(KC):
                        ins.append(I("matmul", out=bank.ap[:, 2 * j:2 * j + 2], lhsT=w.v[:, k, jj * 128:(jj + 1) * 128], rhs=cvb.v[:, k, :],
                                     start=(k == 0), stop=(k == KC - 1)))
                P.pe(ins, reads=[w, cvb], writes=[bank])
            modT = arena.alloc((48, 2), F32)
            par = par_t[:, l, :]
            P.dve(I("tensor_tensor", out=modT.v, in0=bank.ap[:, 0:96].rearrange("p (j c) -> p j c", c=2),
                    in1=par[:, PO_BMOD:PO_BMOD + 48].unsqueeze(2).to_broadcast([128, 48, 2]), op=ALU.add),
                  reads=[bank, "par"], writes=[modT])
            ng = lambda f: par[:, PO_NG + f * 8:PO_NG + f * 8 + 8].unsqueeze(2).to_broadcast([128, KC, 2])
            mv = lambda i: modT.v[:, i * 8:(i + 1) * 8, :]
            P.dve([I("scalar_tensor_tensor", out=ab_t[:, 0], in0=mv(1), scalar=1.0, in1=ng(0), op0=ALU.add, op1=ALU.mult),
                   I("tensor_copy", out=ab_t[:, 1], in_=mv(0)),
                   I("tensor_tensor", out=ab_t[:, 2], in0=mv(2), in1=ng(1), op=ALU.mult),
                   I("scalar_tensor_tensor", out=ab_t[:, 3], in0=mv(4), scalar=1.0, in1=ng(2), op0=ALU.add, op1=ALU.mult),
                   I("tensor_copy", out=ab_t[:, 4], in_=mv(3)),
                   I("tensor_tensor", out=ab_t[:, 5], in0=mv(5), in1=ng(3), op=ALU.mult)],
                  reads=[modT, "par"], writes=["ab"])
            arena.release(m)

        def ms_block(src_k, n, bank, sqs, reads):
            for k in range(KC):
                sq = sqs[k % 2]
                P.act(I("activation", out=sq.flat[:, :n], in_=src_k(k), func=AF.Square), reads=reads, writes=[sq])
                P.pe(I("matmul", out=bank.ap[:, :n], lhsT=onesDb, rhs=sq.flat[:, :n], start=(k == 0), stop=(k == KC - 1)),
                     reads=[sq, CONST], writes=[bank])

        def rstd_from(bank, n, rs, scale=None):
            P.act(I("activation", out=rs.flat[:, :n], in_=bank.ap[:, :n], func=AF.Ln, bias=EPS, scale=(1.0 if scale is None else scale)),
                  reads=[bank], writes=[rs])
            P.act(I("activation", out=rs.flat[:, :n], in_=rs.flat[:, :n], func=AF.Exp, scale=-0.5), reads=[rs], writes=[rs])

        def sigmoid_from(bank, n, dst):
            P.act(I("activation", out=dst.flat[:, :n], in_=bank.ap[:, :n], func=AF.Exp, scale=-1.0), reads=[bank], writes=[dst])
            P.act(I("activation", out=dst.flat[:, :n], in_=dst.flat[:, :n], func=AF.Ln, bias=1.0, scale=1.0), reads=[dst], writes=[dst])
            P.act(I("activation", out=dst.flat[:, :n], in_=dst.flat[:, :n], func=AF.Exp, scale=-1.0), reads=[dst], writes=[dst])

        def phase_norm_h(ai, bi_):
            m = arena.mark()
            sqs = [arena.alloc(512, BF16) for _ in range(2)]
            rs = arena.alloc(512, F32)
            tmps = [arena.alloc(512, F32) for _ in range(2)]
            bank = banks[6]
            for (t0, n) in blocks:
                col = 0 if t0 < L else 1
                ms_block(lambda k: xT[:, k, t0:t0 + n], n, bank, sqs, xkk(t0, n))
                rstd_from(bank, n, rs)
                for k in range(KC):
                    tmp = tmps[k % 2]
                    P.dve(I("scalar_tensor_tensor", out=tmp.flat[:, :n], in0=xT[:, k, t0:t0 + n], scalar=ab_t[:, ai, k, col:col + 1],
                            in1=rs.flat[:, :n], op0=ALU.mult, op1=ALU.mult),
                          reads=xkk(t0, n) + [rs, "ab"], writes=[tmp])
                    P.act(I("activation", out=hT[:, k, t0:t0 + n], in_=tmp.flat[:, :n], func=AF.Identity,
                            bias=ab_t[:, bi_, k, col:col + 1], scale=1.0),
                          reads=[tmp, "ab"], writes=hk(t0, n))
            arena.release(m)

        def load_w(dram_ap, buf):
            P.dma("pool", I("dma_start", out=buf.v, in_=dram_ap.rearrange("(k p) n -> p k n", p=128)), writes=[buf])

        def proj_fm(bank, w, c0, M, t0, n):
            P.pe([I("matmul", out=bank.ap[0:M, :n], lhsT=w.v[:, k, c0:c0 + M], rhs=hT[:, k, t0:t0 + n], start=(k == 0), stop=(k == KC - 1))
                  for k in range(KC)], reads=[w] + hk(t0, n), writes=[bank])

        def scan(H, hg, la_of, dt_of, kt_chunk, n_kt, kidx, qg_of, vtok_of, extra_terms, extra_reads, finish, dbuf, shared=None, skip_out=(), pre=None):
            NG = H // hg
            HP = H * 64
            units = [(0, min(H, 4))] + ([(4, H - 4)] if H > 4 else [])
            NU = len(units)
            NPAR = 2 if dbuf else 1
            bS, bG, bT, bY = banks[0], banks[3], banks[4], banks[5]
            pbanks = [banks[1], banks[2], banks[6], banks[7]]
            bZ, bF = bG, bS
            m = arena.mark()
            sb_store = arena.alloc((NT, HP), BF16)
            Sst = [arena.alloc(HP, F32) for _ in range(2)]
            sfb = arena.alloc(HP, BF16)
            mk2 = lambda shape, dt: [[arena.alloc(shape, dt) for _ in range(2)] for _ in range(NPAR)]
            class _Pair:
                def __init__(self, buf, aps):
                    self.buf, self.aps = buf, aps

                def __getitem__(self, d):
                    return Buf(self.aps[d], self.buf.keys)
            ptc = [arena.alloc((4, H), F32) for _ in range(NPAR)]
            difc = [arena.alloc((2, H), F32) for _ in range(NPAR)]
            wvc = [arena.alloc((2, H), F32) for _ in range(NPAR)]
            decc = [arena.alloc((2, H), F32) for _ in range(NPAR)]
            cwc = [arena.alloc((2, H), F32) for _ in range(NPAR)]
            Vdt = mk2(HP, BF16) if dt_of(0) is not None else None
            E = mk2((H, 128), BF16)
            Ebc = mk2((H, 128), BF16)
            Gm = mk2((NG, 128), BF16)
            MT, QsT = E, Ebc
            Vw = shared if shared is not None else arena.alloc(HP, BF16)
            ktok = arena.alloc((n_kt, 128), BF16)
            R = [[arena.alloc((hn, 128), F32) for (_, hn) in units] for _ in range(2)]
            Ud = [Uf32, Ub32]
            md = [mFb, mBb]
            hq = lambda ap_: ap_.rearrange("p (h q) -> p h q", h=H)
            g4 = lambda ap_: ap_.rearrange("p (g a) i -> p g a i", g=NG)

            def small(c, d, p):
                lap, lar = la_of(c)
                P.pe([I("matmul", out=bS.ap[:, 0:H], lhsT=Ud[d], rhs=lap[:, d, :], start=True, stop=True),
                      I("matmul", out=bS.ap[:, H:2 * H], lhsT=ones32, rhs=lap[:, d, :], start=True, stop=True)],
                     reads=[CONST] + lar, writes=[bS])
                P.act([I("copy", out=ptc[p].v[:, d, :], in_=bS.ap[:, 0:H]), I("copy", out=ptc[p].v[:, 2 + d, :], in_=bS.ap[:, H:2 * H])],
                      reads=[bS], writes=[ptc[p]])
                P.dve(I("tensor_tensor", out=difc[p].v[:, d, :], in0=ptc[p].v[:, 2 + d, :], in1=ptc[p].v[:, d, :], op=ALU.subtract),
                      reads=[ptc[p]], writes=[difc[p]])
                P.act(I("activation", out=wvc[p].v[:, d, :], in_=difc[p].v[:, d, :], func=AF.Exp), reads=[difc[p]], writes=[wvc[p]])
                P.act(I("activation", out=decc[p].v[:, d, :], in_=ptc[p].v[:, 2 + d, :], func=AF.Exp), reads=[ptc[p]], writes=[decc[p]])
                dtv = dt_of(c)
                if dtv is not None:
                    P.dve(I("tensor_tensor", out=cwc[p].v[:, d, :], in0=wvc[p].v[:, d, :], in1=dtv[0][:, d, :], op=ALU.mult),
                          reads=[wvc[p]] + dtv[1], writes=[cwc[p]])
                else:
                    P.dve(I("tensor_copy", out=cwc[p].v[:, d, :], in_=wvc[p].v[:, d, :]), reads=[wvc[p]], writes=[cwc[p]])

            def small2(c, p):
                lap, lar = la_of(c)
                P.pe([I("matmul", out=bS.ap[:, 0:H], lhsT=Ud[0], rhs=lap[:, 0, :], start=True, stop=True),
                      I("matmul", out=bS.ap[:, H:2 * H], lhsT=Ud[1], rhs=lap[:, 1, :], start=True, stop=True),
                      I("matmul", out=bS.ap[:, 2 * H:4 * H], lhsT=ones32, rhs=lap.rearrange("p d h -> p (d h)"), start=True, stop=True)],
                     reads=[CONST] + lar, writes=[bS])
                P.act(I("copy", out=ptc[p].flat, in_=bS.ap[:, 0:4 * H]), reads=[bS], writes=[ptc[p]])
                P.dve(I("tensor_tensor", out=difc[p].flat, in0=ptc[p].flat[:, 2 * H:4 * H], in1=ptc[p].flat[:, 0:2 * H], op=ALU.subtract),
                      reads=[ptc[p]], writes=[difc[p]])
                P.act(I("activation", out=wvc[p].flat, in_=difc[p].flat, func=AF.Exp), reads=[difc[p]], writes=[wvc[p]])
                P.act(I("activation", out=decc[p].flat, in_=ptc[p].flat[:, 2 * H:4 * H], func=AF.Exp), reads=[ptc[p]], writes=[decc[p]])
                dtv = dt_of(c)
                if dtv is not None:
                    P.dve(I("tensor_tensor", out=cwc[p].flat, in0=wvc[p].flat, in1=dtv[0].rearrange("p d h -> p (d h)"), op=ALU.mult),
                          reads=[wvc[p]] + dtv[1], writes=[cwc[p]])
                else:
                    P.dve(I("tensor_copy", out=cwc[p].flat, in_=wvc[p].flat), reads=[wvc[p]], writes=[cwc[p]])

            pt = [_Pair(ptc[p], [ptc[p].v[:, 0, :], ptc[p].v[:, 1, :]]) for p in range(NPAR)]
            dec = [_Pair(decc[p], [decc[p].v[:, 0, :], decc[p].v[:, 1, :]]) for p in range(NPAR)]
            cw = [_Pair(cwc[p], [cwc[p].v[:, 0, :], cwc[p].v[:, 1, :]]) for p in range(NPAR)]

            def dstate(c, d, S, p):
                vt, vr = vtok_of(c)
                P.dve(I("tensor_tensor", out=hq(Vw.flat), in0=hq(vt), in1=cw[p][d].ap.unsqueeze(2).to_broadcast([128, H, 64]), op=ALU.mult),
                      reads=vr + [cw[p][d]], writes=[Vw])
                bt16 = b16(bT)
                P.pe([I("transpose", out=bt16[:, i * 128:(i + 1) * 128], in_=kt_chunk(i, c)[0], identity=identb) for i in range(n_kt)],
                     reads=[CONST] + kt_chunk(0, c)[1], writes=[bT])
                P.act(I("copy", out=ktok.flat, in_=bt16[:, 0:n_kt * 128]), reads=[bT], writes=[ktok])
                P.pe([I("matmul", out=bT.ap[:, g * hg * 64:(g + 1) * hg * 64], lhsT=ktok.v[:, kidx(g), :], rhs=Vw.flat[:, g * hg * 64:(g + 1) * hg * 64],
                        start=True, stop=True) for g in range(NG)], reads=[ktok, Vw], writes=[bT])
                P.dve(I("tensor_tensor", out=hq(S.flat), in0=hq(S.flat), in1=dec[p][d].ap.unsqueeze(2).to_broadcast([128, H, 64]), op=ALU.mult),
                      reads=[S, dec[p][d]], writes=[S])
                P.dve(I("tensor_tensor", out=S.flat, in0=S.flat, in1=bT.ap[:, 0:HP], op=ALU.add), reads=[S, bT], writes=[S])

            P.pool(I("memset", ap=Sst[1].flat, constant=0.0), writes=[Sst[1]])
            P.pool(I("memset", ap=Sst[0].flat, constant=0.0), writes=[Sst[0]])
            for c in bwd_order:
                P.act(I("copy", out=sb_store.v[:, c, :], in_=Sst[1].flat), reads=[Sst[1]], writes=[sb_store.sub(c * HP, (c + 1) * HP)])
                small(c, 1, 0)
                dstate(c, 1, Sst[1], 0)

            loc = {}

            def local_part(c, p):
                qg, qgr = qg_of(c, p)
                lap, lar = la_of(c)
                loc[c] = (qg, qgr)
                P.pe([I("matmul", out=bG.ap[:, g * 128:(g + 1) * 128], lhsT=kt_chunk(kidx(g), c)[0], rhs=qg[:, g, :], start=True, stop=True)
                      for g in range(NG)], reads=kt_chunk(0, c)[1] + qgr, writes=[bG])
                for d in range(2):
                    P.dve(I("tensor_tensor", out=Gm[p][d].v, in0=bG.ap[:, 0:NG * 128].rearrange("p (g i) -> p g i", g=NG),
                            in1=md[d].unsqueeze(1).to_broadcast([128, NG, 128]), op=ALU.mult),
                          reads=[bG, CONST], writes=[Gm[p][d]])
                small2(c, p)
                chains = [(d, ui, h0, hn) for d in range(2) for ui, (h0, hn) in enumerate(units)]
                bk = lambda d, ui: pbanks[(d * NU + ui) % 4]
                for (d, ui, h0, hn) in chains:
                    P.dve(I("tensor_tensor", out=R[d][ui].v, in0=Ud[d].unsqueeze(1).to_broadcast([128, hn, 128]),
                            in1=lap[:, d, h0:h0 + hn].unsqueeze(2).to_broadcast([128, hn, 128]), op=ALU.mult),
                          reads=[CONST] + lar, writes=[R[d][ui]])
                for (d, ui, h0, hn) in chains:
                    P.pe(I("matmul", out=bk(d, ui).ap[:, 0:hn * 128], lhsT=ones32, rhs=R[d][ui].flat, start=True, stop=True),
                         reads=[CONST, R[d][ui]], writes=[bk(d, ui)])
                for (d, ui, h0, hn) in chains:
                    P.act(I("activation", out=Ebc[p][d].flat[:, h0 * 128:(h0 + hn) * 128], in_=bk(d, ui).ap[:, 0:hn * 128], func=AF.Exp),
                          reads=[bk(d, ui)], writes=[Ebc[p][d].sub(h0 * 128, (h0 + hn) * 128)])
                    P.dve(I("tensor_tensor", out=R[d][ui].v, in0=bk(d, ui).ap[:, 0:hn * 128].rearrange("p (h i) -> p h i", h=hn),
                            in1=pt[p][d].ap[:, h0:h0 + hn].unsqueeze(2).to_broadcast([128, hn, 128]), op=ALU.subtract),
                          reads=[bk(d, ui), pt[p][d]], writes=[R[d][ui]])
                for (d, ui, h0, hn) in chains:
                    P.dve(I("tensor_tensor", out=R[d][ui].v, in0=R[d][ui].v, in1=md[d].unsqueeze(1).to_broadcast([128, hn, 128]), op=ALU.mult),
                          reads=[R[d][ui], CONST], writes=[R[d][ui]])
                for (d, ui, h0, hn) in chains:
                    P.act(I("activation", out=E[p][d].flat[:, h0 * 128:(h0 + hn) * 128], in_=R[d][ui].flat, func=AF.Exp),
                          reads=[R[d][ui]], writes=[E[p][d].sub(h0 * 128, (h0 + hn) * 128)])
                vt, vr = vtok_of(c)
                dtv = dt_of(c)
                for d in range(2):
                    P.dve(I("tensor_tensor", out=g4(QsT[p][d].v), in0=g4(Ebc[p][d].v), in1=qg.unsqueeze(2).to_broadcast([128, NG, hg, 128]), op=ALU.mult),
                          reads=[Ebc[p][d]] + qgr, writes=[QsT[p][d]])
                    if dtv is not None:
                        P.pool(I("tensor_tensor", out=hq(Vdt[p][d].flat), in0=hq(vt), in1=dtv[0][:, d, :].unsqueeze(2).to_broadcast([128, H, 64]), op=ALU.mult),
                               reads=vr + dtv[1], writes=[Vdt[p][d]])
                for d in range(2):
                    P.dve(I("tensor_tensor", out=g4(MT[p][d].v), in0=g4(E[p][d].v), in1=Gm[p][d].v.unsqueeze(2).to_broadcast([128, NG, hg, 128]), op=ALU.mult),
                          reads=[E[p][d], Gm[p][d]], writes=[MT[p][d]])
                if pre is not None:
                    pre(c, p, bG)

            def state_part(c, p):
                vt, vr = vtok_of(c)
                dtv = dt_of(c)
                ins = []
                for h in range(H):
                    out = bY.ap[(h % 2) * 64:(h % 2) * 64 + 64, (h // 2) * 128:(h // 2 + 1) * 128]
                    terms = []
                    for d in range(2):
                        lhs = Vdt[p][d].flat[:, h * 64:(h + 1) * 64] if dtv is not None else vt[:, h * 64:(h + 1) * 64]
                        terms.append((lhs, MT[p][d].v[:, h, :]))
                    terms.append((sfb.flat[:, h * 64:(h + 1) * 64], QsT[p][0].v[:, h, :]))
                    terms.append((sb_store.v[:, c, h * 64:(h + 1) * 64], QsT[p][1].v[:, h, :]))
                    terms += extra_terms(c, h)
                    for i, (lh, rh) in enumerate(terms):
                        ins.append(I("matmul", out=out, lhsT=lh, rhs=rh, start=(i == 0), stop=(i == len(terms) - 1)))
                P.pe(ins, reads=([Vdt[p][0], Vdt[p][1]] if dtv is not None else []) + [MT[p][0], MT[p][1], QsT[p][0], QsT[p][1], sfb, sb_store.sub(c * HP, (c + 1) * HP)] + vr + extra_reads,
                     writes=[bY])
                dstate(c, 0, Sst[0], p)
                P.act(I("copy", out=sfb.flat, in_=Sst[0].flat), reads=[Sst[0]], writes=[sfb])
                finish(c, bY, bF, p)

            order = []
            for c in fwd_order:
                if c in skip_out:
                    small(c, 0, 0)
                    dstate(c, 0, Sst[0], 0)
                else:
                    order.append(c)
            P.act(I("copy", out=sfb.flat, in_=Sst[0].flat), reads=[Sst[0]], writes=[sfb])
            if dbuf:
                local_part(order[0], 0)
                for i, c in enumerate(order):
                    if i + 1 < len(order):
                        local_part(order[i + 1], (i + 1) % 2)
                    state_part(c, i % 2)
            else:
                for c in order:
                    local_part(c, 0)
                    state_part(c, 0)
            arena.release(m)

        def phase_ssd(l, oT_ssd, ctx_out):
            par = par_t[:, l, :]
            m0 = arena.mark()
            BT = arena.alloc((2, T), BF16)
            CT = arena.alloc((2, T), BF16)
            Xtok = arena.alloc((NT, 384), BF16)
            dtv = arena.alloc((NT, 2, 6), F32)
            lav = arena.alloc((NT, 2, 6), F32)
            wz = arena.alloc((KC, 384), BF16)
            load_w(win_d[l][:, CO_Z:CO_Z + 384], wz)
            m1 = arena.mark()
            wdt = arena.alloc((KC, 12), BF16)
            load_w(win_d[l][:, CO_DT:CO_DT + 12], wdt)
            bank = banks[0]
            P.pe([I("matmul", out=bank.ap[:, tt * 12:(tt + 1) * 12], lhsT=hT[:, k, tt * 128:(tt + 1) * 128], rhs=wdt.v[:, k, :],
                    start=(k == 0), stop=(k == KC - 1)) for tt in range(NT) for k in range(KC)],
                 reads=[wdt] + hk(0, T), writes=[bank])
            d3 = lambda b: b.flat.rearrange("p (t c) -> p t c", c=12)
            P.dve(I("tensor_tensor", out=d3(dtv), in0=bank.ap[:, 0:NT * 12].rearrange("p (t c) -> p t c", c=12),
                    in1=par[:, PO_DTB:PO_DTB + 12].unsqueeze(1).to_broadcast([128, NT, 12]), op=ALU.add),
                  reads=[bank, "par"], writes=[dtv])
            aexp = sml_t[:, 0:12]
            P.act(I("activation", out=dtv.flat, in_=dtv.flat, func=AF.Exp), reads=[dtv], writes=[dtv])
            P.act(I("activation", out=dtv.flat, in_=dtv.flat, func=AF.Ln, bias=1.0, scale=1.0), reads=[dtv], writes=[dtv])
            P.act(I("activation", out=aexp, in_=par[:, PO_ALOG:PO_ALOG + 12], func=AF.Exp), reads=["par"], writes=["aexp"])
            P.dve(I("scalar_tensor_tensor", out=d3(lav), in0=d3(dtv), scalar=-1.0, in1=aexp.unsqueeze(1).to_broadcast([128, NT, 12]),
                    op0=ALU.mult, op1=ALU.mult), reads=[dtv, "aexp"], writes=[lav])
            if stop_after == "ssd_a":
                dump(lav.flat, 0, NT * 12, [lav])
                raise _Stop()
            P.dve([I("tensor_scalar", out=dI_t[:, h, :], in0=identb, scalar1=par[:, PO_SD + h:PO_SD + h + 1], scalar2=None, op0=ALU.mult) for h in range(6)],
                  reads=[CONST, "par"], writes=["dI"])
            rawp = arena.alloc(T + 8, F32)
            acc = arena.alloc(T, F32)
            xs = [arena.alloc(T, BF16) for _ in range(2)]
            wx = [arena.alloc((KC, 128), BF16) for _ in range(2)]
            P.pool(I("memset", ap=rawp.flat, constant=0.0), writes=[rawp])
            roff = lambda t0: (2 + t0) if t0 < L else (t0 + 6)
            for cc in range(7):
                w = wx[cc % 2]
                load_w(win_d[l][:, CO_XBC + cc * 128:CO_XBC + (cc + 1) * 128], w)
                for bi, (t0, n) in enumerate(blocks):
                    bank = banks[1 + bi % 2]
                    proj_fm(bank, w, 0, 128, t0, n)
                    P.act(I("copy", out=rawp.flat[:, roff(t0):roff(t0) + n], in_=bank.ap[:, :n]), reads=[bank], writes=[rawp])
                for (s0, sn, ro) in [(0, L, 2), (L, LC, L + 6)]:
                    sa = acc.sub(s0, s0 + sn)
                    P.dve(I("tensor_scalar", out=sa.ap, in0=rawp.flat[:, ro - 2:ro - 2 + sn], scalar1=par[:, PO_CW + cc * 5:PO_CW + cc * 5 + 1],
                            scalar2=par[:, PO_CB + cc:PO_CB + cc + 1], op0=ALU.mult, op1=ALU.add), reads=[rawp, "par"], writes=[sa])
                    for j in range(1, 5):
                        P.dve(I("scalar_tensor_tensor", out=sa.ap, in0=rawp.flat[:, ro - 2 + j:ro - 2 + j + sn],
                                scalar=par[:, PO_CW + cc * 5 + j:PO_CW + cc * 5 + j + 1], in1=sa.ap, op0=ALU.mult, op1=ALU.add),
                              reads=[rawp, "par", sa], writes=[sa])
                if cc < 3:
                    dst = xs[cc % 2]
                    dflat_ = dst.flat
                elif cc < 5:
                    dst = BT.sub((cc - 3) * T, (cc - 2) * T)
                    dflat_ = dst.ap
                else:
                    dst = CT.sub((cc - 5) * T, (cc - 4) * T)
                    dflat_ = dst.ap
                P.act(I("activation", out=dflat_, in_=acc.flat, func=AF.Silu), reads=[acc], writes=[dst])
                if cc < 3:
                    for t8 in range(0, NT, 8):
                        nt8 = min(8, NT - t8)
                        bank = banks[3 + (t8 // 8) % 2]
                        P.pe([I("transpose", out=b16(bank)[:, i * 128:(i + 1) * 128], in_=dflat_[:, (t8 + i) * 128:(t8 + i + 1) * 128], identity=identb)
                              for i in range(nt8)], reads=[dst, CONST], writes=[bank])
                        P.dve(I("tensor_copy", out=Xtok.v[:, t8:t8 + nt8, cc * 128:(cc + 1) * 128],
                                in_=b16(bank)[:, 0:nt8 * 128].rearrange("p (t f) -> p t f", f=128)),
                              reads=[bank], writes=[Xtok])
            if stop_after == "ssd_b":
                dump(Xtok.flat[:, 0:768], 0, 768, [Xtok])
                raise _Stop()
            arena.release(m1)
            sz = arena.alloc(384, F32)
            vv = sz
            sq = arena.alloc(384, BF16)
            rs = arena.alloc(128, F32)

            def pre(c, p, bZ):
                P.pe([I("matmul", out=bZ.ap[:, pc * 128:(pc + 1) * 128], lhsT=wz.v[:, k, pc * 128:(pc + 1) * 128], rhs=hT[:, k, c * 128:(c + 1) * 128],
                        start=(k == 0), stop=(k == KC - 1)) for pc in range(3) for k in range(KC)],
                     reads=[wz] + hk(c * 128, 128), writes=[bZ])
                sigmoid_from(bZ, 384, sz)
                P.dve(I("tensor_tensor", out=sz.flat, in0=bZ.ap[:, 0:384], in1=sz.flat, op=ALU.mult), reads=[bZ, sz], writes=[sz])

            def finish(c, bY, bF, p):
                P.dve(I("tensor_tensor", out=vv.flat, in0=bY.ap[:, 0:384], in1=sz.flat, op=ALU.mult), reads=[bY, sz], writes=[vv])
                P.act(I("activation", out=sq.flat, in_=vv.flat, func=AF.Square), reads=[vv], writes=[sq])
                P.pe([I("matmul", out=bF.ap[:, 0:128], lhsT=ones1b, rhs=sq.flat[:, pc * 128:(pc + 1) * 128], start=(pc == 0), stop=(pc == 2)) for pc in range(3)],
                     reads=[sq, CONST], writes=[bF])
                rstd_from(bF, 128, rs, scale=1.0 / 384.0)
                P.dve([I("scalar_tensor_tensor", out=oT_ssd.v[:, pc, c * 128:(c + 1) * 128], in0=vv.flat[:, pc * 128:(pc + 1) * 128],
                         scalar=par[:, PO_SNG + pc:PO_SNG + pc + 1], in1=rs.flat, op0=ALU.mult, op1=ALU.mult) for pc in range(3)],
                      reads=[vv, rs, "par"], writes=[oT_ssd.sub(pc * T + c * 128, pc * T + (c + 1) * 128) for pc in range(3)])

            scan(H=6, hg=3,
                 la_of=lambda c: (lav.v[:, c], [lav]),
                 dt_of=lambda c: (dtv.v[:, c], [dtv]),
                 kt_chunk=lambda i, c: (BT.v[:, i, c * 128:(c + 1) * 128], [BT]),
                 n_kt=2, kidx=lambda g: g,
                 qg_of=lambda c, p: (CT.v[:, :, c * 128:(c + 1) * 128], [CT]),
                 vtok_of=lambda c: (Xtok.v[:, c, :], [Xtok]),
                 extra_terms=lambda c, h: [(Xtok.v[:, c, h * 64:(h + 1) * 64], dI_t[:, h, :])],
                 extra_reads=["dI"],
                 finish=finish, dbuf=False, shared=sq, skip_out=(() if ctx_out else tuple(ctx_chunks)), pre=pre)
            arena.release(m0)

        def rope_store(src, src_reads, n, t0, dst_ap, dst_buf, ropeb, scr, scale=1.0, dsts=None):
            qb, t1, t2 = scr
            bR = banks[7]
            if scale == 1.0:
                P.act(I("copy", out=qb.flat[:, :n], in_=src), reads=src_reads, writes=[qb])
            else:
                P.act(I("mul", out=qb.flat[:, :n], in_=src, mul=scale), reads=src_reads, writes=[qb])
            P.pe(I("matmul", out=bR.ap[:, :n], lhsT=rotb, rhs=qb.flat[:, :n], start=True, stop=True), reads=[qb, CONST], writes=[bR])
            P.dve(I("tensor_tensor", out=t1.flat[:, :n], in0=qb.flat[:, :n], in1=ropeb.v[:, 0, t0:t0 + n], op=ALU.mult), reads=[qb, ropeb], writes=[t1])
            P.dve(I("tensor_tensor", out=t2.flat[:, :n], in0=bR.ap[:, :n], in1=ropeb.v[:, 1, t0:t0 + n], op=ALU.mult), reads=[bR, ropeb], writes=[t2])
            if dsts is None:
                dsts = [(0, 128, dst_ap, dst_buf)]
            for (p0, p1, d_ap, d_buf) in dsts:
                P.pool(I("tensor_tensor", out=d_ap, in0=t1.flat[p0:p1, :n], in1=t2.flat[p0:p1, :n], op=ALU.add), reads=[t1, t2], writes=[d_buf])

        def load_rope():
            ropeb = arena.alloc((2, T), BF16)
            P.dma("pool", I("dma_start", out=ropeb.v, in_=rope_d.rearrange("p (a t) -> p a t", a=2)), writes=[ropeb])
            return ropeb

        def phase_ret(l, oT_ret, ctx_out):
            par = par_t[:, l, :]
            m0 = arena.mark()
            QTr = arena.alloc((2, T), BF16)
            KTr = arena.alloc((2, T), BF16)
            Vtok = arena.alloc((NT, 256), BF16)
            wg = arena.alloc((KC, 256), BF16)
            load_w(win_d[l][:, CO_GR:CO_GR + 256], wg)
            lar = arena.alloc((2, 4), F32)
            m1 = arena.mark()
            ropeb = load_rope()
            wq = arena.alloc((KC, 256), BF16)
            wk_ = arena.alloc((KC, 256), BF16)
            wv = arena.alloc((KC, 256), BF16)
            load_w(win_d[l][:, CO_QR:CO_QR + 256], wq)
            load_w(win_d[l][:, CO_KR:CO_KR + 256], wk_)
            load_w(win_d[l][:, CO_VR:CO_VR + 256], wv)
            scr = (arena.alloc(512, BF16), arena.alloc(512, F32), arena.alloc(512, F32))
            tl = sml_t[:, 16:24]
            P.act(I("activation", out=tl, in_=par[:, PO_RLOG:PO_RLOG + 8], func=AF.Exp, scale=-1.0), reads=["par"], writes=["tl"])
            P.act(I("activation", out=tl, in_=tl, func=AF.Ln, bias=1.0, scale=1.0), reads=["tl"], writes=["tl"])
            P.dve(I("tensor_scalar", out=lar.flat, in0=tl, scalar1=-1.0, scalar2=None, op0=ALU.mult), reads=["tl"], writes=[lar])
            for (w, dstT, scale) in [(wq, QTr, 1.0), (wk_, KTr, 0.125)]:
                for pc in range(2):
                    for bi, (t0, n) in enumerate(blocks):
                        bank = banks[1 + bi % 2]
                        proj_fm(bank, w, pc * 128, 128, t0, n)
                        rope_store(bank.ap[:, :n], [bank], n, t0, dstT.v[:, pc, t0:t0 + n], dstT.sub(pc * T + t0, pc * T + t0 + n), ropeb, scr, scale)
            for tt in range(0, NT, 2):
                n2 = min(2, NT - tt)
                bank = banks[3 + (tt // 2) % 2]
                P.pe([I("matmul", out=bank.ap[:, i * 256:(i + 1) * 256], lhsT=hT[:, k, (tt + i) * 128:(tt + i + 1) * 128], rhs=wv.v[:, k, :],
                        start=(k == 0), stop=(k == KC - 1)) for i in range(n2) for k in range(KC)],
                     reads=[wv] + hk(tt * 128, n2 * 128), writes=[bank])
                P.act(I("copy", out=Vtok.v[:, tt:tt + n2, :], in_=bank.ap[:, 0:n2 * 256].rearrange("p (t f) -> p t f", f=256)),
                      reads=[bank], writes=[Vtok])
            arena.release(m1)
            Qz = [arena.alloc((4, 128), BF16) for _ in range(2)]
            y32 = arena.alloc(256, F32)
            yb = arena.alloc(256, BF16)
            yc = arena.alloc(256, F32)
            sq = arena.alloc(256, BF16)
            rs = arena.alloc(256, F32)
            sgs = [arena.alloc(256, F32) for _ in range(2)]

            def qg_of(c, p):
                Qz_ = Qz[p]
                P.dve(I("tensor_tensor", out=Qz_.flat.rearrange("p (a b i) -> p a b i", a=2, b=2),
                        in0=QTr.v[:, :, c * 128:(c + 1) * 128].unsqueeze(2).to_broadcast([128, 2, 2, 128]),
                        in1=hmb.unsqueeze(1).unsqueeze(3).to_broadcast([128, 2, 2, 128]), op=ALU.mult),
                      reads=[QTr, CONST], writes=[Qz_])
                return (Qz_.v, [Qz_])

            def pre(c, p, bZ):
                sg = sgs[p]
                P.pe([I("matmul", out=bZ.ap[:, pc * 128:(pc + 1) * 128], lhsT=wg.v[:, k, pc * 128:(pc + 1) * 128], rhs=hT[:, k, c * 128:(c + 1) * 128],
                        start=(k == 0), stop=(k == KC - 1)) for pc in range(2) for k in range(KC)],
                     reads=[wg] + hk(c * 128, 128), writes=[bZ])
                sigmoid_from(bZ, 256, sg)
                P.dve(I("tensor_tensor", out=sg.flat, in0=bZ.ap[:, 0:256], in1=sg.flat, op=ALU.mult), reads=[bZ, sg], writes=[sg])

            def finish(c, bY, bF, p):
                sg = sgs[p]
                P.act(I("copy", out=y32.flat, in_=bY.ap[:, 0:256]), reads=[bY], writes=[y32])
                P.dve(I("tensor_copy", out=yb.flat, in_=y32.flat), reads=[y32], writes=[yb])
                P.pe(I("matmul", out=bF.ap[:, 0:256], lhsT=blk64b, rhs=yb.flat, start=True, stop=True), reads=[yb, CONST], writes=[bF])
                P.dve(I("tensor_tensor", out=yc.flat, in0=y32.flat, in1=bF.ap[:, 0:256], op=ALU.subtract), reads=[y32, bF], writes=[yc])
                P.act(I("activation", out=sq.flat, in_=yc.flat, func=AF.Square), reads=[yc], writes=[sq])
                P.pe(I("matmul", out=bF.ap[:, 0:256], lhsT=blk64b, rhs=sq.flat, start=True, stop=True), reads=[sq, CONST], writes=[bF])
                rstd_from(bF, 256, rs)
                P.dve(I("tensor_tensor", out=yc.flat, in0=yc.flat, in1=rs.flat, op=ALU.mult), reads=[yc, rs], writes=[yc])
                P.act([I("activation", out=yc.flat[:, pc * 128:(pc + 1) * 128], in_=yc.flat[:, pc * 128:(pc + 1) * 128], func=AF.Identity,
                         bias=par[:, PO_RGB + pc:PO_RGB + pc + 1], scale=par[:, PO_RGG + pc:PO_RGG + pc + 1]) for pc in range(2)],
                      reads=[yc, "par"], writes=[yc])
                P.dve(I("tensor_tensor", out=oT_ret.v[:, :, c * 128:(c + 1) * 128], in0=yc.flat.rearrange("p (a i) -> p a i", a=2),
                        in1=sg.flat.rearrange("p (a i) -> p a i", a=2), op=ALU.mult),
                      reads=[yc, sg], writes=[oT_ret.sub(pc * T + c * 128, pc * T + (c + 1) * 128) for pc in range(2)])

            scan(H=4, hg=1,
                 la_of=lambda c: (lar.v, [lar]),
                 dt_of=lambda c: None,
                 kt_chunk=lambda i, c: (KTr.v[:, i, c * 128:(c + 1) * 128], [KTr]),
                 n_kt=2, kidx=lambda g: g // 2,
                 qg_of=qg_of,
                 vtok_of=lambda c: (Vtok.v[:, c, :], [Vtok]),
                 extra_terms=lambda c, h: [],
                 extra_reads=[],
                 finish=finish, dbuf=True, skip_out=(() if ctx_out else tuple(ctx_chunks)), pre=pre)
            arena.release(m0)

        def phase_att(l, oT_att, ctx_out):
            par = par_t[:, l, :]
            m0 = arena.mark()
            ropeb = load_rope()
            kTd = arena.alloc((2, T), BF16)
            Vext = arena.alloc((NT, 2, 128), BF16)
            qz = [arena.alloc(T, BF16) for _ in range(2)]
            mA = arena.mark()
            scr = (arena.alloc(512, BF16), arena.alloc(512, F32), arena.alloc(512, F32))
            sqb = arena.alloc(512, BF16)
            rs = arena.alloc(512, F32)
            qn = arena.alloc(512, F32)
            mB = arena.mark()
            gq8 = sml_t[:, 32:33]
            P.dve(I("tensor_scalar", out=gq8, in0=par[:, PO_QG:PO_QG + 1], scalar1=0.125, scalar2=None, op0=ALU.mult), reads=["par"], writes=["gq8"])
            bN = banks[6]

            def normrope(bank, n, t0, gcol, greads, dsts):
                P.act(I("activation", out=sqb.flat[:, :n], in_=bank.ap[:, :n], func=AF.Square), reads=[bank], writes=[sqb])
                P.pe(I("matmul", out=bN.ap[:, :n], lhsT=blk64b, rhs=sqb.flat[:, :n], start=True, stop=True), reads=[sqb, CONST], writes=[bN])
                rstd_from(bN, n, rs)
                P.dve(I("scalar_tensor_tensor", out=qn.flat[:, :n], in0=bank.ap[:, :n], scalar=gcol, in1=rs.flat[:, :n], op0=ALU.mult, op1=ALU.mult),
                      reads=[bank, rs] + greads, writes=[qn])
                rope_store(qn.flat[:, :n], [qn], n, t0, None, None, ropeb, scr, dsts=dsts)

            wkd = arena.alloc((2, KC, 128), BF16)
            wv = arena.alloc((KC, 128), BF16)
            for g in range(2):
                for hh in range(2):
                    P.dma("pool", I("dma_start", out=wkd.v[:, g, :, hh * 64:(hh + 1) * 64],
                                    in_=win_d[l][:, CO_KA + g * 64:CO_KA + (g + 1) * 64].rearrange("(k p) n -> p k n", p=128)), writes=[wkd])
            load_w(win_d[l][:, CO_VA:CO_VA + 128], wv)
            P.pool(I("memset", ap=Vext.v[:, :, :, 64:65], constant=1.0), writes=[Vext])
            P.pool(I("memset", ap=qz[0].flat[64:128, :], constant=0.0), writes=[qz[0]])
            P.pool(I("memset", ap=qz[1].flat[0:64, :], constant=0.0), writes=[qz[1]])
            for g in range(2):
                for bi, (t0, n) in enumerate(blocks):
                    bank = banks[bi % 2]
                    P.pe([I("matmul", out=bank.ap[:, :n], lhsT=wkd.v[:, g, k, :], rhs=hT[:, k, t0:t0 + n], start=(k == 0), stop=(k == KC - 1))
                          for k in range(KC)], reads=[wkd] + hk(t0, n), writes=[bank])
                    normrope(bank, n, t0, par[:, PO_KG:PO_KG + 1], ["par"], [(0, 128, kTd.v[:, g, t0:t0 + n], kTd.sub(g * T + t0, g * T + t0 + n))])
            for tt in range(0, NT, 4):
                n4 = min(4, NT - tt)
                bank = banks[2 + (tt // 4) % 2]
                P.pe([I("matmul", out=bank.ap[:, i * 128:(i + 1) * 128], lhsT=hT[:, k, (tt + i) * 128:(tt + i + 1) * 128], rhs=wv.v[:, k, :],
                        start=(k == 0), stop=(k == KC - 1)) for i in range(n4) for k in range(KC)],
                     reads=[wv] + hk(tt * 128, n4 * 128), writes=[bank])
                bv = bank.ap[:, 0:n4 * 128].rearrange("p (t g d) -> p t g d", g=2, d=64)
                P.act(I("copy", out=Vext.v[:, tt:tt + n4, :, 0:64], in_=bv), reads=[bank], writes=[Vext])
                P.dve(I("tensor_copy", out=Vext.v[:, tt:tt + n4, :, 65:128], in_=bv[:, :, :, 0:63]), reads=[bank], writes=[Vext])
            arena.release(mB)
            wq = arena.alloc((KC, 128), BF16)
            end_off = arena.off
            arena.off = mA
            PT = [arena.alloc(512, BF16) for _ in range(4)]
            rr = arena.alloc(512, F32)
            bcs = arena.alloc(512, F32)
            tn = arena.alloc(512, BF16)
            assert arena.off <= mB
            arena.off = end_off
            pti = 0
            sbanks = [banks[0], banks[1], banks[2], banks[5]]
            obanks = [banks[3], banks[4]]
            bBc = banks[7]
            si = 0
            oi = 0
            for qc in range(3):
                w = wq
                load_w(win_d[l][:, CO_QA + qc * 128:CO_QA + (qc + 1) * 128], w)
                for bi, (t0, n) in enumerate(blocks):
                    bank = banks[bi % 2]
                    proj_fm(bank, w, 0, 128, t0, n)
                    normrope(bank, n, t0, gq8, ["gq8"], [(0, 64, qz[0].flat[0:64, t0:t0 + n], qz[0].sub(t0, t0 + n)),
                                                         (64, 128, qz[1].flat[64:128, t0:t0 + n], qz[1].sub(t0, t0 + n))])
                its = []
                for (t0, n) in blocks:
                    is_ctx = t0 >= L
                    if is_ctx and not ctx_out:
                        continue
                    kcs = ctx_chunks if is_ctx else list(range(NT))
                    for hh in range(2):
                        bO = obanks[oi % 2]
                        oi += 1
                        for ki, kc in enumerate(kcs):
                            its.append(dict(t0=t0, n=n, hh=hh, kc=kc, first=(ki == 0), last=(ki == len(kcs) - 1), bO=bO,
                                            bS=sbanks[si % 4], pt=PT[pti % 4]))
                            si += 1
                            pti += 1

                def emit_S(it):
                    t0, n, hh, kc, bSx, pt_ = it["t0"], it["n"], it["hh"], it["kc"], it["bS"], it["pt"]
                    g = (2 * qc + hh) // 3
                    P.pe(I("matmul", out=bSx.ap[:, :n], lhsT=kTd.v[:, g, kc * 128:(kc + 1) * 128], rhs=qz[hh].flat[:, t0:t0 + n], start=True, stop=True),
                         reads=[kTd.sub(g * T + kc * 128, g * T + (kc + 1) * 128), qz[hh].sub(t0, t0 + n)], writes=[bSx])
                    P.act(I("activation", out=pt_.flat[:, :n], in_=bSx.ap[:, :n], func=AF.Exp), reads=[bSx], writes=[pt_])

                def emit_PV(it):
                    t0, n, hh, kc, bO, pt_ = it["t0"], it["n"], it["hh"], it["kc"], it["bO"], it["pt"]
                    first, last = it["first"], it["last"]
                    g = (2 * qc + hh) // 3
                    P.pe(I("matmul", out=bO.ap[:, :n], lhsT=Vext.v[:, kc, g, :], rhs=pt_.flat[:, :n], start=first, stop=last),
                         reads=[Vext, pt_], writes=[bO])
                    if not last:
                        return
                    P.dve(I("reciprocal", out=rr.flat[64:65, :n], in_=bO.ap[64:65, :n]), reads=[bO], writes=[rr])
                    P.pe(I("matmul", out=bBc.ap[0:64, :n], lhsT=ones32[64:65, 0:64], rhs=rr.flat[64:65, :n], start=True, stop=True),
                         reads=[rr, CONST], writes=[bBc])
                    P.act(I("copy", out=bcs.flat[0:64, :n], in_=bBc.ap[0:64, :n]), reads=[bBc], writes=[bcs])
                    dst = oT_att.sub(qc * T + t0, qc * T + t0 + n)
                    if hh == 0:
                        P.dve(I("tensor_tensor", out=oT_att.v[0:64, qc, t0:t0 + n], in0=bO.ap[0:64, :n], in1=bcs.flat[0:64, :n], op=ALU.mult),
                              reads=[bO, bcs], writes=[dst])
                    else:
                        P.dve(I("tensor_tensor", out=tn.flat[0:64, :n], in0=bO.ap[0:64, :n], in1=bcs.flat[0:64, :n], op=ALU.mult),
                              reads=[bO, bcs], writes=[tn])
                        P.pe(I("matmul", out=bBc.ap[64:128, :n], lhsT=identb[0:64, 0:64], rhs=tn.flat[0:64, :n], start=True, stop=True),
                             reads=[tn, CONST], writes=[bBc])
                        P.act(I("copy", out=oT_att.v[64:128, qc, t0:t0 + n], in_=bBc.ap[64:128, :n]), reads=[bBc], writes=[dst])

                LOOK = 2
                for i in range(len(its) + LOOK):
                    if i < len(its):
                        emit_S(its[i])
                    if i >= LOOK:
                        emit_PV(its[i - LOOK])
            arena.release(m0)

        def phase_out(l, oT_parts, ctx_out):
            m0 = arena.mark()
            wo = arena.alloc((KC, D), BF16)
            load_w(wout_d[l], wo)
            yv = arena.alloc((KC, 512), F32)
            sqs = [arena.alloc(512, BF16) for _ in range(2)]
            rs = arena.alloc(512, F32)
            tmps = [arena.alloc(512, F32) for _ in range(2)]
            bMS = banks[6]
            pieces = []
            for (ob, nch) in oT_parts:
                for i in range(nch):
                    pieces.append((ob, i))
            for (t0, n) in blocks:
                if t0 >= L and not ctx_out:
                    continue
                col = 0 if t0 < L else 1
                for dc in range(KC):
                    bank = banks[dc % 4]
                    P.pe([I("matmul", out=bank.ap[:, :n], lhsT=wo.v[:, k, dc * 128:(dc + 1) * 128], rhs=ob.v[:, i, t0:t0 + n], start=(k == 0), stop=(k == KC - 1))
                          for k, (ob, i) in enumerate(pieces)],
                         reads=[wo] + [ob.sub(i * T + t0, i * T + t0 + n) for (ob, i) in pieces], writes=[bank])
                    P.act(I("copy", out=yv.v[:, dc, :n], in_=bank.ap[:, :n]), reads=[bank], writes=[yv.sub(dc * 512, dc * 512 + 512)])
                ms_block(lambda k: yv.v[:, k, :n], n, bMS, sqs, [yv])
                rstd_from(bMS, n, rs)
                for k in range(KC):
                    tmp = tmps[k % 2]
                    P.dve(I("scalar_tensor_tensor", out=tmp.flat[:, :n], in0=yv.v[:, k, :n], scalar=ab_t[:, 2, k, col:col + 1],
                            in1=rs.flat[:, :n], op0=ALU.mult, op1=ALU.mult),
                          reads=[yv.sub(k * 512, k * 512 + 512), rs, "ab"], writes=[tmp])
                    P.pool(I("tensor_tensor", out=xT[:, k, t0:t0 + n], in0=xT[:, k, t0:t0 + n], in1=tmp.flat[:, :n], op=ALU.add),
                           reads=[tmp] + xkk(t0, n), writes=xkk(t0, n))
            arena.release(m0)

        def phase_ffn(l, ctx_out):
            m0 = arena.mark()
            Tf = T if ctx_out else L
            GSZ = ((Tf // 2 + 127) // 128) * 128
            groups = []
            t = 0
            while t < Tf:
                gn = min(GSZ, Tf - t)
                groups.append((t, gn))
                t += gn
            yacc = arena.alloc((KC, GSZ), F32)
            w1 = [arena.alloc((KC, 512), BF16) for _ in range(2)]
            w2 = [arena.alloc((4, D), BF16) for _ in range(2)]
            uT = [arena.alloc((4, 512), BF16) for _ in range(2)]
            rl = [arena.alloc(512, BF16) for _ in range(2)]
            sqs = [arena.alloc(512, BF16) for _ in range(2)]
            rs = arena.alloc(512, F32)
            tmps = [arena.alloc(512, F32) for _ in range(2)]
            bMS = banks[7]
            ui = 0
            wi = 0
            for (g0, gn) in groups:
                subs = []
                t = g0
                while t < g0 + gn:
                    n = min(512, g0 + gn - t)
                    if t < L < t + n:
                        n = L - t
                    subs.append((t, n))
                    t += n
                its = []
                for fg in range(8):
                    for (t0, n) in subs:
                        its.append(dict(fg=fg, t0=t0, n=n, first_of_fg=((t0, n) == subs[0])))

                def u_phase(it):
                    nonlocal ui, wi
                    fg, t0, n = it["fg"], it["t0"], it["n"]
                    if it["first_of_fg"]:
                        it["wa"], it["wb"] = w1[wi % 2], w2[wi % 2]
                        wi += 1
                        load_w(w1_d[l][:, fg * 512:(fg + 1) * 512], it["wa"])
                        load_w(w2_d[l][fg * 512:(fg + 1) * 512, :], it["wb"])
                        cur["wa"], cur["wb"] = it["wa"], it["wb"]
                    else:
                        it["wa"], it["wb"] = cur["wa"], cur["wb"]
                    u = uT[ui % 2]
                    ui += 1
                    it["u"] = u
                    for fc in range(4):
                        bank = banks[fc % 2]
                        proj_fm(bank, it["wa"], fc * 128, 128, t0, n)
                        uk = u.sub(fc * 512, fc * 512 + 512)
                        r_ = rl[fc % 2]
                        P.act(I("activation", out=r_.flat[:, :n], in_=bank.ap[:, :n], func=AF.Relu), reads=[bank], writes=[r_])
                        if fc % 2 == 0:
                            P.dve(I("tensor_tensor", out=u.v[:, fc, :n], in0=r_.flat[:, :n], in1=r_.flat[:, :n], op=ALU.mult), reads=[r_], writes=[uk])
                        else:
                            P.act(I("activation", out=u.v[:, fc, :n], in_=r_.flat[:, :n], func=AF.Square), reads=[r_], writes=[uk])

                def y_phase(it):
                    fg, t0, n, u, wb = it["fg"], it["t0"], it["n"], it["u"], it["wb"]
                    for dc in range(KC):
                        bank = banks[2 + dc % 4]
                        P.pe([I("matmul", out=bank.ap[:, :n], lhsT=wb.v[:, fc, dc * 128:(dc + 1) * 128], rhs=u.v[:, fc, :n], start=(fc == 0), stop=(fc == 3))
                              for fc in range(4)], reads=[wb, u], writes=[bank])
                        ya = yacc.v[:, dc, t0 - g0:t0 - g0 + n]
                        yk = yacc.sub(dc * GSZ + t0 - g0, dc * GSZ + t0 - g0 + n)
                        if fg == 0:
                            P.act(I("copy", out=ya, in_=bank.ap[:, :n]), reads=[bank], writes=[yk])
                        else:
                            P.dve(I("tensor_tensor", out=ya, in0=ya, in1=bank.ap[:, :n], op=ALU.add), reads=[bank, yk], writes=[yk])

                cur = {}
                for i in range(len(its) + 1):
                    if i < len(its):
                        u_phase(its[i])
                    if i >= 1:
                        y_phase(its[i - 1])
                for (t0, n) in subs:
                    col = 0 if t0 < L else 1
                    o = t0 - g0
                    ms_block(lambda k: yacc.v[:, k, o:o + n], n, bMS, sqs, [yacc])
                    rstd_from(bMS, n, rs)
                    for k in range(KC):
                        tmp = tmps[k % 2]
                        P.dve(I("scalar_tensor_tensor", out=tmp.flat[:, :n], in0=yacc.v[:, k, o:o + n], scalar=ab_t[:, 5, k, col:col + 1],
                                in1=rs.flat[:, :n], op0=ALU.mult, op1=ALU.mult),
                              reads=[yacc, rs, "ab"], writes=[tmp])
                        P.pool(I("tensor_tensor", out=xT[:, k, t0:t0 + n], in0=xT[:, k, t0:t0 + n], in1=tmp.flat[:, :n], op=ALU.add),
                               reads=[tmp] + xkk(t0, n), writes=xkk(t0, n))
            arena.release(m0)

        def phase_store():
            m = arena.mark()
            stg = [arena.alloc(D, F32) for _ in range(2)]
            outs = []
            for tt in range(NTL):
                s = stg[tt % 2]
                for half in range(2):
                    bank = banks[(tt % 2) * 2 + half]
                    P.pe([I("transpose", out=bank.ap[:, kk * 128:(kk + 1) * 128], in_=xT[:, half * 4 + kk, tt * 128:(tt + 1) * 128], identity=ident32)
                          for kk in range(4)], reads=xkk(tt * 128, 128) + [CONST], writes=[bank])
                    if half == 0:
                        P.act(I("copy", out=s.flat[:, 0:512], in_=bank.ap[:, :]), reads=[bank], writes=[s.sub(0, 512)])
                    else:
                        P.dve(I("tensor_copy", out=s.flat[:, 512:1024], in_=bank.ap[:, :]), reads=[bank], writes=[s.sub(512, 1024)])
                P.dma("sp", I("dma_start", out=out_d[tt * 128:(tt + 1) * 128, :], in_=s.flat), reads=[s], writes=[("out", tt)])
                outs.append(("out", tt))
            P.op("sp", [], reads=outs)
            arena.release(m)

        def assemble():
            phase_load()
            if stop_after == "load":
                dump(xT[:, 0, :], 0, T, xkk(0, T))
                return
            for l in range(depth):
                ctx_out = l < depth - 1
                phase_mod(l)
                if stop_after == "mod":
                    dump(ab_t[:].rearrange("p a k c -> p (a k c)"), 0, 96, ["ab"])
                    return
                phase_norm_h(0, 1)
                if stop_after == "norm1":
                    dump(hT[:, 0, :], 0, T, hk(0, T))
                    return
                base = arena.mark()
                oT_ssd = arena.alloc((3, T), BF16)
                oT_ret = arena.alloc((2, T), BF16)
                oT_att = arena.alloc((3, T), BF16)
                arena.off = oT_ssd.off + oT_ssd.n * 2
                phase_ssd(l, oT_ssd, ctx_out)
                if stop_after == "ssd":
                    dump(oT_ssd.flat, 0, 3 * T, [oT_ssd])
                    return
                arena.off = oT_ret.off + oT_ret.n * 2
                phase_ret(l, oT_ret, ctx_out)
                if stop_after == "ret":
                    dump(oT_ret.flat, 0, 2 * T, [oT_ret])
                    return
                arena.off = oT_att.off + oT_att.n * 2
                phase_att(l, oT_att, ctx_out)
                if stop_after == "att":
                    dump(oT_att.flat, 0, 3 * T, [oT_att])
                    return
                phase_out(l, [(oT_att, 3), (oT_ret, 2), (oT_ssd, 3)], ctx_out)
                arena.release(base)
                if stop_after == "out":
                    dump(xT[:, 0, 0:128], 0, 128, xkk(0, 128))
                    return
                phase_norm_h(3, 4)
                phase_ffn(l, ctx_out)
                if stop_after == "layer":
                    dump(xT[:, 0, 0:128], 0, 128, xkk(0, 128))
                    return
            phase_store()

        try:
            assemble()
        except _Stop:
            pass
        P.emit(st)
    build.arena_hi = arena.hi
    build.n_ops = len(P.ops)
    build.stats = P.stats
    return nc


def make_in_maps(inp, L, LC, depth, nb):
    con = host_consts()
    rope = host_rope(L, LC)
    par = host_params(inp, depth)
    f32 = lambda a: np.ascontiguousarray(np.asarray(a, dtype=np.float32))
    shared = {
        "par": par, "con": con, "rope": rope,
        "w_mod": f32(inp["w_mod"]), "w_in": f32(inp["w_in"]), "w_out": f32(inp["w_out"]),
        "w_ff1": f32(inp["w_ff1"]), "w_ff2": f32(inp["w_ff2"]),
    }
    maps = []
    for b in range(nb):
        cv = np.zeros((128, KC, 2), np.float32)
        cv[:, :, 0] = np.asarray(inp["c"][b]).reshape(KC, 128).T
        cv[:, :, 1] = np.asarray(inp["c_ctx"]).reshape(KC, 128).T
        m = dict(shared)
        m["x"] = f32(inp["x"][b])
        m["ctx"] = f32(inp["ctx"][b])
        m["cv"] = cv.reshape(128, 16)
        maps.append(m)
    return maps


def kernel(**inputs):
    inp = {k: np.asarray(v) for k, v in inputs.items()}
    B, L, _ = inp["x"].shape
    LC = inp["ctx"].shape[1]
    depth = inp["w_mod"].shape[0]
    nc = build(L, LC, depth)
    maps = make_in_maps(inp, L, LC, depth, B)
    res = run_bass_kernel_spmd(nc, maps, core_ids=list(range(B)))
    return np.stack([np.asarray(r["out"], dtype=np.float32) for r in res.results], axis=0)
```

```python
import numpy as np
from contextlib import ExitStack
import concourse.bass as bass
import concourse.mybir as mybir
from concourse.bass_utils import run_bass_kernel_spmd

F32 = mybir.dt.float32
BF16 = mybir.dt.bfloat16
AF = mybir.ActivationFunctionType
ALU = mybir.AluOpType

D = 1024
KC = 8
D_IN = 2956
D_FF = 4096
EPS = 1e-6
GRID_W = 64
ROPE_THETA = 10000.0

ENGINES = ("pe", "act", "dve", "pool", "sp")
SEM_CHUNK = 1000
DMA_POOL = 8


class Buf:
    def __init__(self, ap, keys):
        self.ap = ap
        self.keys = tuple(keys)


def _keys(items):
    out = []
    for it in items:
        if isinstance(it, Buf):
            out.extend(it.keys)
        elif isinstance(it, (list,)):
            out.extend(_keys(it))
        else:
            out.append(it)
    return out


class Op:
    __slots__ = ("eng", "fn", "dma", "deps", "has_dep", "ms", "dma_idx", "idx")


class Prog:
    def __init__(self, nc):
        self.nc = nc
        self.ops = []
        self.last_writer = {}
        self.readers = {}

    def op(self, eng, fn, reads=(), writes=(), dma=False):
        if isinstance(fn, tuple):
            fn = [fn]
        o = Op()
        o.eng, o.fn, o.dma = eng, fn, dma
        o.has_dep, o.ms, o.dma_idx = False, None, None
        o.idx = len(self.ops)
        rk = _keys(reads)
        wk = _keys(writes)
        pr = [k for k in rk if isinstance(k, tuple) and k and k[0] == "ps"]
        if pr:
            rk = [k for k in rk if not (isinstance(k, tuple) and k and k[0] == "ps")]
            wk = wk + pr
        deps = set()
        lw, rd = self.last_writer, self.readers
        for r in rk:
            w = lw.get(r)
            if w is not None:
                deps.add(w)
        for r in wk:
            w = lw.get(r)
            if w is not None:
                deps.add(w)
            x = rd.get(r)
            if x:
                deps.update(x)
        for r in rk:
            rd.setdefault(r, []).append(o.idx)
        for r in wk:
            lw[r] = o.idx
            rd[r] = []
        deps.discard(o.idx)
        if eng == "pe":
            deps = {d for d in deps if self.ops[d].eng != "pe"}
        o.deps = deps
        self.ops.append(o)
        return o

    def pe(self, fn, reads=(), writes=()):
        return self.op("pe", fn, reads, writes)

    def act(self, fn, reads=(), writes=()):
        return self.op("act", fn, reads, writes)

    def dve(self, fn, reads=(), writes=()):
        return self.op("dve", fn, reads, writes)

    def pool(self, fn, reads=(), writes=()):
        return self.op("pool", fn, reads, writes)

    def dma(self, eng, fn, reads=(), writes=()):
        return self.op(eng, fn, reads, writes, dma=True)

    def emit(self, stack):
        nc = self.nc
        ops = self.ops
        for o in ops:
            for d in o.deps:
                ops[d].has_dep = True
        cnt = {e: 0 for e in ENGINES}
        dcnt = {e: 0 for e in ENGINES}
        for o in ops:
            if o.dma:
                o.dma_idx = dcnt[o.eng]
                dcnt[o.eng] += 1
            elif o.has_dep:
                cnt[o.eng] += 1
                o.ms = cnt[o.eng]
        self.stats = dict(ms=dict(cnt), dma=dict(dcnt), n_ops={e: sum(1 for o in ops if o.eng == e) for e in ENGINES})
        esems, dsems = {}, {}
        for e in ENGINES:
            n = (cnt[e] + SEM_CHUNK - 1) // SEM_CHUNK
            esems[e] = [stack.enter_context(nc.semaphore(f"c_{e}_{i}")) for i in range(n)]
            n = min(DMA_POOL, dcnt[e])
            dsems[e] = [stack.enter_context(nc.semaphore(f"d_{e}_{i}")) for i in range(n)]

        def sem_of(o):
            if o.dma:
                return dsems[o.eng][o.dma_idx % DMA_POOL], 16 * (o.dma_idx // DMA_POOL + 1)
            m = o.ms - 1
            return esems[o.eng][m // SEM_CHUNK], (m % SEM_CHUNK) + 1

        by_eng = {e: [o for o in ops if o.eng == e] for e in ENGINES}
        block = stack.enter_context(nc.Block())

        def run(e, eng):
            waited = {}
            for o in by_eng[e]:
                need = {}
                for d in o.deps:
                    s, v = sem_of(ops[d])
                    k = id(s)
                    if need.get(k, (None, 0))[1] < v:
                        need[k] = (s, v)
                if o.dma and o.dma_idx >= DMA_POOL:
                    s = dsems[e][o.dma_idx % DMA_POOL]
                    v = 16 * (o.dma_idx // DMA_POOL)
                    k = id(s)
                    if need.get(k, (None, 0))[1] < v:
                        need[k] = (s, v)
                for k, (s, v) in need.items():
                    if waited.get(k, 0) < v:
                        eng.wait_ge(s, v)
                        waited[k] = v
                inst = None
                for (mname, kw) in o.fn:
                    inst = getattr(eng, mname)(**kw)
                if inst is None:
                    continue
                if o.dma:
                    inst.then_inc(sem_of(o)[0], 16)
                elif o.ms is not None:
                    inst.then_inc(sem_of(o)[0], 1)

        if by_eng["pe"]:
            block.tensor(lambda eng: run("pe", eng))
        if by_eng["act"]:
            block.scalar(lambda eng: run("act", eng))
        if by_eng["dve"]:
            block.vector(lambda eng: run("dve", eng))
        if by_eng["pool"]:
            block.gpsimd(lambda eng: run("pool", eng))
        if by_eng["sp"]:
            block.sync(lambda eng: run("sp", eng))


SLOT = 512


class ABuf(Buf):
    def __init__(self, flat, off, esz, shape):
        self.flat = flat
        self.off = off
        self.esz = esz
        self.shape = tuple(shape)
        n = int(np.prod(shape))
        self.n = n
        keys = [("A", s) for s in range(off // SLOT, (off + n * esz - 1) // SLOT + 1)]
        if len(shape) == 1:
            v = flat
        elif len(shape) == 2:
            v = flat.rearrange("p (a b) -> p a b", a=shape[0])
        elif len(shape) == 3:
            v = flat.rearrange("p (a b c) -> p a b c", a=shape[0], b=shape[1])
        elif len(shape) == 4:
            v = flat.rearrange("p (a b c d) -> p a b c d", a=shape[0], b=shape[1], c=shape[2])
        else:
            raise ValueError(shape)
        self.v = v
        Buf.__init__(self, v, keys)

    def sub(self, lo, hi):
        b0 = self.off + lo * self.esz
        b1 = self.off + hi * self.esz
        keys = [("A", s) for s in range(b0 // SLOT, (b1 - 1) // SLOT + 1)]
        return Buf(self.flat[:, lo:hi], keys)


class Arena:
    def __init__(self, nc, st, nbytes):
        self.nbytes = nbytes
        self.t = st.enter_context(nc.sbuf_tensor("arena", [128, nbytes // 2], BF16))
        self.off = 0
        self.hi = 0

    def alloc(self, shape, dt):
        if isinstance(shape, int):
            shape = (shape,)
        esz = 4 if dt == F32 else 2
        n = int(np.prod(shape))
        nb = n * esz
        off = (self.off + 63) // 64 * 64
        assert off + nb <= self.nbytes, f"arena overflow: need {off + nb} of {self.nbytes}"
        self.off = off + nb
        self.hi = max(self.hi, self.off)
        flat = self.t[:, off // 2:(off + nb) // 2]
        if dt == F32:
            flat = flat.bitcast(F32)
        return ABuf(flat, off, esz, shape)

    def mark(self):
        return self.off

    def release(self, m):
        self.off = m


C_ID, C_UF, C_UB, C_ONE, C_BLK, C_ROT, C_OND, C_MF, C_MB, C_HM = range(10)
NCON = 10 * 128


def host_consts():
    c = np.zeros((128, NCON), np.float32)
    i = np.arange(128)
    c[:, C_ID * 128:(C_ID + 1) * 128] = np.eye(128)
    c[:, C_UF * 128:(C_UF + 1) * 128] = (i[:, None] <= i[None, :])
    c[:, C_UB * 128:(C_UB + 1) * 128] = (i[:, None] >= i[None, :])
    c[:, C_ONE * 128:(C_ONE + 1) * 128] = 1.0
    blk = (i[:, None] // 64 == i[None, :] // 64).astype(np.float32) / 64.0
    c[:, C_BLK * 128:(C_BLK + 1) * 128] = blk
    rot = np.zeros((128, 128), np.float32)
    for d in range(128):
        if d % 32 < 16:
            rot[d + 16, d] = -1.0
        else:
            rot[d - 16, d] = 1.0
    c[:, C_ROT * 128:(C_ROT + 1) * 128] = rot
    c[:, C_OND * 128:(C_OND + 1) * 128] = 1.0 / 1024.0
    c[:, C_MF * 128:(C_MF + 1) * 128] = (i[None, :] >= i[:, None])
    c[:, C_MB * 128:(C_MB + 1) * 128] = (i[None, :] < i[:, None])
    hm = np.zeros((128, 128), np.float32)
    hm[:64, 0] = 1.0
    hm[64:, 1] = 1.0
    c[:, C_HM * 128:(C_HM + 1) * 128] = hm
    return c


def host_rope(L, LC):
    T = L + LC
    rows = L // GRID_W
    row = np.broadcast_to(np.arange(rows)[:, None], (rows, GRID_W)).reshape(L)
    col = np.broadcast_to(np.arange(GRID_W)[None, :], (rows, GRID_W)).reshape(L)
    half = 32
    inv_freq = (ROPE_THETA ** (-np.arange(0, half, 2, dtype=np.float32) / half)).astype(np.float32)
    ang = np.stack([row, col], axis=-1).astype(np.float32)[:, :, None] * inv_freq
    ang = np.concatenate([ang, ang], axis=-1).reshape(L, 64)
    cos = np.ones((64, T), np.float32)
    sin = np.zeros((64, T), np.float32)
    cos[:, :L] = np.cos(ang).T
    sin[:, :L] = np.sin(ang).T
    tab = np.zeros((128, 2, T), np.float32)
    tab[:64, 0] = cos
    tab[64:, 0] = cos
    tab[:64, 1] = sin
    tab[64:, 1] = sin
    return tab.reshape(128, 2 * T)


PO_BMOD = 0
PO_NG = PO_BMOD + 48
PO_QG = PO_NG + 32
PO_KG = PO_QG + 1
PO_RGG = PO_KG + 1
PO_RGB = PO_RGG + 2
PO_RLOG = PO_RGB + 2
PO_CW = PO_RLOG + 8
PO_CB = PO_CW + 35
PO_DTB = PO_CB + 7
PO_ALOG = PO_DTB + 12
PO_SD = PO_ALOG + 12
PO_SNG = PO_SD + 6
NPAR = PO_SNG + 3


def host_params(inp, depth):
    par = np.zeros((128, depth, NPAR), np.float32)
    fm = lambda v: np.ascontiguousarray(v.reshape(-1, 128).T)
    for l in range(depth):
        p = par[:, l]
        p[:, PO_BMOD:PO_BMOD + 48] = fm(inp["b_mod"][l])
        p[:, PO_NG:PO_NG + 32] = fm(inp["norm_g"][l].reshape(-1))
        p[:, PO_QG] = np.tile(inp["q_norm_g"][l], 2)
        p[:, PO_KG] = np.tile(inp["k_norm_g"][l], 2)
        p[:, PO_RGG:PO_RGG + 2] = fm(inp["ret_gn_g"][l])
        p[:, PO_RGB:PO_RGB + 2] = fm(inp["ret_gn_b"][l])
        p[:, PO_RLOG:PO_RLOG + 8] = np.broadcast_to(inp["ret_decay_logit"][l].reshape(1, 8), (128, 8))
        cw = inp["ssd_conv_w"][l]
        for cc in range(7):
            p[:, PO_CW + cc * 5:PO_CW + cc * 5 + 5] = cw[:, cc * 128:(cc + 1) * 128].T
        p[:, PO_CB:PO_CB + 7] = fm(inp["ssd_conv_b"][l])
        p[:, PO_DTB:PO_DTB + 12] = np.broadcast_to(inp["ssd_dt_bias"][l].reshape(1, 12), (128, 12))
        p[:, PO_ALOG:PO_ALOG + 12] = np.broadcast_to(inp["ssd_a_log"][l].reshape(1, 12), (128, 12))
        p[:, PO_SD:PO_SD + 6] = np.broadcast_to(inp["ssd_d"][l].reshape(1, 6), (128, 6))
        p[:, PO_SNG:PO_SNG + 3] = fm(inp["ssd_norm_g"][l])
    return par.reshape(128, depth * NPAR)


CO_QA, CO_KA, CO_VA = 0, 384, 512
CO_QR, CO_KR, CO_VR, CO_GR = 640, 896, 1152, 1408
CO_Z, CO_XBC, CO_DT = 1664, 2048, 2944


def I(m, **kw):
    return (m, kw)


class _Stop(Exception):
    pass


def build(L, LC, depth, stop_after=None, dbg=None):
    T = L + LC
    NT = T // 128
    NTL = L // 128
    blocks = [(i * 512, 512) for i in range(L // 512)] + [(L, LC)]
    lat_chunks = list(range(NTL))
    ctx_chunks = list(range(NTL, NT))
    fwd_order = ctx_chunks + lat_chunks
    bwd_order = ctx_chunks[::-1] + lat_chunks[::-1]

    nc = bass.Bass("TRN2", target_bir_lowering=False)
    dt_in = lambda name, shape: nc.dram_tensor(name, shape, F32, kind="ExternalInput").ap()
    x_d = dt_in("x", [L, D])
    ctx_d = dt_in("ctx", [LC, D])
    cv_d = dt_in("cv", [128, 16])
    par_d = dt_in("par", [128, depth * NPAR])
    con_d = dt_in("con", [128, NCON])
    rope_d = dt_in("rope", [128, 2 * T])
    wmod_d = dt_in("w_mod", [depth, D, 6 * D])
    win_d = dt_in("w_in", [depth, D, D_IN])
    wout_d = dt_in("w_out", [depth, D, D])
    w1_d = dt_in("w_ff1", [depth, D, D_FF])
    w2_d = dt_in("w_ff2", [depth, D_FF, D])
    out_d = nc.dram_tensor("out", [L, D], F32, kind="ExternalOutput").ap()
    dbg_d = None
    if dbg is not None:
        dbg_d = nc.dram_tensor("dbg", [128, dbg], F32, kind="ExternalOutput").ap()

    P = Prog(nc)
    st = ExitStack()
    with st:
        sbt = lambda name, shape, dt: st.enter_context(nc.sbuf_tensor(name, shape, dt))
        xT = sbt("xT", [128, KC, T], F32)
        hT = sbt("hT", [128, KC, T], BF16)
        c32_t = sbt("c32", [128, 4, 128], F32)
        cb_t = sbt("cb", [128, 6, 128], BF16)
        idb_t = sbt("idb", [128, 128], BF16)
        one_b_t = sbt("oneb", [128, 128], BF16)
        par_t = sbt("par_sb", [128, depth, NPAR], F32)
        ab_t = sbt("ab", [128, 6, KC, 2], F32)
        dI_t = sbt("dI", [128, 6, 128], BF16)
        sml_t = sbt("sml", [128, 64], F32)
        arena = Arena(nc, st, (nc.sbuf_bytes_remaining - 1024) // 64 * 64)
        banks = [Buf(st.enter_context(nc.psum_tensor(f"ps{i}", [128, 512], F32)), [("ps", i)]) for i in range(8)]

        ident32 = c32_t[:, 0, :]
        Uf32 = c32_t[:, 1, :]
        Ub32 = c32_t[:, 2, :]
        ones32 = c32_t[:, 3, :]
        blk64b = cb_t[:, 0, :]
        rotb = cb_t[:, 1, :]
        onesDb = cb_t[:, 2, :]
        mFb = cb_t[:, 3, :]
        mBb = cb_t[:, 4, :]
        hmb = cb_t[:, 5, 0:2]
        identb = idb_t[:, :]
        ones1b = one_b_t[:, :]
        CONST = "const"

        def hk(t0, n):
            return [("hT", t) for t in range(t0 // 128, (t0 + n + 127) // 128)]

        def xkk(t0, n):
            r = []
            for t in range(t0 // 128, (t0 + n + 127) // 128):
                r += [("xT", t, 0), ("xT", t, 1)]
            return r

        def b16(bank):
            return bank.ap[:].bitcast(BF16)

        P.dma("sp", I("dma_start", out=c32_t[:, 0:3, :], in_=con_d[:, 0:384].rearrange("p (a b) -> p a b", a=3)), writes=[CONST])
        P.dma("sp", I("dma_start", out=c32_t[:, 3, :], in_=con_d[:, C_ONE * 128:(C_ONE + 1) * 128]), writes=[CONST])
        P.dma("sp", I("dma_start", out=par_t[:], in_=par_d.rearrange("p (l n) -> p l n", l=depth)), writes=["par"])
        P.dma("pool", I("dma_start", out=cb_t[:], in_=con_d[:, C_BLK * 128:(C_HM + 1) * 128].rearrange("p (a b) -> p a b", a=6)), writes=[CONST])
        P.dma("pool", I("dma_start", out=idb_t[:], in_=con_d[:, C_ID * 128:(C_ID + 1) * 128]), writes=[CONST])
        P.dma("pool", I("dma_start", out=one_b_t[:], in_=con_d[:, C_ONE * 128:(C_ONE + 1) * 128]), writes=[CONST])

        def dump(ap_, col0, ncols, reads):
            if dbg_d is None:
                return
            if ap_.dtype != F32:
                tmp = arena.alloc(ncols, F32)
                P.dve(I("tensor_copy", out=tmp.flat, in_=ap_), reads=reads, writes=[tmp])
                ap_, reads = tmp.flat, [tmp]
            P.dma("sp", I("dma_start", out=dbg_d[:, col0:col0 + ncols], in_=ap_), reads=reads, writes=[("dbg", col0)])
            P.op("sp", [], reads=[("dbg", col0)])

        def phase_load():
            m = arena.mark()
            stg = [arena.alloc(D, F32) for _ in range(2)]
            for tt in range(NT):
                s = stg[tt % 2]
                src = x_d[tt * 128:(tt + 1) * 128, :] if tt < NTL else ctx_d[(tt - NTL) * 128:(tt - NTL + 1) * 128, :]
                P.dma("sp", I("dma_start", out=s.flat, in_=src), writes=[s])
                for half in range(2):
                    bank = banks[(tt % 2) * 2 + half]
                    P.pe([I("transpose", out=bank.ap[:, kk * 128:(kk + 1) * 128], in_=s.flat[:, (half * 4 + kk) * 128:(half * 4 + kk + 1) * 128], identity=ident32)
                          for kk in range(4)], reads=[s, CONST], writes=[bank])
                    dst = xT[:, half * 4:(half + 1) * 4, tt * 128:(tt + 1) * 128]
                    srcp = bank.ap[:].rearrange("p (k n) -> p k n", k=4)
                    if half == 0:
                        P.act(I("copy", out=dst, in_=srcp), reads=[bank], writes=[("xT", tt, half)])
                    else:
                        P.dve(I("tensor_copy", out=dst, in_=srcp), reads=[bank], writes=[("xT", tt, half)])
            arena.release(m)

        def phase_mod(l):
            m = arena.mark()
            cv32 = arena.alloc((KC, 2), F32)
            cvb = arena.alloc((KC, 2), BF16)
            P.dma("sp", I("dma_start", out=cv32.flat, in_=cv_d), writes=[cv32])
            P.act(I("activation", out=cvb.flat, in_=cv32.flat, func=AF.Silu), reads=[cv32], writes=[cvb])
            wm = [arena.alloc((KC, 768), BF16) for _ in range(2)]
            bank = banks[7]
            for piece in range(8):
                w = wm[piece % 2]
                P.dma("pool", I("dma_start", out=w.v, in_=wmod_d[l][:, piece * 768:(piece + 1) * 768].rearrange("(k p) n -> p k n", p=128)), writes=[w])
                ins = []
                for jj in range(6):
                    j = piece * 6 + jj
                    for k in range(KC):
                        ins.append(I("matmul", out=bank.ap[:, 2 * j:2 * j + 2], lhsT=w.v[:, k, jj * 128:(jj + 1) * 128], rhs=cvb.v[:, k, :],
                                     start=(k == 0), stop=(k == KC - 1)))
                P.pe(ins, reads=[w, cvb], writes=[bank])
            modT = arena.alloc((48, 2), F32)
            par = par_t[:, l, :]
            P.dve(I("tensor_tensor", out=modT.v, in0=bank.ap[:, 0:96].rearrange("p (j c) -> p j c", c=2),
                    in1=par[:, PO_BMOD:PO_BMOD + 48].unsqueeze(2).to_broadcast([128, 48, 2]), op=ALU.add),
                  reads=[bank, "par"], writes=[modT])
            ng = lambda f: par[:, PO_NG + f * 8:PO_NG + f * 8 + 8].unsqueeze(2).to_broadcast([128, KC, 2])
            mv = lambda i: modT.v[:, i * 8:(i + 1) * 8, :]
            P.dve([I("scalar_tensor_tensor", out=ab_t[:, 0], in0=mv(1), scalar=1.0, in1=ng(0), op0=ALU.add, op1=ALU.mult),
                   I("tensor_copy", out=ab_t[:, 1], in_=mv(0)),
                   I("tensor_tensor", out=ab_t[:, 2], in0=mv(2), in1=ng(1), op=ALU.mult),
                   I("scalar_tensor_tensor", out=ab_t[:, 3], in0=mv(4), scalar=1.0, in1=ng(2), op0=ALU.add, op1=ALU.mult),
                   I("tensor_copy", out=ab_t[:, 4], in_=mv(3)),
                   I("tensor_tensor", out=ab_t[:, 5], in0=mv(5), in1=ng(3), op=ALU.mult)],
                  reads=[modT, "par"], writes=["ab"])
            arena.release(m)

        def ms_block(src_k, n, bank, sqs, reads):
            for k in range(KC):
                sq = sqs[k % 2]
                P.act(I("activation", out=sq.flat[:, :n], in_=src_k(k), func=AF.Square), reads=reads, writes=[sq])
                P.pe(I("matmul", out=bank.ap[:, :n], lhsT=onesDb, rhs=sq.flat[:, :n], start=(k == 0), stop=(k == KC - 1)),
                     reads=[sq, CONST], writes=[bank])

        def rstd_from(bank, n, rs, scale=None):
            P.act(I("activation", out=rs.flat[:, :n], in_=bank.ap[:, :n], func=AF.Ln, bias=EPS, scale=(1.0 if scale is None else scale)),
                  reads=[bank], writes=[rs])
            P.act(I("activation", out=rs.flat[:, :n], in_=rs.flat[:, :n], func=AF.Exp, scale=-0.5), reads=[rs], writes=[rs])

        def sigmoid_from(bank, n, dst):
            P.act(I("activation", out=dst.flat[:, :n], in_=bank.ap[:, :n], func=AF.Exp, scale=-1.0), reads=[bank], writes=[dst])
            P.act(I("activation", out=dst.flat[:, :n], in_=dst.flat[:, :n], func=AF.Ln, bias=1.0, scale=1.0), reads=[dst], writes=[dst])
            P.act(I("activation", out=dst.flat[:, :n], in_=dst.flat[:, :n], func=AF.Exp, scale=-1.0), reads=[dst], writes=[dst])

        def phase_norm_h(ai, bi_):
            m = arena.mark()
            sqs = [arena.alloc(512, BF16) for _ in range(2)]
            rs = arena.alloc(512, F32)
            tmps = [arena.alloc(512, F32) for _ in range(2)]
            bank = banks[6]
            for (t0, n) in blocks:
                col = 0 if t0 < L else 1
                ms_block(lambda k: xT[:, k, t0:t0 + n], n, bank, sqs, xkk(t0, n))
                rstd_from(bank, n, rs)
                for k in range(KC):
                    tmp = tmps[k % 2]
                    P.dve(I("scalar_tensor_tensor", out=tmp.flat[:, :n], in0=xT[:, k, t0:t0 + n], scalar=ab_t[:, ai, k, col:col + 1],
                            in1=rs.flat[:, :n], op0=ALU.mult, op1=ALU.mult),
                          reads=xkk(t0, n) + [rs, "ab"], writes=[tmp])
                    P.act(I("activation", out=hT[:, k, t0:t0 + n], in_=tmp.flat[:, :n], func=AF.Identity,
                            bias=ab_t[:, bi_, k, col:col + 1], scale=1.0),
                          reads=[tmp, "ab"], writes=hk(t0, n))
            arena.release(m)

        def load_w(dram_ap, buf):
            P.dma("pool", I("dma_start", out=buf.v, in_=dram_ap.rearrange("(k p) n -> p k n", p=128)), writes=[buf])

        def proj_fm(bank, w, c0, M, t0, n):
            P.pe([I("matmul", out=bank.ap[0:M, :n], lhsT=w.v[:, k, c0:c0 + M], rhs=hT[:, k, t0:t0 + n], start=(k == 0), stop=(k == KC - 1))
                  for k in range(KC)], reads=[w] + hk(t0, n), writes=[bank])

        def scan(H, hg, la_of, dt_of, kt_chunk, n_kt, kidx, qg_of, vtok_of, extra_terms, extra_reads, finish, dbuf, shared=None, skip_out=(), pre=None):
            NG = H // hg
            HP = H * 64
            units = [(0, min(H, 4))] + ([(4, H - 4)] if H > 4 else [])
            NU = len(units)
            NPAR = 2 if dbuf else 1
            bS, bG, bT, bY = banks[0], banks[3], banks[4], banks[5]
            pbanks = [banks[1], banks[2], banks[6], banks[7]]
            bZ, bF = bG, bS
            m = arena.mark()
            sb_store = arena.alloc((NT, HP), BF16)
            Sst = [arena.alloc(HP, F32) for _ in range(2)]
            sfb = arena.alloc(HP, BF16)
            mk2 = lambda shape, dt: [[arena.alloc(shape, dt) for _ in range(2)] for _ in range(NPAR)]
            class _Pair:
                def __init__(self, buf, aps):
                    self.buf, self.aps = buf, aps

                def __getitem__(self, d):
                    return Buf(self.aps[d], self.buf.keys)
            ptc = [arena.alloc((4, H), F32) for _ in range(NPAR)]
            difc = [arena.alloc((2, H), F32) for _ in range(NPAR)]
            wvc = [arena.alloc((2, H), F32) for _ in range(NPAR)]
            decc = [arena.alloc((2, H), F32) for _ in range(NPAR)]
            cwc = [arena.alloc((2, H), F32) for _ in range(NPAR)]
            Vdt = mk2(HP, BF16) if dt_of(0) is not None else None
            E = mk2((H, 128), BF16)
            Ebc = mk2((H, 128), BF16)
            Gm = mk2((NG, 128), BF16)
            MT, QsT = E, Ebc
            Vw = shared if shared is not None else arena.alloc(HP, BF16)
            ktok = arena.alloc((n_kt, 128), BF16)
            R = [[arena.alloc((hn, 128), F32) for (_, hn) in units] for _ in range(2)]
            Ud = [Uf32, Ub32]
            md = [mFb, mBb]
            hq = lambda ap_: ap_.rearrange("p (h q) -> p h q", h=H)
            g4 = lambda ap_: ap_.rearrange("p (g a) i -> p g a i", g=NG)

            def small(c, d, p):
                lap, lar = la_of(c)
                P.pe([I("matmul", out=bS.ap[:, 0:H], lhsT=Ud[d], rhs=lap[:, d, :], start=True, stop=True),
                      I("matmul", out=bS.ap[:, H:2 * H], lhsT=ones32, rhs=lap[:, d, :], start=True, stop=True)],
                     reads=[CONST] + lar, writes=[bS])
                P.act([I("copy", out=ptc[p].v[:, d, :], in_=bS.ap[:, 0:H]), I("copy", out=ptc[p].v[:, 2 + d, :], in_=bS.ap[:, H:2 * H])],
                      reads=[bS], writes=[ptc[p]])
                P.dve(I("tensor_tensor", out=difc[p].v[:, d, :], in0=ptc[p].v[:, 2 + d, :], in1=ptc[p].v[:, d, :], op=ALU.subtract),
                      reads=[ptc[p]], writes=[difc[p]])
                P.act(I("activation", out=wvc[p].v[:, d, :], in_=difc[p].v[:, d, :], func=AF.Exp), reads=[difc[p]], writes=[wvc[p]])
                P.act(I("activation", out=decc[p].v[:, d, :], in_=ptc[p].v[:, 2 + d, :], func=AF.Exp), reads=[ptc[p]], writes=[decc[p]])
                dtv = dt_of(c)
                if dtv is not None:
                    P.dve(I("tensor_tensor", out=cwc[p].v[:, d, :], in0=wvc[p].v[:, d, :], in1=dtv[0][:, d, :], op=ALU.mult),
                          reads=[wvc[p]] + dtv[1], writes=[cwc[p]])
                else:
                    P.dve(I("tensor_copy", out=cwc[p].v[:, d, :], in_=wvc[p].v[:, d, :]), reads=[wvc[p]], writes=[cwc[p]])

            def small2(c, p):
                lap, lar = la_of(c)
                P.pe([I("matmul", out=bS.ap[:, 0:H], lhsT=Ud[0], rhs=lap[:, 0, :], start=True, stop=True),
                      I("matmul", out=bS.ap[:, H:2 * H], lhsT=Ud[1], rhs=lap[:, 1, :], start=True, stop=True),
                      I("matmul", out=bS.ap[:, 2 * H:4 * H], lhsT=ones32, rhs=lap.rearrange("p d h -> p (d h)"), start=True, stop=True)],
                     reads=[CONST] + lar, writes=[bS])
                P.act(I("copy", out=ptc[p].flat, in_=bS.ap[:, 0:4 * H]), reads=[bS], writes=[ptc[p]])
                P.dve(I("tensor_tensor", out=difc[p].flat, in0=ptc[p].flat[:, 2 * H:4 * H], in1=ptc[p].flat[:, 0:2 * H], op=ALU.subtract),
                      reads=[ptc[p]], writes=[difc[p]])
                P.act(I("activation", out=wvc[p].flat, in_=difc[p].flat, func=AF.Exp), reads=[difc[p]], writes=[wvc[p]])
                P.act(I("activation", out=decc[p].flat, in_=ptc[p].flat[:, 2 * H:4 * H], func=AF.Exp), reads=[ptc[p]], writes=[decc[p]])
                dtv = dt_of(c)
                if dtv is not None:
                    P.dve(I("tensor_tensor", out=cwc[p].flat, in0=wvc[p].flat, in1=dtv[0].rearrange("p d h -> p (d h)"), op=ALU.mult),
                          reads=[wvc[p]] + dtv[1], writes=[cwc[p]])
                else:
                    P.dve(I("tensor_copy", out=cwc[p].flat, in_=wvc[p].flat), reads=[wvc[p]], writes=[cwc[p]])

            pt = [_Pair(ptc[p], [ptc[p].v[:, 0, :], ptc[p].v[:, 1, :]]) for p in range(NPAR)]
            dec = [_Pair(decc[p], [decc[p].v[:, 0, :], decc[p].v[:, 1, :]]) for p in range(NPAR)]
            cw = [_Pair(cwc[p], [cwc[p].v[:, 0, :], cwc[p].v[:, 1, :]]) for p in range(NPAR)]

            def dstate(c, d, S, p):
                vt, vr = vtok_of(c)
                P.dve(I("tensor_tensor", out=hq(Vw.flat), in0=hq(vt), in1=cw[p][d].ap.unsqueeze(2).to_broadcast([128, H, 64]), op=ALU.mult),
                      reads=vr + [cw[p][d]], writes=[Vw])
                bt16 = b16(bT)
                P.pe([I("transpose", out=bt16[:, i * 128:(i + 1) * 128], in_=kt_chunk(i, c)[0], identity=identb) for i in range(n_kt)],
                     reads=[CONST] + kt_chunk(0, c)[1], writes=[bT])
                P.act(I("copy", out=ktok.flat, in_=bt16[:, 0:n_kt * 128]), reads=[bT], writes=[ktok])
                P.pe([I("matmul", out=bT.ap[:, g * hg * 64:(g + 1) * hg * 64], lhsT=ktok.v[:, kidx(g), :], rhs=Vw.flat[:, g * hg * 64:(g + 1) * hg * 64],
                        start=True, stop=True) for g in range(NG)], reads=[ktok, Vw], writes=[bT])
                P.dve(I("tensor_tensor", out=hq(S.flat), in0=hq(S.flat), in1=dec[p][d].ap.unsqueeze(2).to_broadcast([128, H, 64]), op=ALU.mult),
                      reads=[S, dec[p][d]], writes=[S])
                P.dve(I("tensor_tensor", out=S.flat, in0=S.flat, in1=bT.ap[:, 0:HP], op=ALU.add), reads=[S, bT], writes=[S])

            P.pool(I("memset", ap=Sst[1].flat, constant=0.0), writes=[Sst[1]])
            P.pool(I("memset", ap=Sst[0].flat, constant=0.0), writes=[Sst[0]])
            for c in bwd_order:
                P.act(I("copy", out=sb_store.v[:, c, :], in_=Sst[1].flat), reads=[Sst[1]], writes=[sb_store.sub(c * HP, (c + 1) * HP)])
                small(c, 1, 0)
                dstate(c, 1, Sst[1], 0)

            loc = {}

            def local_part(c, p):
                qg, qgr = qg_of(c, p)
                lap, lar = la_of(c)
                loc[c] = (qg, qgr)
                P.pe([I("matmul", out=bG.ap[:, g * 128:(g + 1) * 128], lhsT=kt_chunk(kidx(g), c)[0], rhs=qg[:, g, :], start=True, stop=True)
                      for g in range(NG)], reads=kt_chunk(0, c)[1] + qgr, writes=[bG])
                for d in range(2):
                    P.dve(I("tensor_tensor", out=Gm[p][d].v, in0=bG.ap[:, 0:NG * 128].rearrange("p (g i) -> p g i", g=NG),
                            in1=md[d].unsqueeze(1).to_broadcast([128, NG, 128]), op=ALU.mult),
                          reads=[bG, CONST], writes=[Gm[p][d]])
                small2(c, p)
                chains = [(d, ui, h0, hn) for d in range(2) for ui, (h0, hn) in enumerate(units)]
                bk = lambda d, ui: pbanks[(d * NU + ui) % 4]
                for (d, ui, h0, hn) in chains:
                    P.dve(I("tensor_tensor", out=R[d][ui].v, in0=Ud[d].unsqueeze(1).to_broadcast([128, hn, 128]),
                            in1=lap[:, d, h0:h0 + hn].unsqueeze(2).to_broadcast([128, hn, 128]), op=ALU.mult),
                          reads=[CONST] + lar, writes=[R[d][ui]])
                for (d, ui, h0, hn) in chains:
                    P.pe(I("matmul", out=bk(d, ui).ap[:, 0:hn * 128], lhsT=ones32, rhs=R[d][ui].flat, start=True, stop=True),
                         reads=[CONST, R[d][ui]], writes=[bk(d, ui)])
                for (d, ui, h0, hn) in chains:
                    P.act(I("activation", out=Ebc[p][d].flat[:, h0 * 128:(h0 + hn) * 128], in_=bk(d, ui).ap[:, 0:hn * 128], func=AF.Exp),
                          reads=[bk(d, ui)], writes=[Ebc[p][d].sub(h0 * 128, (h0 + hn) * 128)])
                    P.dve(I("tensor_tensor", out=R[d][ui].v, in0=bk(d, ui).ap[:, 0:hn * 128].rearrange("p (h i) -> p h i", h=hn),
                            in1=pt[p][d].ap[:, h0:h0 + hn].unsqueeze(2).to_broadcast([128, hn, 128]), op=ALU.subtract),
                          reads=[bk(d, ui), pt[p][d]], writes=[R[d][ui]])
                for (d, ui, h0, hn) in chains:
                    P.dve(I("tensor_tensor", out=R[d][ui].v, in0=R[d][ui].v, in1=md[d].unsqueeze(1).to_broadcast([128, hn, 128]), op=ALU.mult),
                          reads=[R[d][ui], CONST], writes=[R[d][ui]])
                for (d, ui, h0, hn) in chains:
                    P.act(I("activation", out=E[p][d].flat[:, h0 * 128:(h0 + hn) * 128], in_=R[d][ui].flat, func=AF.Exp),
                          reads=[R[d][ui]], writes=[E[p][d].sub(h0 * 128, (h0 + hn) * 128)])
                vt, vr = vtok_of(c)
                dtv = dt_of(c)
                for d in range(2):
                    P.dve(I("tensor_tensor", out=g4(QsT[p][d].v), in0=g4(Ebc[p][d].v), in1=qg.unsqueeze(2).to_broadcast([128, NG, hg, 128]), op=ALU.mult),
                          reads=[Ebc[p][d]] + qgr, writes=[QsT[p][d]])
                    if dtv is not None:
                        P.pool(I("tensor_tensor", out=hq(Vdt[p][d].flat), in0=hq(vt), in1=dtv[0][:, d, :].unsqueeze(2).to_broadcast([128, H, 64]), op=ALU.mult),
                               reads=vr + dtv[1], writes=[Vdt[p][d]])
                for d in range(2):
                    P.dve(I("tensor_tensor", out=g4(MT[p][d].v), in0=g4(E[p][d].v), in1=Gm[p][d].v.unsqueeze(2).to_broadcast([128, NG, hg, 128]), op=ALU.mult),
                          reads=[E[p][d], Gm[p][d]], writes=[MT[p][d]])
                if pre is not None:
                    pre(c, p, bG)

            def state_part(c, p):
                vt, vr = vtok_of(c)
                dtv = dt_of(c)
                ins = []
                for h in range(H):
                    out = bY.ap[(h % 2) * 64:(h % 2) * 64 + 64, (h // 2) * 128:(h // 2 + 1) * 128]
                    terms = []
                    for d in range(2):
                        lhs = Vdt[p][d].flat[:, h * 64:(h + 1) * 64] if dtv is not None else vt[:, h * 64:(h + 1) * 64]
                        terms.append((lhs, MT[p][d].v[:, h, :]))
                    terms.append((sfb.flat[:, h * 64:(h + 1) * 64], QsT[p][0].v[:, h, :]))
                    terms.append((sb_store.v[:, c, h * 64:(h + 1) * 64], QsT[p][1].v[:, h, :]))
                    terms += extra_terms(c, h)
                    for i, (lh, rh) in enumerate(terms):
                        ins.append(I("matmul", out=out, lhsT=lh, rhs=rh, start=(i == 0), stop=(i == len(terms) - 1)))
                P.pe(ins, reads=([Vdt[p][0], Vdt[p][1]] if dtv is not None else []) + [MT[p][0], MT[p][1], QsT[p][0], QsT[p][1], sfb, sb_store.sub(c * HP, (c + 1) * HP)] + vr + extra_reads,
                     writes=[bY])
                dstate(c, 0, Sst[0], p)
                P.act(I("copy", out=sfb.flat, in_=Sst[0].flat), reads=[Sst[0]], writes=[sfb])
                finish(c, bY, bF, p)

            order = []
            for c in fwd_order:
                if c in skip_out:
                    small(c, 0, 0)
                    dstate(c, 0, Sst[0], 0)
                else:
                    order.append(c)
            P.act(I("copy", out=sfb.flat, in_=Sst[0].flat), reads=[Sst[0]], writes=[sfb])
            if dbuf:
                local_part(order[0], 0)
                for i, c in enumerate(order):
                    if i + 1 < len(order):
                        local_part(order[i + 1], (i + 1) % 2)
                    state_part(c, i % 2)
            else:
                for c in order:
                    local_part(c, 0)
                    state_part(c, 0)
            arena.release(m)

        def phase_ssd(l, oT_ssd, ctx_out):
            par = par_t[:, l, :]
            m0 = arena.mark()
            BT = arena.alloc((2, T), BF16)
            CT = arena.alloc((2, T), BF16)
            Xtok = arena.alloc((NT, 384), BF16)
            dtv = arena.alloc((NT, 2, 6), F32)
            lav = arena.alloc((NT, 2, 6), F32)
            wz = arena.alloc((KC, 384), BF16)
            load_w(win_d[l][:, CO_Z:CO_Z + 384], wz)
            m1 = arena.mark()
            wdt = arena.alloc((KC, 12), BF16)
            load_w(win_d[l][:, CO_DT:CO_DT + 12], wdt)
            bank = banks[0]
            P.pe([I("matmul", out=bank.ap[:, tt * 12:(tt + 1) * 12], lhsT=hT[:, k, tt * 128:(tt + 1) * 128], rhs=wdt.v[:, k, :],
                    start=(k == 0), stop=(k == KC - 1)) for tt in range(NT) for k in range(KC)],
                 reads=[wdt] + hk(0, T), writes=[bank])
            d3 = lambda b: b.flat.rearrange("p (t c) -> p t c", c=12)
            P.dve(I("tensor_tensor", out=d3(dtv), in0=bank.ap[:, 0:NT * 12].rearrange("p (t c) -> p t c", c=12),
                    in1=par[:, PO_DTB:PO_DTB + 12].unsqueeze(1).to_broadcast([128, NT, 12]), op=ALU.add),
                  reads=[bank, "par"], writes=[dtv])
            aexp = sml_t[:, 0:12]
            P.act(I("activation", out=dtv.flat, in_=dtv.flat, func=AF.Exp), reads=[dtv], writes=[dtv])
            P.act(I("activation", out=dtv.flat, in_=dtv.flat, func=AF.Ln, bias=1.0, scale=1.0), reads=[dtv], writes=[dtv])
            P.act(I("activation", out=aexp, in_=par[:, PO_ALOG:PO_ALOG + 12], func=AF.Exp), reads=["par"], writes=["aexp"])
            P.dve(I("scalar_tensor_tensor", out=d3(lav), in0=d3(dtv), scalar=-1.0, in1=aexp.unsqueeze(1).to_broadcast([128, NT, 12]),
                    op0=ALU.mult, op1=ALU.mult), reads=[dtv, "aexp"], writes=[lav])
            if stop_after == "ssd_a":
                dump(lav.flat, 0, NT * 12, [lav])
                raise _Stop()
            P.dve([I("tensor_scalar", out=dI_t[:, h, :], in0=identb, scalar1=par[:, PO_SD + h:PO_SD + h + 1], scalar2=None, op0=ALU.mult) for h in range(6)],
                  reads=[CONST, "par"], writes=["dI"])
            rawp = arena.alloc(T + 8, F32)
            acc = arena.alloc(T, F32)
            xs = [arena.alloc(T, BF16) for _ in range(2)]
            wx = [arena.alloc((KC, 128), BF16) for _ in range(2)]
            P.pool(I("memset", ap=rawp.flat, constant=0.0), writes=[rawp])
            roff = lambda t0: (2 + t0) if t0 < L else (t0 + 6)
            for cc in range(7):
                w = wx[cc % 2]
                load_w(win_d[l][:, CO_XBC + cc * 128:CO_XBC + (cc + 1) * 128], w)
                for bi, (t0, n) in enumerate(blocks):
                    bank = banks[1 + bi % 2]
                    proj_fm(bank, w, 0, 128, t0, n)
                    P.act(I("copy", out=rawp.flat[:, roff(t0):roff(t0) + n], in_=bank.ap[:, :n]), reads=[bank], writes=[rawp])
                for (s0, sn, ro) in [(0, L, 2), (L, LC, L + 6)]:
                    sa = acc.sub(s0, s0 + sn)
                    P.dve(I("tensor_scalar", out=sa.ap, in0=rawp.flat[:, ro - 2:ro - 2 + sn], scalar1=par[:, PO_CW + cc * 5:PO_CW + cc * 5 + 1],
                            scalar2=par[:, PO_CB + cc:PO_CB + cc + 1], op0=ALU.mult, op1=ALU.add), reads=[rawp, "par"], writes=[sa])
                    for j in range(1, 5):
                        P.dve(I("scalar_tensor_tensor", out=sa.ap, in0=rawp.flat[:, ro - 2 + j:ro - 2 + j + sn],
                                scalar=par[:, PO_CW + cc * 5 + j:PO_CW + cc * 5 + j + 1], in1=sa.ap, op0=ALU.mult, op1=ALU.add),
                              reads=[rawp, "par", sa], writes=[sa])
                if cc < 3:
                    dst = xs[cc % 2]
                    dflat_ = dst.flat
                elif cc < 5:
                    dst = BT.sub((cc - 3) * T, (cc - 2) * T)
                    dflat_ = dst.ap
                else:
                    dst = CT.sub((cc - 5) * T, (cc - 4) * T)
                    dflat_ = dst.ap
                P.act(I("activation", out=dflat_, in_=acc.flat, func=AF.Silu), reads=[acc], writes=[dst])
                if cc < 3:
                    for t8 in range(0, NT, 8):
                        nt8 = min(8, NT - t8)
                        bank = banks[3 + (t8 // 8) % 2]
                        P.pe([I("transpose", out=b16(bank)[:, i * 128:(i + 1) * 128], in_=dflat_[:, (t8 + i) * 128:(t8 + i + 1) * 128], identity=identb)
                              for i in range(nt8)], reads=[dst, CONST], writes=[bank])
                        P.dve(I("tensor_copy", out=Xtok.v[:, t8:t8 + nt8, cc * 128:(cc + 1) * 128],
                                in_=b16(bank)[:, 0:nt8 * 128].rearrange("p (t f) -> p t f", f=128)),
                              reads=[bank], writes=[Xtok])
            if stop_after == "ssd_b":
                dump(Xtok.flat[:, 0:768], 0, 768, [Xtok])
                raise _Stop()
            arena.release(m1)
            sz = arena.alloc(384, F32)
            vv = sz
            sq = arena.alloc(384, BF16)
            rs = arena.alloc(128, F32)

            def pre(c, p, bZ):
                P.pe([I("matmul", out=bZ.ap[:, pc * 128:(pc + 1) * 128], lhsT=wz.v[:, k, pc * 128:(pc + 1) * 128], rhs=hT[:, k, c * 128:(c + 1) * 128],
                        start=(k == 0), stop=(k == KC - 1)) for pc in range(3) for k in range(KC)],
                     reads=[wz] + hk(c * 128, 128), writes=[bZ])
                sigmoid_from(bZ, 384, sz)
                P.dve(I("tensor_tensor", out=sz.flat, in0=bZ.ap[:, 0:384], in1=sz.flat, op=ALU.mult), reads=[bZ, sz], writes=[sz])

            def finish(c, bY, bF, p):
                P.dve(I("tensor_tensor", out=vv.flat, in0=bY.ap[:, 0:384], in1=sz.flat, op=ALU.mult), reads=[bY, sz], writes=[vv])
                P.act(I("activation", out=sq.flat, in_=vv.flat, func=AF.Square), reads=[vv], writes=[sq])
                P.pe([I("matmul", out=bF.ap[:, 0:128], lhsT=ones1b, rhs=sq.flat[:, pc * 128:(pc + 1) * 128], start=(pc == 0), stop=(pc == 2)) for pc in range(3)],
                     reads=[sq, CONST], writes=[bF])
                rstd_from(bF, 128, rs, scale=1.0 / 384.0)
                P.dve([I("scalar_tensor_tensor", out=oT_ssd.v[:, pc, c * 128:(c + 1) * 128], in0=vv.flat[:, pc * 128:(pc + 1) * 128],
                         scalar=par[:, PO_SNG + pc:PO_SNG + pc + 1], in1=rs.flat, op0=ALU.mult, op1=ALU.mult) for pc in range(3)],
                      reads=[vv, rs, "par"], writes=[oT_ssd.sub(pc * T + c * 128, pc * T + (c + 1) * 128) for pc in range(3)])

            scan(H=6, hg=3,
                 la_of=lambda c: (lav.v[:, c], [lav]),
                 dt_of=lambda c: (dtv.v[:, c], [dtv]),
                 kt_chunk=lambda i, c: (BT.v[:, i, c * 128:(c + 1) * 128], [BT]),
                 n_kt=2, kidx=lambda g: g,
                 qg_of=lambda c, p: (CT.v[:, :, c * 128:(c + 1) * 128], [CT]),
                 vtok_of=lambda c: (Xtok.v[:, c, :], [Xtok]),
                 extra_terms=lambda c, h: [(Xtok.v[:, c, h * 64:(h + 1) * 64], dI_t[:, h, :])],
                 extra_reads=["dI"],
                 finish=finish, dbuf=False, shared=sq, skip_out=(() if ctx_out else tuple(ctx_chunks)), pre=pre)
            arena.release(m0)

        def rope_store(src, src_reads, n, t0, dst_ap, dst_buf, ropeb, scr, scale=1.0, dsts=None):
            qb, t1, t2 = scr
            bR = banks[7]
            if scale == 1.0:
                P.act(I("copy", out=qb.flat[:, :n], in_=src), reads=src_reads, writes=[qb])
            else:
                P.act(I("mul", out=qb.flat[:, :n], in_=src, mul=scale), reads=src_reads, writes=[qb])
            P.pe(I("matmul", out=bR.ap[:, :n], lhsT=rotb, rhs=qb.flat[:, :n], start=True, stop=True), reads=[qb, CONST], writes=[bR])
            P.dve(I("tensor_tensor", out=t1.flat[:, :n], in0=qb.flat[:, :n], in1=ropeb.v[:, 0, t0:t0 + n], op=ALU.mult), reads=[qb, ropeb], writes=[t1])
            P.dve(I("tensor_tensor", out=t2.flat[:, :n], in0=bR.ap[:, :n], in1=ropeb.v[:, 1, t0:t0 + n], op=ALU.mult), reads=[bR, ropeb], writes=[t2])
            if dsts is None:
                dsts = [(0, 128, dst_ap, dst_buf)]
            for (p0, p1, d_ap, d_buf) in dsts:
                P.pool(I("tensor_tensor", out=d_ap, in0=t1.flat[p0:p1, :n], in1=t2.flat[p0:p1, :n], op=ALU.add), reads=[t1, t2], writes=[d_buf])

        def load_rope():
            ropeb = arena.alloc((2, T), BF16)
            P.dma("pool", I("dma_start", out=ropeb.v, in_=rope_d.rearrange("p (a t) -> p a t", a=2)), writes=[ropeb])
            return ropeb

        def phase_ret(l, oT_ret, ctx_out):
            par = par_t[:, l, :]
            m0 = arena.mark()
            QTr = arena.alloc((2, T), BF16)
            KTr = arena.alloc((2, T), BF16)
            Vtok = arena.alloc((NT, 256), BF16)
            wg = arena.alloc((KC, 256), BF16)
            load_w(win_d[l][:, CO_GR:CO_GR + 256], wg)
            lar = arena.alloc((2, 4), F32)
            m1 = arena.mark()
            ropeb = load_rope()
            wq = arena.alloc((KC, 256), BF16)
            wk_ = arena.alloc((KC, 256), BF16)
            wv = arena.alloc((KC, 256), BF16)
            load_w(win_d[l][:, CO_QR:CO_QR + 256], wq)
            load_w(win_d[l][:, CO_KR:CO_KR + 256], wk_)
            load_w(win_d[l][:, CO_VR:CO_VR + 256], wv)
            scr = (arena.alloc(512, BF16), arena.alloc(512, F32), arena.alloc(512, F32))
            tl = sml_t[:, 16:24]
            P.act(I("activation", out=tl, in_=par[:, PO_RLOG:PO_RLOG + 8], func=AF.Exp, scale=-1.0), reads=["par"], writes=["tl"])
            P.act(I("activation", out=tl, in_=tl, func=AF.Ln, bias=1.0, scale=1.0), reads=["tl"], writes=["tl"])
            P.dve(I("tensor_scalar", out=lar.flat, in0=tl, scalar1=-1.0, scalar2=None, op0=ALU.mult), reads=["tl"], writes=[lar])
            rjobs = [(w, dstT, scale, pc, t0, n) for (w, dstT, scale) in [(wq, QTr, 1.0), (wk_, KTr, 0.125)] for pc in range(2) for (t0, n) in blocks]
            for j in range(len(rjobs) + 1):
                if j < len(rjobs):
                    w, dstT, scale, pc, t0, n = rjobs[j]
                    proj_fm(banks[1 + j % 2], w, pc * 128, 128, t0, n)
                if j >= 1:
                    w, dstT, scale, pc, t0, n = rjobs[j - 1]
                    bank = banks[1 + (j - 1) % 2]
                    rope_store(bank.ap[:, :n], [bank], n, t0, dstT.v[:, pc, t0:t0 + n], dstT.sub(pc * T + t0, pc * T + t0 + n), ropeb, scr, scale)
            for tt in range(0, NT, 2):
                n2 = min(2, NT - tt)
                bank = banks[3 + (tt // 2) % 2]
                P.pe([I("matmul", out=bank.ap[:, i * 256:(i + 1) * 256], lhsT=hT[:, k, (tt + i) * 128:(tt + i + 1) * 128], rhs=wv.v[:, k, :],
                        start=(k == 0), stop=(k == KC - 1)) for i in range(n2) for k in range(KC)],
                     reads=[wv] + hk(tt * 128, n2 * 128), writes=[bank])
                P.act(I("copy", out=Vtok.v[:, tt:tt + n2, :], in_=bank.ap[:, 0:n2 * 256].rearrange("p (t f) -> p t f", f=256)),
                      reads=[bank], writes=[Vtok])
            arena.release(m1)
            Qz = [arena.alloc((4, 128), BF16) for _ in range(2)]
            y32 = arena.alloc(256, F32)
            yb = arena.alloc(256, BF16)
            yc = arena.alloc(256, F32)
            sq = arena.alloc(256, BF16)
            rs = arena.alloc(256, F32)
            sgs = [arena.alloc(256, F32) for _ in range(2)]

            def qg_of(c, p):
                Qz_ = Qz[p]
                P.dve(I("tensor_tensor", out=Qz_.flat.rearrange("p (a b i) -> p a b i", a=2, b=2),
                        in0=QTr.v[:, :, c * 128:(c + 1) * 128].unsqueeze(2).to_broadcast([128, 2, 2, 128]),
                        in1=hmb.unsqueeze(1).unsqueeze(3).to_broadcast([128, 2, 2, 128]), op=ALU.mult),
                      reads=[QTr, CONST], writes=[Qz_])
                return (Qz_.v, [Qz_])

            def pre(c, p, bZ):
                sg = sgs[p]
                P.pe([I("matmul", out=bZ.ap[:, pc * 128:(pc + 1) * 128], lhsT=wg.v[:, k, pc * 128:(pc + 1) * 128], rhs=hT[:, k, c * 128:(c + 1) * 128],
                        start=(k == 0), stop=(k == KC - 1)) for pc in range(2) for k in range(KC)],
                     reads=[wg] + hk(c * 128, 128), writes=[bZ])
                sigmoid_from(bZ, 256, sg)
                P.dve(I("tensor_tensor", out=sg.flat, in0=bZ.ap[:, 0:256], in1=sg.flat, op=ALU.mult), reads=[bZ, sg], writes=[sg])

            def finish(c, bY, bF, p):
                sg = sgs[p]
                P.act(I("copy", out=y32.flat, in_=bY.ap[:, 0:256]), reads=[bY], writes=[y32])
                P.dve(I("tensor_copy", out=yb.flat, in_=y32.flat), reads=[y32], writes=[yb])
                P.pe(I("matmul", out=bF.ap[:, 0:256], lhsT=blk64b, rhs=yb.flat, start=True, stop=True), reads=[yb, CONST], writes=[bF])
                P.dve(I("tensor_tensor", out=yc.flat, in0=y32.flat, in1=bF.ap[:, 0:256], op=ALU.subtract), reads=[y32, bF], writes=[yc])
                P.act(I("activation", out=sq.flat, in_=yc.flat, func=AF.Square), reads=[yc], writes=[sq])
                P.pe(I("matmul", out=bF.ap[:, 0:256], lhsT=blk64b, rhs=sq.flat, start=True, stop=True), reads=[sq, CONST], writes=[bF])
                rstd_from(bF, 256, rs)
                P.dve(I("tensor_tensor", out=yc.flat, in0=yc.flat, in1=rs.flat, op=ALU.mult), reads=[yc, rs], writes=[yc])
                P.act([I("activation", out=yc.flat[:, pc * 128:(pc + 1) * 128], in_=yc.flat[:, pc * 128:(pc + 1) * 128], func=AF.Identity,
                         bias=par[:, PO_RGB + pc:PO_RGB + pc + 1], scale=par[:, PO_RGG + pc:PO_RGG + pc + 1]) for pc in range(2)],
                      reads=[yc, "par"], writes=[yc])
                P.dve(I("tensor_tensor", out=oT_ret.v[:, :, c * 128:(c + 1) * 128], in0=yc.flat.rearrange("p (a i) -> p a i", a=2),
                        in1=sg.flat.rearrange("p (a i) -> p a i", a=2), op=ALU.mult),
                      reads=[yc, sg], writes=[oT_ret.sub(pc * T + c * 128, pc * T + (c + 1) * 128) for pc in range(2)])

            scan(H=4, hg=1,
                 la_of=lambda c: (lar.v, [lar]),
                 dt_of=lambda c: None,
                 kt_chunk=lambda i, c: (KTr.v[:, i, c * 128:(c + 1) * 128], [KTr]),
                 n_kt=2, kidx=lambda g: g // 2,
                 qg_of=qg_of,
                 vtok_of=lambda c: (Vtok.v[:, c, :], [Vtok]),
                 extra_terms=lambda c, h: [],
                 extra_reads=[],
                 finish=finish, dbuf=True, skip_out=(() if ctx_out else tuple(ctx_chunks)), pre=pre)
            arena.release(m0)

        def phase_att(l, oT_att, ctx_out):
            par = par_t[:, l, :]
            m0 = arena.mark()
            ropeb = load_rope()
            kTd = arena.alloc((2, T), BF16)
            Vext = arena.alloc((NT, 2, 128), BF16)
            qz = [arena.alloc(T, BF16) for _ in range(2)]
            mA = arena.mark()
            scr = (arena.alloc(512, BF16), arena.alloc(512, F32), arena.alloc(512, F32))
            sqb = arena.alloc(512, BF16)
            rs = arena.alloc(512, F32)
            qn = arena.alloc(512, F32)
            mB = arena.mark()
            gq8 = sml_t[:, 32:33]
            P.dve(I("tensor_scalar", out=gq8, in0=par[:, PO_QG:PO_QG + 1], scalar1=0.125, scalar2=None, op0=ALU.mult), reads=["par"], writes=["gq8"])
            bN = banks[6]

            def normrope(bank, n, t0, gcol, greads, dsts):
                P.act(I("activation", out=sqb.flat[:, :n], in_=bank.ap[:, :n], func=AF.Square), reads=[bank], writes=[sqb])
                P.pe(I("matmul", out=bN.ap[:, :n], lhsT=blk64b, rhs=sqb.flat[:, :n], start=True, stop=True), reads=[sqb, CONST], writes=[bN])
                rstd_from(bN, n, rs)
                P.dve(I("scalar_tensor_tensor", out=qn.flat[:, :n], in0=bank.ap[:, :n], scalar=gcol, in1=rs.flat[:, :n], op0=ALU.mult, op1=ALU.mult),
                      reads=[bank, rs] + greads, writes=[qn])
                rope_store(qn.flat[:, :n], [qn], n, t0, None, None, ropeb, scr, dsts=dsts)

            wkd = arena.alloc((2, KC, 128), BF16)
            wv = arena.alloc((KC, 128), BF16)
            for g in range(2):
                for hh in range(2):
                    P.dma("pool", I("dma_start", out=wkd.v[:, g, :, hh * 64:(hh + 1) * 64],
                                    in_=win_d[l][:, CO_KA + g * 64:CO_KA + (g + 1) * 64].rearrange("(k p) n -> p k n", p=128)), writes=[wkd])
            load_w(win_d[l][:, CO_VA:CO_VA + 128], wv)
            P.pool(I("memset", ap=Vext.v[:, :, :, 64:65], constant=1.0), writes=[Vext])
            P.pool(I("memset", ap=qz[0].flat[64:128, :], constant=0.0), writes=[qz[0]])
            P.pool(I("memset", ap=qz[1].flat[0:64, :], constant=0.0), writes=[qz[1]])
            kjobs = [(g, bi, t0, n) for g in range(2) for bi, (t0, n) in enumerate(blocks)]
            for j in range(len(kjobs) + 1):
                if j < len(kjobs):
                    g, bi, t0, n = kjobs[j]
                    bank = banks[j % 2]
                    P.pe([I("matmul", out=bank.ap[:, :n], lhsT=wkd.v[:, g, k, :], rhs=hT[:, k, t0:t0 + n], start=(k == 0), stop=(k == KC - 1))
                          for k in range(KC)], reads=[wkd] + hk(t0, n), writes=[bank])
                if j >= 1:
                    g, bi, t0, n = kjobs[j - 1]
                    normrope(banks[(j - 1) % 2], n, t0, par[:, PO_KG:PO_KG + 1], ["par"],
                             [(0, 128, kTd.v[:, g, t0:t0 + n], kTd.sub(g * T + t0, g * T + t0 + n))])
            for tt in range(0, NT, 4):
                n4 = min(4, NT - tt)
                bank = banks[2 + (tt // 4) % 2]
                P.pe([I("matmul", out=bank.ap[:, i * 128:(i + 1) * 128], lhsT=hT[:, k, (tt + i) * 128:(tt + i + 1) * 128], rhs=wv.v[:, k, :],
                        start=(k == 0), stop=(k == KC - 1)) for i in range(n4) for k in range(KC)],
                     reads=[wv] + hk(tt * 128, n4 * 128), writes=[bank])
                bv = bank.ap[:, 0:n4 * 128].rearrange("p (t g d) -> p t g d", g=2, d=64)
                P.act(I("copy", out=Vext.v[:, tt:tt + n4, :, 0:64], in_=bv), reads=[bank], writes=[Vext])
                P.dve(I("tensor_copy", out=Vext.v[:, tt:tt + n4, :, 65:128], in_=bv[:, :, :, 0:63]), reads=[bank], writes=[Vext])
            arena.release(mB)
            wq = arena.alloc((KC, 128), BF16)
            end_off = arena.off
            arena.off = mA
            PT = [arena.alloc(512, BF16) for _ in range(4)]
            rr = arena.alloc(512, F32)
            bcs = arena.alloc(512, F32)
            tn = arena.alloc(512, BF16)
            assert arena.off <= mB
            arena.off = end_off
            pti = 0
            sbanks = [banks[0], banks[1], banks[2], banks[5]]
            obanks = [banks[3], banks[4]]
            bBc = banks[7]
            si = 0
            oi = 0
            for qc in range(3):
                w = wq
                load_w(win_d[l][:, CO_QA + qc * 128:CO_QA + (qc + 1) * 128], w)
                for j in range(len(blocks) + 1):
                    if j < len(blocks):
                        t0, n = blocks[j]
                        proj_fm(banks[j % 2], w, 0, 128, t0, n)
                    if j >= 1:
                        t0, n = blocks[j - 1]
                        normrope(banks[(j - 1) % 2], n, t0, gq8, ["gq8"], [(0, 64, qz[0].flat[0:64, t0:t0 + n], qz[0].sub(t0, t0 + n)),
                                                                            (64, 128, qz[1].flat[64:128, t0:t0 + n], qz[1].sub(t0, t0 + n))])
                its = []
                for (t0, n) in blocks:
                    is_ctx = t0 >= L
                    if is_ctx and not ctx_out:
                        continue
                    kcs = ctx_chunks if is_ctx else list(range(NT))
                    for hh in range(2):
                        bO = obanks[oi % 2]
                        oi += 1
                        for ki, kc in enumerate(kcs):
                            its.append(dict(t0=t0, n=n, hh=hh, kc=kc, first=(ki == 0), last=(ki == len(kcs) - 1), bO=bO,
                                            bS=sbanks[si % 4], pt=PT[pti % 4]))
                            si += 1
                            pti += 1

                def emit_S(it):
                    t0, n, hh, kc, bSx, pt_ = it["t0"], it["n"], it["hh"], it["kc"], it["bS"], it["pt"]
                    g = (2 * qc + hh) // 3
                    P.pe(I("matmul", out=bSx.ap[:, :n], lhsT=kTd.v[:, g, kc * 128:(kc + 1) * 128], rhs=qz[hh].flat[:, t0:t0 + n], start=True, stop=True),
                         reads=[kTd.sub(g * T + kc * 128, g * T + (kc + 1) * 128), qz[hh].sub(t0, t0 + n)], writes=[bSx])
                    P.act(I("activation", out=pt_.flat[:, :n], in_=bSx.ap[:, :n], func=AF.Exp), reads=[bSx], writes=[pt_])

                def emit_PV(it):
                    t0, n, hh, kc, bO, pt_ = it["t0"], it["n"], it["hh"], it["kc"], it["bO"], it["pt"]
                    first, last = it["first"], it["last"]
                    g = (2 * qc + hh) // 3
                    P.pe(I("matmul", out=bO.ap[:, :n], lhsT=Vext.v[:, kc, g, :], rhs=pt_.flat[:, :n], start=first, stop=last),
                         reads=[Vext, pt_], writes=[bO])
                    if not last:
                        return
                    P.dve(I("reciprocal", out=rr.flat[64:65, :n], in_=bO.ap[64:65, :n]), reads=[bO], writes=[rr])
                    P.pe(I("matmul", out=bBc.ap[0:64, :n], lhsT=ones32[64:65, 0:64], rhs=rr.flat[64:65, :n], start=True, stop=True),
                         reads=[rr, CONST], writes=[bBc])
                    P.act(I("copy", out=bcs.flat[0:64, :n], in_=bBc.ap[0:64, :n]), reads=[bBc], writes=[bcs])
                    dst = oT_att.sub(qc * T + t0, qc * T + t0 + n)
                    if hh == 0:
                        P.dve(I("tensor_tensor", out=oT_att.v[0:64, qc, t0:t0 + n], in0=bO.ap[0:64, :n], in1=bcs.flat[0:64, :n], op=ALU.mult),
                              reads=[bO, bcs], writes=[dst])
                    else:
                        P.dve(I("tensor_tensor", out=tn.flat[0:64, :n], in0=bO.ap[0:64, :n], in1=bcs.flat[0:64, :n], op=ALU.mult),
                              reads=[bO, bcs], writes=[tn])
                        P.pe(I("matmul", out=bBc.ap[64:128, :n], lhsT=identb[0:64, 0:64], rhs=tn.flat[0:64, :n], start=True, stop=True),
                             reads=[tn, CONST], writes=[bBc])
                        P.act(I("copy", out=oT_att.v[64:128, qc, t0:t0 + n], in_=bBc.ap[64:128, :n]), reads=[bBc], writes=[dst])

                LOOK = 2
                for i in range(len(its) + LOOK):
                    if i < len(its):
                        emit_S(its[i])
                    if i >= LOOK:
                        emit_PV(its[i - LOOK])
            arena.release(m0)

        def phase_out(l, oT_parts, ctx_out):
            m0 = arena.mark()
            wo = arena.alloc((KC, D), BF16)
            load_w(wout_d[l], wo)
            yv = arena.alloc((KC, 512), F32)
            sqs = [arena.alloc(512, BF16) for _ in range(2)]
            rs = arena.alloc(512, F32)
            tmps = [arena.alloc(512, F32) for _ in range(2)]
            bMS = banks[6]
            pieces = []
            for (ob, nch) in oT_parts:
                for i in range(nch):
                    pieces.append((ob, i))
            for (t0, n) in blocks:
                if t0 >= L and not ctx_out:
                    continue
                col = 0 if t0 < L else 1
                for dc in range(KC):
                    bank = banks[dc % 4]
                    P.pe([I("matmul", out=bank.ap[:, :n], lhsT=wo.v[:, k, dc * 128:(dc + 1) * 128], rhs=ob.v[:, i, t0:t0 + n], start=(k == 0), stop=(k == KC - 1))
                          for k, (ob, i) in enumerate(pieces)],
                         reads=[wo] + [ob.sub(i * T + t0, i * T + t0 + n) for (ob, i) in pieces], writes=[bank])
                    P.act(I("copy", out=yv.v[:, dc, :n], in_=bank.ap[:, :n]), reads=[bank], writes=[yv.sub(dc * 512, dc * 512 + 512)])
                ms_block(lambda k: yv.v[:, k, :n], n, bMS, sqs, [yv])
                rstd_from(bMS, n, rs)
                for k in range(KC):
                    tmp = tmps[k % 2]
                    P.dve(I("scalar_tensor_tensor", out=tmp.flat[:, :n], in0=yv.v[:, k, :n], scalar=ab_t[:, 2, k, col:col + 1],
                            in1=rs.flat[:, :n], op0=ALU.mult, op1=ALU.mult),
                          reads=[yv.sub(k * 512, k * 512 + 512), rs, "ab"], writes=[tmp])
                    P.pool(I("tensor_tensor", out=xT[:, k, t0:t0 + n], in0=xT[:, k, t0:t0 + n], in1=tmp.flat[:, :n], op=ALU.add),
                           reads=[tmp] + xkk(t0, n), writes=xkk(t0, n))
            arena.release(m0)

        def phase_ffn(l, ctx_out):
            m0 = arena.mark()
            Tf = T if ctx_out else L
            GSZ = ((Tf // 2 + 127) // 128) * 128
            groups = []
            t = 0
            while t < Tf:
                gn = min(GSZ, Tf - t)
                groups.append((t, gn))
                t += gn
            yacc = arena.alloc((KC, GSZ), F32)
            w1 = [arena.alloc((KC, 512), BF16) for _ in range(2)]
            w2 = [arena.alloc((4, D), BF16) for _ in range(2)]
            uT = [arena.alloc((4, 512), BF16) for _ in range(2)]
            rl = [arena.alloc(512, BF16) for _ in range(2)]
            sqs = [arena.alloc(512, BF16) for _ in range(2)]
            rs = arena.alloc(512, F32)
            tmps = [arena.alloc(512, F32) for _ in range(2)]
            bMS = banks[7]
            ui = 0
            wi = 0
            for (g0, gn) in groups:
                subs = []
                t = g0
                while t < g0 + gn:
                    n = min(512, g0 + gn - t)
                    if t < L < t + n:
                        n = L - t
                    subs.append((t, n))
                    t += n
                its = []
                for fg in range(8):
                    for (t0, n) in subs:
                        its.append(dict(fg=fg, t0=t0, n=n, first_of_fg=((t0, n) == subs[0])))

                def u_phase(it):
                    nonlocal ui, wi
                    fg, t0, n = it["fg"], it["t0"], it["n"]
                    if it["first_of_fg"]:
                        it["wa"], it["wb"] = w1[wi % 2], w2[wi % 2]
                        wi += 1
                        load_w(w1_d[l][:, fg * 512:(fg + 1) * 512], it["wa"])
                        load_w(w2_d[l][fg * 512:(fg + 1) * 512, :], it["wb"])
                        cur["wa"], cur["wb"] = it["wa"], it["wb"]
                    else:
                        it["wa"], it["wb"] = cur["wa"], cur["wb"]
                    u = uT[ui % 2]
                    ui += 1
                    it["u"] = u
                    for fc in range(4):
                        bank = banks[fc % 2]
                        proj_fm(bank, it["wa"], fc * 128, 128, t0, n)
                        uk = u.sub(fc * 512, fc * 512 + 512)
                        r_ = rl[fc % 2]
                        P.act(I("activation", out=r_.flat[:, :n], in_=bank.ap[:, :n], func=AF.Relu), reads=[bank], writes=[r_])
                        if fc % 2 == 0:
                            P.dve(I("tensor_tensor", out=u.v[:, fc, :n], in0=r_.flat[:, :n], in1=r_.flat[:, :n], op=ALU.mult), reads=[r_], writes=[uk])
                        else:
                            P.act(I("activation", out=u.v[:, fc, :n], in_=r_.flat[:, :n], func=AF.Square), reads=[r_], writes=[uk])

                def y_phase(it):
                    fg, t0, n, u, wb = it["fg"], it["t0"], it["n"], it["u"], it["wb"]
                    for dc in range(KC):
                        bank = banks[2 + dc % 4]
                        P.pe([I("matmul", out=bank.ap[:, :n], lhsT=wb.v[:, fc, dc * 128:(dc + 1) * 128], rhs=u.v[:, fc, :n], start=(fc == 0), stop=(fc == 3))
                              for fc in range(4)], reads=[wb, u], writes=[bank])
                        ya = yacc.v[:, dc, t0 - g0:t0 - g0 + n]
                        yk = yacc.sub(dc * GSZ + t0 - g0, dc * GSZ + t0 - g0 + n)
                        if fg == 0:
                            P.act(I("copy", out=ya, in_=bank.ap[:, :n]), reads=[bank], writes=[yk])
                        else:
                            P.dve(I("tensor_tensor", out=ya, in0=ya, in1=bank.ap[:, :n], op=ALU.add), reads=[bank, yk], writes=[yk])

                cur = {}
                for i in range(len(its) + 1):
                    if i < len(its):
                        u_phase(its[i])
                    if i >= 1:
                        y_phase(its[i - 1])
                for (t0, n) in subs:
                    col = 0 if t0 < L else 1
                    o = t0 - g0
                    ms_block(lambda k: yacc.v[:, k, o:o + n], n, bMS, sqs, [yacc])
                    rstd_from(bMS, n, rs)
                    for k in range(KC):
                        tmp = tmps[k % 2]
                        P.dve(I("scalar_tensor_tensor", out=tmp.flat[:, :n], in0=yacc.v[:, k, o:o + n], scalar=ab_t[:, 5, k, col:col + 1],
                                in1=rs.flat[:, :n], op0=ALU.mult, op1=ALU.mult),
                              reads=[yacc, rs, "ab"], writes=[tmp])
                        P.pool(I("tensor_tensor", out=xT[:, k, t0:t0 + n], in0=xT[:, k, t0:t0 + n], in1=tmp.flat[:, :n], op=ALU.add),
                               reads=[tmp] + xkk(t0, n), writes=xkk(t0, n))
            arena.release(m0)

        def phase_store():
            m = arena.mark()
            stg = [arena.alloc(D, F32) for _ in range(2)]
            outs = []
            for tt in range(NTL):
                s = stg[tt % 2]
                for half in range(2):
                    bank = banks[(tt % 2) * 2 + half]
                    P.pe([I("transpose", out=bank.ap[:, kk * 128:(kk + 1) * 128], in_=xT[:, half * 4 + kk, tt * 128:(tt + 1) * 128], identity=ident32)
                          for kk in range(4)], reads=xkk(tt * 128, 128) + [CONST], writes=[bank])
                    if half == 0:
                        P.act(I("copy", out=s.flat[:, 0:512], in_=bank.ap[:, :]), reads=[bank], writes=[s.sub(0, 512)])
                    else:
                        P.dve(I("tensor_copy", out=s.flat[:, 512:1024], in_=bank.ap[:, :]), reads=[bank], writes=[s.sub(512, 1024)])
                P.dma("sp", I("dma_start", out=out_d[tt * 128:(tt + 1) * 128, :], in_=s.flat), reads=[s], writes=[("out", tt)])
                outs.append(("out", tt))
            P.op("sp", [], reads=outs)
            arena.release(m)

        def assemble():
            phase_load()
            if stop_after == "load":
                dump(xT[:, 0, :], 0, T, xkk(0, T))
                return
            for l in range(depth):
                ctx_out = l < depth - 1
                phase_mod(l)
                if stop_after == "mod":
                    dump(ab_t[:].rearrange("p a k c -> p (a k c)"), 0, 96, ["ab"])
                    return
                phase_norm_h(0, 1)
                if stop_after == "norm1":
                    dump(hT[:, 0, :], 0, T, hk(0, T))
                    return
                base = arena.mark()
                oT_ssd = arena.alloc((3, T), BF16)
                oT_ret = arena.alloc((2, T), BF16)
                oT_att = arena.alloc((3, T), BF16)
                arena.off = oT_ssd.off + oT_ssd.n * 2
                phase_ssd(l, oT_ssd, ctx_out)
                if stop_after == "ssd":
                    dump(oT_ssd.flat, 0, 3 * T, [oT_ssd])
                    return
                arena.off = oT_ret.off + oT_ret.n * 2
                phase_ret(l, oT_ret, ctx_out)
                if stop_after == "ret":
                    dump(oT_ret.flat, 0, 2 * T, [oT_ret])
                    return
                arena.off = oT_att.off + oT_att.n * 2
                phase_att(l, oT_att, ctx_out)
                if stop_after == "att":
                    dump(oT_att.flat, 0, 3 * T, [oT_att])
                    return
                phase_out(l, [(oT_att, 3), (oT_ret, 2), (oT_ssd, 3)], ctx_out)
                arena.release(base)
                if stop_after == "out":
                    dump(xT[:, 0, 0:128], 0, 128, xkk(0, 128))
                    return
                phase_norm_h(3, 4)
                phase_ffn(l, ctx_out)
                if stop_after == "layer":
                    dump(xT[:, 0, 0:128], 0, 128, xkk(0, 128))
                    return
            phase_store()

        try:
            assemble()
        except _Stop:
            pass
        P.emit(st)
    build.arena_hi = arena.hi
    build.n_ops = len(P.ops)
    build.stats = P.stats
    return nc


def make_in_maps(inp, L, LC, depth, nb):
    con = host_consts()
    rope = host_rope(L, LC)
    par = host_params(inp, depth)
    f32 = lambda a: np.ascontiguousarray(np.asarray(a, dtype=np.float32))
    shared = {
        "par": par, "con": con, "rope": rope,
        "w_mod": f32(inp["w_mod"]), "w_in": f32(inp["w_in"]), "w_out": f32(inp["w_out"]),
        "w_ff1": f32(inp["w_ff1"]), "w_ff2": f32(inp["w_ff2"]),
    }
    maps = []
    for b in range(nb):
        cv = np.zeros((128, KC, 2), np.float32)
        cv[:, :, 0] = np.asarray(inp["c"][b]).reshape(KC, 128).T
        cv[:, :, 1] = np.asarray(inp["c_ctx"]).reshape(KC, 128).T
        m = dict(shared)
        m["x"] = f32(inp["x"][b])
        m["ctx"] = f32(inp["ctx"][b])
        m["cv"] = cv.reshape(128, 16)
        maps.append(m)
    return maps


def kernel(**inputs):
    inp = {k: np.asarray(v) for k, v in inputs.items()}
    B, L, _ = inp["x"].shape
    LC = inp["ctx"].shape[1]
    depth = inp["w_mod"].shape[0]
    nc = build(L, LC, depth)
    maps = make_in_maps(inp, L, LC, depth, B)
    res = run_bass_kernel_spmd(nc, maps, core_ids=list(range(B)))
    return np.stack([np.asarray(r["out"], dtype=np.float32) for r in res.results], axis=0)
```
